# Optimizing a Trainium2 kernel written in Bass

```python
import jax
import jax.numpy as jnp
from jax import lax
import numpy as np

D_MODEL = 1024
BATCH = 8
SEQ = 8192
DEPTH = 4

GRID_W = 64
CTX_LEN = 256
HEAD_DIM = 64
MIX_WIDTH = D_MODEL
POOL_WIDTH = MIX_WIDTH // 4
N_POOL = 4
POOL_DIM = POOL_WIDTH // N_POOL
POOL_WINDOWS = (2, 4, 8, 16)
NA_WIDTH = 3 * MIX_WIDTH // 8
NA_HEADS = NA_WIDTH // HEAD_DIM
NA_ROWS = 8
NA_COLS = 16
GQA_WIDTH = MIX_WIDTH - POOL_WIDTH - NA_WIDTH
GQA_Q_HEADS = GQA_WIDTH // HEAD_DIM
GQA_KV_HEADS = 2
GQA_KV_WIDTH = GQA_KV_HEADS * HEAD_DIM
IN_SIZES = (POOL_WIDTH, NA_WIDTH, NA_WIDTH, NA_WIDTH, GQA_WIDTH, GQA_KV_WIDTH, GQA_KV_WIDTH)
IN_WIDTH = POOL_WIDTH + 3 * NA_WIDTH + GQA_WIDTH + 2 * GQA_KV_WIDTH
Q_BLOCK = 128
ROPE_THETA = 10000.0
ROPE_AXIS_DIM = HEAD_DIM // 2
D_FF = 2816
CONV_WIDTH = 3
DEEPNORM_ALPHA = (2 * DEPTH) ** 0.25
DEEPNORM_BETA = (8 * DEPTH) ** -0.25
LN_EPS = 1e-6

kernel_name = 'hybrid_pool_natten_gqa_dit'


def layer_norm(x, g, b):
    xf = x.astype(jnp.float32)
    mu = jnp.mean(xf, -1, keepdims=True)
    var = jnp.mean(jnp.square(xf - mu), -1, keepdims=True)
    y = (xf - mu) * lax.rsqrt(var + LN_EPS) * g.astype(jnp.float32) + b.astype(jnp.float32)
    return y.astype(x.dtype)


def rms_norm(x, g):
    xf = x.astype(jnp.float32)
    y = xf * lax.rsqrt(jnp.mean(xf * xf, -1, keepdims=True) + LN_EPS) * g.astype(jnp.float32)
    return y.astype(x.dtype)


def axial_rope_tables(L):
    t = jnp.arange(L, dtype=jnp.int32)
    inv = ROPE_THETA ** (-jnp.arange(0, ROPE_AXIS_DIM, 2, dtype=jnp.float32) / ROPE_AXIS_DIM)
    ang_r = (t // GRID_W).astype(jnp.float32)[:, None] * inv
    ang_c = (t % GRID_W).astype(jnp.float32)[:, None] * inv
    return (jnp.cos(ang_r), jnp.sin(ang_r), jnp.cos(ang_c), jnp.sin(ang_c))


def _rotate(x, cos, sin):
    x1, x2 = jnp.split(x, 2, axis=-1)
    cos = cos[:, None, :].astype(x.dtype)
    sin = sin[:, None, :].astype(x.dtype)
    return jnp.concatenate([x1 * cos - x2 * sin, x1 * sin + x2 * cos], axis=-1)


def apply_axial_rope(x, tabs):
    cos_r, sin_r, cos_c, sin_c = tabs
    x_row, x_col = jnp.split(x, 2, axis=-1)
    return jnp.concatenate([_rotate(x_row, cos_r, sin_r), _rotate(x_col, cos_c, sin_c)], axis=-1)


def project(h, w):
    z = h @ w
    B, L, _ = z.shape
    points, acc = [], 0
    for s in IN_SIZES[:-1]:
        acc += s
        points.append(acc)
    parts = jnp.split(z, points, axis=-1)
    pool_in = parts[0]
    q_na, k_na, v_na = [t.reshape(B, L, NA_HEADS, HEAD_DIM) for t in parts[1:4]]
    q_g = parts[4].reshape(B, L, GQA_Q_HEADS, HEAD_DIM)
    k_g, v_g = [t.reshape(B, L, GQA_KV_HEADS, HEAD_DIM) for t in parts[5:7]]
    return pool_in, q_na, k_na, v_na, q_g, k_g, v_g


def multiscale_pool(u, w_grp, scale):
    B, L, _ = u.shape
    ug = u.reshape(B, L, N_POOL, POOL_DIM)
    csum = jnp.concatenate([jnp.zeros((B, 1, N_POOL, POOL_DIM), jnp.float32),
                            jnp.cumsum(ug.astype(jnp.float32), axis=1)], axis=1)
    t = jnp.arange(L)
    means = []
    for g, w in enumerate(POOL_WINDOWS):
        lo = jnp.clip(t - w // 2, 0, L - 1)
        hi = jnp.clip(t + w // 2 - 1, 0, L - 1)
        cnt = (hi - lo + 1).astype(jnp.float32)[None, :, None]
        means.append((csum[:, hi + 1, g] - csum[:, lo, g]) / cnt)
    pooled = jnp.stack(means, axis=2).astype(u.dtype) - ug
    y = jnp.einsum('blgc,gcd->blgd', pooled, w_grp)
    return y.reshape(B, L, N_POOL * POOL_DIM) * scale


def attend(q, k, v):
    B, Lq, H, dh = q.shape
    hkv = k.shape[2]
    qg = q.reshape(B, Lq, hkv, H // hkv, dh)
    s = jnp.einsum('bqkgd,bskd->bkgqs', qg, k, preferred_element_type=jnp.float32) * (dh ** -0.5)
    p = jax.nn.softmax(s, axis=-1).astype(v.dtype)
    return jnp.einsum('bkgqs,bskd->bqkgd', p, v).reshape(B, Lq, H * dh)


def blocked_gqa(q, k, v, k_ctx, v_ctx):
    B, L, H, dh = q.shape
    k_all = jnp.concatenate([k_ctx, k], axis=1)
    v_all = jnp.concatenate([v_ctx, v], axis=1)
    qb = q.reshape(B, L // Q_BLOCK, Q_BLOCK, H, dh).swapaxes(0, 1)
    out = lax.map(lambda q_blk: attend(q_blk, k_all, v_all), qb)
    return out.swapaxes(0, 1).reshape(B, L, H * dh)


def neighbourhood_attention(q, k, v, k_ctx, v_ctx, rpb):
    B, L, H, dh = q.shape
    rows = L // GRID_W
    kh = min(NA_ROWS, rows)
    n_nb = kh * NA_COLS
    qg = q.reshape(B, rows, GRID_W, H, dh) * (dh ** -0.5)
    kg = k.reshape(B, rows, GRID_W, H, dh)
    vg = v.reshape(B, rows, GRID_W, H, dh)
    col = jnp.arange(GRID_W)
    c_start = jnp.clip(col - NA_COLS // 2, 0, GRID_W - NA_COLS)
    c_idx = c_start[:, None] + jnp.arange(NA_COLS)
    dc = c_idx - col[:, None] + (NA_COLS - 1)
    rpb_c = rpb[:, :, dc].astype(jnp.float32)

    def one_row(r):
        r_start = jnp.clip(r - kh // 2, 0, rows - kh)
        k_band = lax.dynamic_slice_in_dim(kg, r_start, kh, axis=1)
        v_band = lax.dynamic_slice_in_dim(vg, r_start, kh, axis=1)
        k_nb = k_band[:, :, c_idx]
        v_nb = v_band[:, :, c_idx]
        q_row = lax.dynamic_index_in_dim(qg, r, axis=1, keepdims=False)
        dr = r_start + jnp.arange(kh) - r + (NA_ROWS - 1)
        bias = rpb_c[:, dr].transpose(0, 2, 1, 3)
        s_nb = jnp.einsum('bqhd,bkqjhd->bhqkj', q_row, k_nb,
                          preferred_element_type=jnp.float32) + bias[None]
        s_ctx = jnp.einsum('bqhd,bchd->bhqc', q_row, k_ctx, preferred_element_type=jnp.float32)
        s = jnp.concatenate([s_nb.reshape(B, H, GRID_W, n_nb), s_ctx], axis=-1)
        p = jax.nn.softmax(s, axis=-1).astype(v.dtype)
        p_nb = p[..., :n_nb].reshape(B, H, GRID_W, kh, NA_COLS)
        p_ctx = p[..., n_nb:]
        return (jnp.einsum('bhqkj,bkqjhd->bqhd', p_nb, v_nb)
                + jnp.einsum('bhqc,bchd->bqhd', p_ctx, v_ctx))

    out = lax.map(one_row, jnp.arange(rows))
    return out.transpose(1, 0, 2, 3, 4).reshape(B, L, H * dh)


def conv_ffn(h, w_up, conv_w, conv_b, w_down):
    u = h @ w_up
    L = u.shape[1]
    half = CONV_WIDTH // 2
    up = jnp.pad(u, ((0, 0), (half, half), (0, 0)))
    acc = up[:, 0:L] * conv_w[0] + conv_b
    for j in range(1, CONV_WIDTH):
        acc = acc + up[:, j:j + L] * conv_w[j]
    a, g = jnp.split(acc, 2, axis=-1)
    return (a * jax.nn.silu(g)) @ w_down


def setup_inputs(seed: int = 0) -> dict:
    key = jax.random.key(seed)
    ks = jax.random.split(key, 22)

    def nrm(k, shape, s):
        return jax.random.normal(k, shape, jnp.float32) * s

    return {
        'x': nrm(ks[0], (BATCH, SEQ, D_MODEL), 1.0),
        'c': nrm(ks[1], (BATCH, D_MODEL), 1.0),
        'ctx': nrm(ks[2], (BATCH, CTX_LEN, D_MODEL), 1.0),
        'c_ctx': nrm(ks[3], (D_MODEL,), 1.0),
        'w_mod': nrm(ks[4], (DEPTH, D_MODEL, 6 * D_MODEL), 0.5 * D_MODEL ** -0.5),
        'b_mod': nrm(ks[5], (DEPTH, 6 * D_MODEL), 0.02),
        'w_in': nrm(ks[6], (DEPTH, D_MODEL, IN_WIDTH), D_MODEL ** -0.5),
        'pool_w': nrm(ks[7], (DEPTH, N_POOL, POOL_DIM, POOL_DIM), POOL_DIM ** -0.5),
        'pool_scale': 1.0 + nrm(ks[8], (DEPTH, POOL_WIDTH), 0.1),
        'na_rpb': nrm(ks[9], (DEPTH, NA_HEADS, 2 * NA_ROWS - 1, 2 * NA_COLS - 1), 0.1),
        'q_norm': 1.0 + nrm(ks[10], (DEPTH, HEAD_DIM), 0.1),
        'k_norm': 1.0 + nrm(ks[11], (DEPTH, HEAD_DIM), 0.1),
        'w_out': nrm(ks[12], (DEPTH, MIX_WIDTH, D_MODEL), MIX_WIDTH ** -0.5 * DEEPNORM_BETA),
        'ln1_g': 1.0 + nrm(ks[13], (DEPTH, D_MODEL), 0.1),
        'ln1_b': nrm(ks[14], (DEPTH, D_MODEL), 0.02),
        'w_up': nrm(ks[15], (DEPTH, D_MODEL, 2 * D_FF), D_MODEL ** -0.5),
        'conv_w': nrm(ks[16], (DEPTH, CONV_WIDTH, 2 * D_FF), CONV_WIDTH ** -0.5),
        'conv_b': nrm(ks[17], (DEPTH, 2 * D_FF), 0.02),
        'w_down': nrm(ks[18], (DEPTH, D_FF, D_MODEL), D_FF ** -0.5 * DEEPNORM_BETA),
        'ln2_g': 1.0 + nrm(ks[19], (DEPTH, D_MODEL), 0.1),
        'ln2_b': nrm(ks[20], (DEPTH, D_MODEL), 0.02),
    }


def reference(x, c, ctx, c_ctx, w_mod, b_mod, w_in, pool_w, pool_scale, na_rpb, q_norm, k_norm,
              w_out, ln1_g, ln1_b, w_up, conv_w, conv_b, w_down, ln2_g, ln2_b):
    rope = axial_rope_tables(x.shape[1])
    xc = ctx
    act_lat = jax.nn.silu(c)
    act_ctx = jax.nn.silu(c_ctx)
    for l in range(DEPTH):
        mod = act_lat @ w_mod[l] + b_mod[l]
        mod_c = act_ctx @ w_mod[l] + b_mod[l]
        sh1, s1, g1, sh2, s2, g2 = [m[:, None, :] for m in jnp.split(mod, 6, axis=-1)]
        sh1c, s1c, g1c, sh2c, s2c, g2c = jnp.split(mod_c, 6, axis=-1)

        p_c, qn_c, kn_c, vn_c, qg_c, kg_c, vg_c = project(xc * (1 + s1c) + sh1c, w_in[l])
        kg_c = rms_norm(kg_c, k_norm[l])

        p, qn, kn, vn, qg, kg, vg = project(x * (1 + s1) + sh1, w_in[l])
        y_pool = multiscale_pool(p, pool_w[l], pool_scale[l])
        y_na = neighbourhood_attention(qn, kn, vn, kn_c, vn_c, na_rpb[l])
        qg = apply_axial_rope(rms_norm(qg, q_norm[l]), rope)
        kg = apply_axial_rope(rms_norm(kg, k_norm[l]), rope)
        y_gqa = blocked_gqa(qg, kg, vg, kg_c, vg_c)
        mix = jnp.concatenate([y_pool, y_na, y_gqa], axis=-1) @ w_out[l]
        x = layer_norm(DEEPNORM_ALPHA * x + g1 * mix, ln1_g[l], ln1_b[l])
        ffn = conv_ffn(x * (1 + s2) + sh2, w_up[l], conv_w[l], conv_b[l], w_down[l])
        x = layer_norm(DEEPNORM_ALPHA * x + g2 * ffn, ln2_g[l], ln2_b[l])

        if l < DEPTH - 1:
            mix_c = jnp.concatenate([
                multiscale_pool(p_c, pool_w[l], pool_scale[l]),
                attend(qn_c, kn_c, vn_c),
                attend(rms_norm(qg_c, q_norm[l]), kg_c, vg_c)], axis=-1) @ w_out[l]
            xc = layer_norm(DEEPNORM_ALPHA * xc + g1c * mix_c, ln1_g[l], ln1_b[l])
            ffn_c = conv_ffn(xc * (1 + s2c) + sh2c, w_up[l], conv_w[l], conv_b[l], w_down[l])
            xc = layer_norm(DEEPNORM_ALPHA * xc + g2c * ffn_c, ln2_g[l], ln2_b[l])
    return x
```

```python
import numpy as np
from contextlib import ExitStack
import concourse.bass as bass
import concourse.mybir as mybir
from concourse.bass_utils import run_bass_kernel_spmd

F32 = mybir.dt.float32
BF16 = mybir.dt.bfloat16
AF = mybir.ActivationFunctionType
ALU = mybir.AluOpType

D = 1024
CTX = 256
DFF = 2816
NJ = 22
GRID_W = 64
NEG = -30000.0
LN_EPS = 1e-6
NVEC = 214
NWIN = 2560
RING = 12


class Prog:
    CE = ('pe', 'act', 'dve', 'pool')

    def __init__(self, nc):
        self.nc = nc
        self.segs = []
        self.chans = {}
        self.nbar = 0
        self._new_seg()

    def _new_seg(self):
        self.ops = []
        self.lastw = {}
        self.rd = {}
        self.chan_cnt = {}
        self.segs.append((self.ops, self.chan_cnt))

    def add(self, eng, fn, reads=(), writes=(), chan=None):
        i = len(self.ops)
        dma = eng == 'sp'
        deps = {}

        def dep(j, kind):
            o = self.ops[j]
            if not dma and not o['dma'] and o['eng'] == eng and kind != 'raw':
                return
            if dma and o['dma'] and kind == 'waw' and o['chan'] == chan:
                return
            deps[j] = True
        for r in reads:
            if r in self.lastw:
                dep(self.lastw[r], 'raw')
            if r.startswith('ps'):
                for j in self.rd.get(r, ()):
                    if self.ops[j]['eng'] != eng:
                        deps[j] = True
        for r in writes:
            if r in self.lastw:
                dep(self.lastw[r], 'waw')
            for j in self.rd.get(r, ()):
                dep(j, 'war')
        for r in reads:
            self.rd.setdefault(r, []).append(i)
        for r in writes:
            self.lastw[r] = i
            self.rd[r] = []
        op = dict(eng=eng, fn=fn, deps=list(deps), dma=dma, chan=chan, flag=False, ticket=None)
        if dma:
            assert chan is not None
            self.chans[chan] = True
            self.chan_cnt[chan] = self.chan_cnt.get(chan, 0) + 16
            op['ticket'] = self.chan_cnt[chan]
            op['flag'] = True
        for j in deps:
            self.ops[j]['flag'] = True
        self.ops.append(op)
        return i

    def barrier(self):
        self._new_seg()

    def emit(self, dummies, final_chans=()):
        nc = self.nc
        for ops, _ in self.segs:
            cnt = {e: 0 for e in self.CE}
            for o in ops:
                if not o['dma'] and o['flag']:
                    cnt[o['eng']] += 1
                    o['ticket'] = cnt[o['eng']]
                    assert cnt[o['eng']] < 30000
            for c, v in _.items():
                assert v < 30000, (c, v)
        with ExitStack() as st:
            sems = {e: st.enter_context(nc.semaphore('S_' + e)) for e in self.CE}
            bsem = {e: st.enter_context(nc.semaphore('B_' + e)) for e in self.CE}
            barc = st.enter_context(nc.semaphore('C_bar'))
            for c in self.chans:
                sems['c:' + c] = st.enter_context(nc.semaphore('C_' + c))
            block = st.enter_context(nc.Block())
            segs = self.segs
            nseg = len(segs)

            def run(engname):
                def body(e):
                    for si, (ops, chan_cnt) in enumerate(segs):
                        known = {}
                        for o in ops:
                            if o['eng'] != engname:
                                continue
                            need = {}
                            for j in o['deps']:
                                d = ops[j]
                                key = ('c:' + d['chan']) if d['dma'] else d['eng']
                                need[key] = max(need.get(key, 0), d['ticket'])
                            for key, v in need.items():
                                if known.get(key, 0) < v:
                                    e.wait_ge(sems[key], v)
                                    known[key] = v
                            ins = o['fn'](e)
                            if o['flag']:
                                if o['dma']:
                                    ins.then_inc(sems['c:' + o['chan']], 16)
                                else:
                                    ins.then_inc(sems[engname], 1)
                        last = si == nseg - 1
                        if engname == 'sp':
                            for c, v in chan_cnt.items():
                                if last and c not in final_chans:
                                    continue
                                e.wait_ge(sems['c:' + c], v)
                            if last:
                                continue
                            for ce in self.CE:
                                e.wait_ge(bsem[ce], 2 * si + 1)
                            for c in chan_cnt:
                                e.sem_clear(sems['c:' + c])
                            dummies['sp'](e).then_inc(barc, 16)
                            for ce in self.CE:
                                e.wait_ge(bsem[ce], 2 * si + 2)
                            dummies['sp'](e).then_inc(barc, 16)
                        else:
                            if last:
                                continue
                            if engname == 'pe':
                                for ce in ('act', 'dve', 'pool'):
                                    e.wait_ge(bsem[ce], 2 * si + 1)
                            dummies[engname](e).then_inc(bsem[engname], 1)
                            e.wait_ge(barc, 16 * (2 * si + 1))
                            e.sem_clear(sems[engname])
                            dummies[engname](e).then_inc(bsem[engname], 1)
                            e.wait_ge(barc, 16 * (2 * si + 2))
                return body
            block.tensor(run('pe'))
            block.scalar(run('act'))
            block.vector(run('dve'))
            block.gpsimd(run('pool'))
            block.sync(run('sp'))


class Arena:
    def __init__(self, nc, base=20480, limit=229344):
        self.nc = nc
        self.off = base
        self.limit = limit
        self.n = 0

    def alloc(self, name, shape, dtype):
        esz = 4 if dtype == F32 else 2
        nbytes = int(np.prod(shape[1:])) * esz
        nbytes = (nbytes + 63) // 64 * 64
        assert self.off + nbytes <= self.limit, (name, self.off, nbytes)
        self.n += 1
        t = self.nc.alloc_sbuf_tensor_at(f"{name}_{self.n}", list(shape), dtype, offset=self.off)
        self.off += nbytes
        return t

    def mark(self):
        return self.off

    def reset(self, m):
        self.off = m


def _partner_sign():
    partner = np.zeros(64, np.int64)
    sign = np.zeros(64, np.float32)
    for i in range(64):
        blk, r = divmod(i, 32)
        if r < 16:
            partner[i] = blk * 32 + r + 16
            sign[i] = -1.0
        else:
            partner[i] = blk * 32 + r - 16
            sign[i] = 1.0
    return partner, sign


def _rope_tables(L):
    partner, sign = _partner_sign()
    t = np.arange(L, dtype=np.int32)
    inv = (np.float32(10000.0) ** (-np.arange(0, 32, 2, dtype=np.float32) / np.float32(32))).astype(np.float32)
    ang_r = (t // GRID_W).astype(np.float32)[:, None] * inv
    ang_c = (t % GRID_W).astype(np.float32)[:, None] * inv
    cos = np.zeros((64, L), np.float32)
    sin = np.zeros((64, L), np.float32)
    for i in range(64):
        blk, r = divmod(i, 32)
        a = ang_r if blk == 0 else ang_c
        cos[i] = np.cos(a[:, r % 16])
        sin[i] = np.sin(a[:, r % 16]) * sign[i]
    return np.concatenate([cos, cos], 0), np.concatenate([sin, sin], 0)


def _band_tables():
    ab = np.zeros((128, 20, 128), np.float32)
    a = np.arange(128)[:, None]
    b = np.arange(128)[None, :]
    for g, w in enumerate((2, 4, 8, 16)):
        h = w // 2
        ab[:, g * 5 + 0, :] = np.where(a - 128 >= b - h, 1.0 / w, 0.0)
        cur = np.where((a >= b - h) & (a <= b + h - 1), 1.0 / w, 0.0)
        ab[:, g * 5 + 1, :] = cur - (a == b)
        ab[:, g * 5 + 2, :] = np.where(128 + a <= b + h - 1, 1.0 / w, 0.0)
        lo = np.maximum(b - h, 0)
        hi = b + h - 1
        cnt = (hi - lo + 1).astype(np.float32)
        ab[:, g * 5 + 3, :] = np.where((a >= lo) & (a <= hi), 1.0 / cnt, 0.0) - (a == b)
        lo = b - h
        hi = np.minimum(b + h - 1, 127)
        cnt = (hi - lo + 1).astype(np.float32)
        ab[:, g * 5 + 4, :] = np.where((a >= lo) & (a <= hi), 1.0 / cnt, 0.0) - (a == b)
    return ab


def _chunk(v, n):
    return np.ascontiguousarray(np.asarray(v, np.float32).reshape(n, 128).T)


def _prep_shared(inp, L, DEPTH):
    partner, _ = _partner_sign()
    a128 = np.arange(128)
    a64 = np.arange(64)
    cols = []
    for i in range(3):
        cols.append(256 + i * 128 + a128)
    for i in range(3):
        cols.append(640 + i * 128 + a128)
    for i in range(3):
        cols.append(np.concatenate([1408 + i * 64 + a64, 1408 + (i + 3) * 64 + a64]))
    cols.append(1792 + a128)
    for i in range(3):
        cols.append(np.concatenate([1408 + i * 64 + partner, 1408 + (i + 3) * 64 + partner]))
    cols.append(np.concatenate([1792 + partner, 1856 + partner]))
    cols.append(np.arange(0, 256))
    cols.append(1024 + np.arange(384))
    cols.append(1920 + a128)
    cols = np.concatenate(cols)
    assert cols.shape[0] == NWIN
    rows = [np.arange(0, 640)]
    for i in range(3):
        rows.append(640 + i * 64 + a64)
        rows.append(640 + (i + 3) * 64 + a64)
    rows = np.concatenate(rows)
    sh = {}
    sh['w_mod'] = np.ascontiguousarray(inp['w_mod'], np.float32)
    bm = np.stack([_chunk(inp['b_mod'][l], 48) for l in range(DEPTH)], 0)
    sh['bmod'] = np.ascontiguousarray(np.repeat(bm, 2, axis=2))
    sh['w_in'] = np.ascontiguousarray(np.asarray(inp['w_in'], np.float32)[:, :, cols])
    sh['w_out'] = np.ascontiguousarray(np.asarray(inp['w_out'], np.float32)[:, rows, :])
    sh['w_up'] = np.ascontiguousarray(inp['w_up'], np.float32)
    sh['w_down'] = np.ascontiguousarray(inp['w_down'], np.float32)
    sh['pw'] = np.ascontiguousarray(np.asarray(inp['pool_w'], np.float32).reshape(DEPTH, 2, 128, 64))
    vecs = np.zeros((DEPTH, 128, NVEC), np.float32)
    for l in range(DEPTH):
        vecs[l, :, 0:8] = _chunk(inp['ln1_g'][l], 8)
        vecs[l, :, 8:16] = _chunk(inp['ln1_b'][l], 8)
        vecs[l, :, 16:24] = _chunk(inp['ln2_g'][l], 8)
        vecs[l, :, 24:32] = _chunk(inp['ln2_b'][l], 8)
        for j in range(3):
            vecs[l, :, 32 + j * 44:32 + (j + 1) * 44] = _chunk(inp['conv_w'][l, j], 44)
        vecs[l, :, 164:208] = _chunk(inp['conv_b'][l], 44)
        vecs[l, :, 208:210] = _chunk(inp['pool_scale'][l], 2)
        qn = np.asarray(inp['q_norm'][l], np.float32)
        kn = np.asarray(inp['k_norm'][l], np.float32)
        vecs[l, :, 210] = np.concatenate([qn, qn])
        vecs[l, :, 211] = np.concatenate([qn[partner], qn[partner]])
        vecs[l, :, 212] = np.concatenate([kn, kn])
        vecs[l, :, 213] = np.concatenate([kn[partner], kn[partner]])
    sh['vecs'] = vecs
    rpb = np.asarray(inp['na_rpb'], np.float32)
    kc = np.arange(64)[:, None]
    qc = np.arange(64)[None, :]
    cs = np.clip(qc - 8, 0, 48)
    valid = (kc >= cs) & (kc < cs + 16)
    dc = np.clip(kc - qc + 15, 0, 30)
    nab = np.full((DEPTH, 64, 6, 15, 64), NEG, np.float32)
    for l in range(DEPTH):
        for h in range(6):
            for idx in range(15):
                g = rpb[l, h, 14 - idx][dc]
                nab[l, :, h, idx, :] = np.where(valid, g, np.float32(NEG))
    sh['nab'] = nab
    sh['ab'] = _band_tables()
    cos, sinp = _rope_tables(L)
    sh['cos'] = cos
    sh['sinp'] = sinp
    return sh


class _S:
    pass


DEBUG = False
STOP = None


class _Stop(Exception):
    pass


def build(L, DEPTH):
    assert L % 512 == 0
    ROWS = L // GRID_W
    alpha = float((2 * DEPTH) ** 0.25)
    eps_ln = float(LN_EPS / (alpha * alpha))
    nc = bass.Bass("TRN2", target_bir_lowering=False)

    def din(name, shape, dt=F32):
        return nc.dram_tensor(name, list(shape), dt, kind="ExternalInput").ap()

    def dscr(name, shape, dt):
        return nc.dram_tensor(name, list(shape), dt, kind=("ExternalOutput" if DEBUG else "Internal")).ap()

    def stop(tag):
        if STOP == tag:
            raise _Stop()
    x_d = din("x", [L, D])
    ctx_d = din("ctx", [CTX, D])
    cvec_d = din("cvec", [128, 16])
    wmod_d = din("w_mod", [DEPTH, D, 6 * D])
    bmod_d = din("bmod", [DEPTH, 128, 96])
    win_d = din("w_in", [DEPTH, D, NWIN])
    wout_d = din("w_out", [DEPTH, D, D])
    wup_d = din("w_up", [DEPTH, D, 2 * DFF])
    wdn_d = din("w_down", [DEPTH, DFF, D])
    pw_d = din("pw", [DEPTH, 2, 128, 64])
    vecs_d = din("vecs", [DEPTH, 128, NVEC])
    nab_d = din("nab", [DEPTH, 64, 6, 15, 64])
    ab_d = din("ab", [128, 20, 128])
    cos_d = din("cos", [128, L])
    sinp_d = din("sinp", [128, L])
    out_d = nc.dram_tensor("out", [L, D], F32, kind="ExternalOutput").ap()
    dum_d = dscr("dum", [128, 8], F32)

    lat = _S()
    cx = _S()
    for s, n, LL, T in ((lat, "l", L, 512), (cx, "c", CTX, 256)):
        s.L = LL
        s.T = T
        s.ctx = n == "c"
        s.n = n
        s.xT = dscr("xT" + n, [128, 8, LL], F32)
        s.x1T = dscr("x1T" + n, [128, 8, LL], F32)
        s.qnT = dscr("qnT" + n, [128, 3, LL], BF16)
        s.qgT = dscr("qgT" + n, [128, 3, LL], BF16)
        s.pin = dscr("pin" + n, [LL, 256], BF16)
        s.catT = dscr("catT" + n, [128, 8, LL], BF16)
        s.actT = dscr("actT" + n, [128, NJ, LL], BF16)
    lat.knT = dscr("knTl", [128, 3, L], BF16)
    lat.vn = dscr("vnl", [L, 384], BF16)

    P = Prog(nc)
    ar = Arena(nc)
    ps = nc.alloc_psum_tensor("ps", [128, 4096], F32)

    def bank(b, n=512, p0=0, p1=128):
        return ps[p0:p1, b * 512:b * 512 + n]

    identf = ar.alloc("identf", [128, 128], F32)
    identb = ar.alloc("identb", [128, 128], BF16)
    onesb = ar.alloc("onesb", [128, 128], BF16)
    blkb = ar.alloc("blkb", [128, 128], BF16)
    ABb = ar.alloc("ABb", [128, 20, 128], BF16)
    MODTs = [ar.alloc(f"MODT{i}", [128, 6, 8, 2], F32) for i in range(2)]
    VECs = [ar.alloc(f"VEC{i}", [128, NVEC], F32) for i in range(2)]
    cur_l = [0]
    ACT2 = ar.alloc("ACT2", [128, 8, 2], F32)
    PWb = ar.alloc("PWb", [128, 2, 64], BF16)
    BMp = ar.alloc("BMp", [128, 96], F32)
    PWs = ar.alloc("PWs", [128, 2, 64], F32)
    M2c = [ar.alloc(f"M2c{i}", [128, 256], F32) for i in range(2)]
    dumt = {e: ar.alloc("dum" + e, [128, 8], F32) for e in ('act', 'dve', 'pool', 'sp')}
    WST = [ar.alloc(f"WST{i}", [128, 8, 256], F32) for i in range(3)]
    NKT = 2 + L // 128
    base_mark = ar.mark()

    dummies = {
        'pe': lambda e: e.matmul(ps[:, 0:1], lhsT=identb[:, 0:128], rhs=identb[:, 0:1], start=True, stop=True),
        'act': lambda e: e.activation(out=dumt['act'][:, 0:1], in_=dumt['act'][:, 1:2], func=AF.Copy),
        'dve': lambda e: e.memset(dumt['dve'][:, 0:1], 0.0),
        'pool': lambda e: e.memset(dumt['pool'][:, 0:1], 0.0),
        'sp': lambda e: e.dma_start(out=dumt['sp'][:, :], in_=dum_d[:, :]),
    }

    def dma(out, in_, reads, writes, chan):
        P.add('sp', lambda e: e.dma_start(out=out, in_=in_), reads=reads, writes=writes, chan=chan)

    def mm(out, pairs, reads, writes, tps=None):
        def fn(e):
            n = len(pairs)
            ins = None
            for i, (lt, r) in enumerate(pairs):
                kw = {}
                if tps is not None:
                    kw['tile_position'] = tps[i]
                ins = e.matmul(out, lhsT=lt, rhs=r, start=(i == 0), stop=(i == n - 1), **kw)
            return ins
        P.add('pe', fn, reads=reads, writes=writes)

    def mms(lst, reads, writes):
        def fn(e):
            ins = None
            for (o, lt, r, s0, s1, tp) in lst:
                kw = {}
                if tp is not None:
                    kw['tile_position'] = tp
                ins = e.matmul(o, lhsT=lt, rhs=r, start=s0, stop=s1, **kw)
            return ins
        P.add('pe', fn, reads=reads, writes=writes)

    def act(out, in_, func, reads, writes, scale=None, bias=None):
        kw = {}
        if scale is not None:
            kw['scale'] = scale
        if bias is not None:
            kw['bias'] = bias
        P.add('act', lambda e: e.activation(out=out, in_=in_, func=func, **kw), reads=reads, writes=writes)

    def tcopy(eng, out, in_, reads, writes):
        if eng == 'act':
            act(out, in_, AF.Copy, reads, writes)
        else:
            P.add(eng, lambda e: e.tensor_copy(out=out, in_=in_), reads=reads, writes=writes)

    def tt(eng, out, in0, in1, op, reads, writes):
        P.add(eng, lambda e: e.tensor_tensor(out=out, in0=in0, in1=in1, op=op), reads=reads, writes=writes)

    def ts(eng, out, in0, s1, s2, op0, op1, reads, writes):
        if s2 is None:
            P.add(eng, lambda e: e.tensor_scalar(out=out, in0=in0, scalar1=s1, scalar2=None, op0=op0),
                  reads=reads, writes=writes)
        else:
            P.add(eng, lambda e: e.tensor_scalar(out=out, in0=in0, scalar1=s1, scalar2=s2, op0=op0, op1=op1),
                  reads=reads, writes=writes)

    def stt(eng, out, in0, sc, in1, op0, op1, reads, writes):
        P.add(eng, lambda e: e.scalar_tensor_tensor(out=out, in0=in0, scalar=sc, in1=in1, op0=op0, op1=op1),
              reads=reads, writes=writes)

    def recip(out, in_, reads, writes):
        P.add('dve', lambda e: e.reciprocal(out=out, in_=in_), reads=reads, writes=writes)

    def memset(eng, ap, v, writes):
        P.add(eng, lambda e: e.memset(ap, v), writes=writes)

    cast_rr = [0]

    def load_weight(dst_fn, src_fn, nchunks, wname, stshape=None):
        for ci in range(nchunks):
            sl = cast_rr[0] % 3
            eng = ('pool', 'dve', 'act')[cast_rr[0] % 3]
            cast_rr[0] += 1
            d = dst_fn(ci)
            s = src_fn(ci)
            shp = d.shape
            if len(shp) == 3:
                stv = WST[sl][:, 0:shp[1], 0:shp[2]]
            else:
                stv = WST[sl][:, 0, 0:shp[1]]
            stv = stv[0:shp[0]]
            dma(stv, s, [], [f'WST{sl}'], f'WST{sl}')
            tcopy(eng, d, stv, [f'WST{sl}'], [f'{wname}{ci}'])

    def MODap(part, c, s):
        return MODTs[cur_l[0] % 2][:, part, c, (1 if s.ctx else 0):(2 if s.ctx else 1)]

    memset('pool', identf[:], 0.0, ['identf'])
    P.add('pool', lambda e: e.affine_select(out=identf[:], in_=identf[:], pattern=[[-1, 128]],
                                            compare_op=ALU.not_equal, fill=1.0, base=0, channel_multiplier=1),
          reads=['identf'], writes=['identf'])
    tcopy('pool', identb[:], identf[:], ['identf'], ['identb'])
    memset('pool', onesb[:], 1.0, ['onesb'])
    memset('pool', blkb[:], 0.0, ['blkb'])
    memset('pool', blkb[0:64, 0:64], 1.0, ['blkb'])
    memset('pool', blkb[64:128, 64:128], 1.0, ['blkb'])
    for e_ in ('act', 'dve', 'pool'):
        memset('pool' if e_ == 'pool' else 'dve', dumt[e_][:], 0.0, ['dum' + e_])
    for q in range(3):
        n0, n1 = q * 7, min(20, q * 7 + 7)
        dma(WST[q][:, 0:n1 - n0, 0:128], ab_d[:, n0:n1, :], [], [f'WST{q}'], f'WST{q}')
        tcopy('dve', ABb[:, n0:n1, :], WST[q][:, 0:n1 - n0, 0:128], [f'WST{q}'], ['ABb'])
    dma(ACT2[:].rearrange("p a b -> p (a b)"), cvec_d[:, :], [], ['ACT2'], 'ACT2')
    act(ACT2[:], ACT2[:], AF.Silu, ['ACT2'], ['ACT2'])

    m0 = ar.mark()
    XIN = [ar.alloc(f"XIN{i}", [128, D], F32) for i in range(2)]
    XTS = [ar.alloc(f"XTS{i}", [128, 8, 512], F32) for i in range(2)]
    for s, src in ((cx, ctx_d), (lat, x_d)):
        for t in range(s.L // s.T):
            N = s.T
            xs = t % 2
            for sub in range(N // 128):
                j = t * (N // 128) + sub
                isl = j % 2
                dma(XIN[isl][:], src[j * 128:(j + 1) * 128, :], [], [f'XIN{isl}'], f'XIN{isl}')
                b0 = 2 * (j % 2)
                P.add('pe', (lambda isl=isl, b0=b0: lambda e: [e.transpose(ps[:, b0 * 512 + c * 128:b0 * 512 + (c + 1) * 128],
                                                                          XIN[isl][:, c * 128:(c + 1) * 128], identf[:])
                                                             for c in range(8)][-1])(),
                      reads=[f'XIN{isl}', 'identf'], writes=[f'ps{b0}', f'ps{b0 + 1}'])
                tcopy('act' if sub % 2 == 0 else 'dve', XTS[xs][:, :, sub * 128:(sub + 1) * 128],
                      ps[:, b0 * 512:b0 * 512 + 1024].rearrange("p (c n) -> p c n", c=8),
                      [f'ps{b0}', f'ps{b0 + 1}'], [f'XTS{xs}'])
            dma(s.xT[:, :, t * N:(t + 1) * N], XTS[xs][:, :, 0:N], [f'XTS{xs}'], [], f'XTS{xs}')
    P.barrier()
    ar.reset(m0)

    def ln_make(s, X, xres, N, gpart, gcol, bcol, mm_fn, tmp, sbk):
        YB, SQ, MEAN, MSQ, SD, RSTD, TT = tmp
        rr = (0, 1, 2)
        b6, b7 = sbk, sbk + 1

        def stats(c):
            mms([(bank(b6, N), onesb[:], YB[c % 2][:, 0:N], c == 0, c == 7, None),
                 (bank(b7, N), onesb[:], SQ[c % 2][:, 0:N], c == 0, c == 7, None)],
                [f'YB{c % 2}', f'SQ{c % 2}', 'onesb'], [f'ps{b6}', f'ps{b7}'])

        def head_chunk(c):
            b = rr[c % 3]
            mm_fn(c, b)
            if c >= 1:
                stats(c - 1)
            stt('dve', X[:, c, 0:N], bank(b, N), MODap(gpart, c, s), X[:, c, 0:N], ALU.mult, ALU.add,
                [f'ps{b}', xres + f'c{c}', f'MOD{cur_l[0] % 2}'], [xres + f'c{c}'])
            tcopy('act', YB[c % 2][:, 0:N], X[:, c, 0:N], [xres + f'c{c}'], [f'YB{c % 2}'])
            act(SQ[c % 2][:, 0:N], X[:, c, 0:N], AF.Square, [xres + f'c{c}'], [f'SQ{c % 2}'])

        def head_post():
            stats(7)

        def tail_pre():
            ts('dve', MEAN[:, 0:N], bank(b6, N), 1.0 / D, None, ALU.mult, None, [f'ps{b6}'], ['MEAN'])
            tt('pool', MSQ[:, 0:N], MEAN[:, 0:N], MEAN[:, 0:N], ALU.mult, ['MEAN'], ['MSQ'])
            stt('dve', SD[:, 0:N], bank(b7, N), 1.0 / D, MSQ[:, 0:N], ALU.mult, ALU.subtract, [f'ps{b7}', 'MSQ'], ['SD'])
            ts('dve', SD[:, 0:N], SD[:, 0:N], eps_ln, None, ALU.add, None, ['SD'], ['SD'])
            act(SD[:, 0:N], SD[:, 0:N], AF.Sqrt, ['SD'], ['SD'])
            recip(RSTD[:, 0:N], SD[:, 0:N], ['SD'], ['RSTD'])

        def tail_chunk(c):
            tt('pool', TT[c % 2][:, 0:N], X[:, c, 0:N], MEAN[:, 0:N], ALU.subtract,
               [xres + f'c{c}', 'MEAN'], [f'TT{c % 2}'])
            tt('dve', TT[c % 2][:, 0:N], TT[c % 2][:, 0:N], RSTD[:, 0:N], ALU.mult,
               [f'TT{c % 2}', 'RSTD'], [f'TT{c % 2}'])
            VEC_ = VECs[cur_l[0] % 2]
            act(X[:, c, 0:N], TT[c % 2][:, 0:N], AF.Identity, [f'TT{c % 2}', f'VEC{cur_l[0] % 2}'], [xres + f'c{c}'],
                scale=VEC_[:, gcol + c:gcol + c + 1], bias=VEC_[:, bcol + c:bcol + c + 1])
        return head_chunk, head_post, tail_pre, tail_chunk

    def ln_pipeline(tiles, make, load, store):
        n = len(tiles)
        load(0)
        prev = None
        for idx in range(n + 1):
            cur = make(idx) if idx < n else None
            if idx + 1 < n:
                load(idx + 1)
            if prev is not None:
                prev[2]()
            for c in range(8):
                if cur is not None:
                    cur[0](c)
                if prev is not None:
                    prev[3](c)
            if cur is not None:
                cur[1]()
            if prev is not None:
                store(idx - 1)
            prev = cur

    def alloc_ln_tmp():
        YB = [ar.alloc(f"YB{i}", [128, 512], BF16) for i in range(2)]
        SQ = [ar.alloc(f"SQ{i}", [128, 512], BF16) for i in range(2)]
        MEAN = ar.alloc("MEAN", [128, 512], F32)
        MSQ = ar.alloc("MSQ", [128, 512], F32)
        SD = ar.alloc("SD", [128, 512], F32)
        RSTD = ar.alloc("RSTD", [128, 512], F32)
        TT = [ar.alloc(f"TT{i}", [128, 512], F32) for i in range(2)]
        return (YB, SQ, MEAN, MSQ, SD, RSTD, TT)

    def _layers():
        stop('pro')
        for l in range(DEPTH):
            last_layer = l == DEPTH - 1
            streams = [cx, lat]
            mL = ar.mark()
            KTg = ar.alloc("KTg", [128, CTX + L], BF16)
            VG = ar.alloc("VG", [128, NKT, 2, 128], BF16)
            KNC = ar.alloc("KNC", [128, 3, CTX], BF16)
            VNC = ar.alloc("VNC", [128, 6, 2, 128], BF16)
            memset('pool', VG[:], 1.0, ['VG'])
            memset('pool', VNC[:], 1.0, ['VNC'])
            cur_l[0] = l
            VEC = VECs[l % 2]
            MODT = MODTs[l % 2]
            MODr = f'MOD{l % 2}'
            VECr = f'VEC{l % 2}'

            def m_steps(lm):
                MT = MODTs[lm % 2]
                mr = f'MOD{lm % 2}'
                steps = []

                def first():
                    dma(VECs[lm % 2][:], vecs_d[lm], [], [f'VEC{lm % 2}'], f'VEC{lm % 2}')
                    dma(BMp[:], bmod_d[lm], [], ['BMp'], 'BMp')
                steps.append(first)
                wmv = wmod_d[lm].rearrange("(k p) n -> p k n", p=128)

                def tr(q):
                    def fn(e):
                        ins = None
                        for jj in range(2):
                            j = q * 2 + jj
                            ins = e.transpose(ps[:, 3584 + 256 + j * 2:3584 + 256 + j * 2 + 2],
                                              M2c[q % 2][0:2, jj * 128:(jj + 1) * 128], identf[0:2, 0:2])
                        return ins
                    P.add('pe', fn, reads=[f'M2c{q % 2}', 'identf'], writes=['ps7t'])
                for q in range(24):
                    def chunk(q=q):
                        sl = q % 3
                        dma(WST[sl][:], wmv[:, :, q * 256:(q + 1) * 256], [], [f'WST{sl}'], f'WST{sl}')
                        mms([(ps[0:2, 3584:3584 + 256], ACT2[:, k, :], WST[sl][:, k, :], k == 0, k == 7, None)
                             for k in range(8)], [f'WST{sl}', 'ACT2'], ['ps7'])
                        if q >= 1:
                            tr(q - 1)
                        tcopy('dve', M2c[q % 2][0:2, :], ps[0:2, 3584:3584 + 256], ['ps7', 'ps7t'], [f'M2c{q % 2}'])
                    steps.append(chunk)

                def fin():
                    tr(23)
                    tt('dve', MT[:].rearrange("p a b c -> p (a b c)"), ps[:, 3584 + 256:3584 + 256 + 96], BMp[:], ALU.add,
                       ['ps7t', 'ps7', 'BMp'], [mr])
                    for part in (1, 4):
                        ts('dve', MT[:, part], MT[:, part], 1.0, None, ALU.add, None, [mr], [mr])
                    for part in (2, 5):
                        ts('dve', MT[:, part], MT[:, part], 1.0 / alpha, None, ALU.mult, None, [mr], [mr])
                    dma(PWs[:], pw_d[lm].rearrange("i p d -> p i d"), [], ['PWs'], 'PWs')
                    tcopy('dve', PWb[:], PWs[:], ['PWs'], ['PWb'])
                steps.append(fin)
                return steps
            if l == 0:
                for f_ in m_steps(0):
                    f_()
            if DEBUG and l == 0:
                modt_d = nc.dram_tensor("modt", [128, 96], F32, kind="ExternalOutput").ap()
                dma(modt_d[:, :], MODT[:].rearrange("p a b c -> p (a b c)"), [MODr], [], 'XT')
                act2_d = nc.dram_tensor("act2", [128, 16], F32, kind="ExternalOutput").ap()
                dma(act2_d[:, :], ACT2[:].rearrange("p a b -> p (a b)"), ['ACT2'], [], 'XT')
            if DEBUG:
                P.barrier()
                stop(f'M{l}')

            m0 = ar.mark()
            WIN = ar.alloc("WIN", [128, 8, NWIN], BF16)
            XT = ar.alloc("XT", [128, 8, 512], F32)
            H = [ar.alloc(f"H{i}", [128, 8, 512], BF16) for i in range(2)]
            QNs = ar.alloc("QNs", [128, 3, 512], BF16)
            KNs = ar.alloc("KNs", [128, 3, 512], BF16)
            QGs = ar.alloc("QGs", [128, 3, 512], BF16)
            PINs = ar.alloc("PINs", [128, 4, 256], BF16)
            VNs = ar.alloc("VNs", [128, 4, 384], BF16)
            CS = [ar.alloc(f"CS{i}", [128, 2, 512], F32) for i in range(2)]
            SQb = [ar.alloc(f"SQb{i}", [128, 512], BF16) for i in range(2)]
            Vt = ar.alloc("Vt", [128, 512], F32)
            RS = ar.alloc("RS", [128, 512], F32)
            TA = [ar.alloc(f"TA{i}", [128, 512], F32) for i in range(2)]
            TB = [ar.alloc(f"TB{i}", [128, 512], F32) for i in range(2)]
            winv = win_d[l].rearrange("(k p) n -> p k n", p=128)
            load_weight(lambda ci: WIN[:, :, ci * 256:(ci + 1) * 256], lambda ci: winv[:, :, ci * 256:(ci + 1) * 256],
                        NWIN // 256, 'WIN')

            def wres(c0, c1):
                return [f'WIN{i}' for i in range(c0 // 256, (c1 - 1) // 256 + 1)]
            if STOP == f'Aw{l}':
                P.barrier()
                stop(f'Aw{l}')
            fmrr = [0]
            tcnt = 0
            a_tiles = [(s, t) for s in streams for t in range(s.L // s.T)]

            def a_load(idx):
                s, t = a_tiles[idx]
                N = s.T
                t0 = t * N
                hs = idx % 2
                dma(XT[:, :, 0:N], s.xT[:, :, t0:t0 + N], [], ['XT'], 'XT')
                if not s.ctx:
                    dma(CS[hs][:, 0, :], cos_d[:, t0:t0 + N], [], [f'CS{hs}'], f'CS{hs}')
                    dma(CS[hs][:, 1, :], sinp_d[:, t0:t0 + N], [], [f'CS{hs}'], f'CS{hs}')
            def a_h(idx):
                s, t = a_tiles[idx]
                N = s.T
                hs = idx % 2
                for c in range(8):
                    act(H[hs][:, c, 0:N], XT[:, c, 0:N], AF.Identity, ['XT', MODr], [f'H{hs}'],
                        scale=MODap(1, c, s), bias=MODap(0, c, s))
            a_load(0)
            a_h(0)
            if len(a_tiles) > 1:
                a_load(1)
            rcnt = [0]
            for a_idx, (s, t) in enumerate(a_tiles):
                N = s.T
                if True:
                    t0 = t * N
                    hs = tcnt % 2
                    tcnt += 1

                    def fm(m, hs=hs, N=N):
                        b = fmrr[0] % 3
                        fmrr[0] += 1
                        mm(bank(b, N), [(WIN[:, k, m * 128:(m + 1) * 128], H[hs][:, k, 0:N]) for k in range(8)],
                           [f'H{hs}'] + wres(m * 128, (m + 1) * 128), [f'ps{b}'])
                        return b
                    need_q = not (s.ctx and last_layer)
                    if STOP == f'Ah{l}':
                        P.barrier()
                        stop(f'Ah{l}')
                    if need_q:
                        for m in range(3):
                            b = fm(m)
                            ts('dve', QNs[:, m, 0:N], bank(b, N), 0.125, None, ALU.mult, None, [f'ps{b}'], ['QNs'])
                        dma(s.qnT[:, :, t0:t0 + N], QNs[:, :, 0:N], ['QNs'], [], 'QNs')
                    for m in range(3, 6):
                        b = fm(m)
                        if s.ctx:
                            tcopy('act', KNC[:, m - 3, 0:N], bank(b, N), [f'ps{b}'], ['KNC'])
                        else:
                            tcopy('act', KNs[:, m - 3, 0:N], bank(b, N), [f'ps{b}'], ['KNs'])
                    if not s.ctx:
                        dma(s.knT[:, :, t0:t0 + N], KNs[:, :, 0:N], ['KNs'], [], 'KNs')
                    if STOP == f'Ak{l}':
                        P.barrier()
                        stop(f'Ak{l}')
                    rope_ms = ([6, 7, 8] if need_q else []) + [9]
                    deferred = None
                    for m in rope_ms:
                        bz = fm(m)
                        bp = None if s.ctx else fm(m + 4)
                        gcol = 210 if m < 9 else 212
                        rq = rcnt[0] % 2
                        rcnt[0] += 1
                        act(SQb[rq][:, 0:N], bank(bz, N), AF.Square, [f'ps{bz}'], [f'SQb{rq}'])
                        if m < 9:
                            dst, dres = QGs[:, m - 6, 0:N], 'QGs'
                        else:
                            koff = 0 if s.ctx else CTX + t0
                            dst, dres = KTg[:, koff:koff + N], 'KTg'
                        if not s.ctx:
                            stt('dve', TA[rq][:, 0:N], bank(bz, N), VEC[:, gcol:gcol + 1], CS[hs][:, 0, 0:N], ALU.mult, ALU.mult,
                                [f'ps{bz}', f'CS{hs}', VECr], [f'TA{rq}'])
                            stt('dve', TB[rq][:, 0:N], bank(bp, N), VEC[:, gcol + 1:gcol + 2], CS[hs][:, 1, 0:N], ALU.mult,
                                ALU.mult, [f'ps{bp}', f'CS{hs}', VECr], [f'TB{rq}'])
                        if deferred is not None:
                            deferred()

                        def deferred(rq=rq, bz=bz, gcol=gcol, dst=dst, dres=dres, s=s, N=N):
                            mm(bank(3, N), [(blkb[:], SQb[rq][:, 0:N])], [f'SQb{rq}', 'blkb'], ['ps3'])
                            ts('dve', Vt[:, 0:N], bank(3, N), 1.0 / 64, LN_EPS, ALU.mult, ALU.add, ['ps3'], ['Vt'])
                            act(Vt[:, 0:N], Vt[:, 0:N], AF.Sqrt, ['Vt'], ['Vt'])
                            recip(RS[:, 0:N], Vt[:, 0:N], ['Vt'], ['RS'])
                            if s.ctx:
                                stt('dve', dst, bank(bz, N), VEC[:, gcol:gcol + 1], RS[:, 0:N], ALU.mult, ALU.mult,
                                    [f'ps{bz}', 'RS', VECr], [dres])
                            else:
                                tt('pool', TA[rq][:, 0:N], TA[rq][:, 0:N], TB[rq][:, 0:N], ALU.add, [f'TA{rq}', f'TB{rq}'],
                                   [f'TA{rq}'])
                                tt('pool', dst, TA[rq][:, 0:N], RS[:, 0:N], ALU.mult, [f'TA{rq}', 'RS'], [dres])
                    if deferred is not None:
                        deferred()
                    if need_q:
                        dma(s.qgT[:, :, t0:t0 + N], QGs[:, :, 0:N], ['QGs'], [], 'QGs')
                    if STOP == f'Ar{l}':
                        P.barrier()
                        stop(f'Ar{l}')
                    if a_idx + 1 < len(a_tiles):
                        a_h(a_idx + 1)
                    if a_idx + 2 < len(a_tiles):
                        a_load(a_idx + 2)
                    for sub in range(N // 128):
                        b0 = 4 + 2 * (sub % 2)
                        hsl = H[hs]
                        mms([(bank(b0), hsl[:, k, sub * 128:(sub + 1) * 128], WIN[:, k, 1792:2304], k == 0, k == 7, None)
                             for k in range(8)] +
                            [(bank(b0 + 1, 256), hsl[:, k, sub * 128:(sub + 1) * 128], WIN[:, k, 2304:2560], k == 0, k == 7, None)
                             for k in range(8)],
                            [f'H{hs}'] + wres(1792, 2560), [f'ps{b0}', f'ps{b0 + 1}'])
                        rd = [f'ps{b0}', f'ps{b0 + 1}']
                        import os
                        SK = os.environ.get('SK', '')
                        if 'a' in SK:
                            continue
                        if need_q and 'p' not in SK:
                            tcopy('act', PINs[:, sub, :], bank(b0, 256), [f'ps{b0}'], ['PINs'])
                        kt = sub if s.ctx else 2 + t * 4 + sub
                        if 'v' in SK:
                            continue
                        if s.ctx and 'n' not in SK:
                            for h in range(6):
                                src = ps[:, b0 * 512 + 256 + h * 64:b0 * 512 + 320 + h * 64] if h < 4 else \
                                    ps[:, (b0 + 1) * 512 + (h - 4) * 64:(b0 + 1) * 512 + (h - 3) * 64]
                                par = h % 2
                                tcopy('act', VNC[:, sub * 3 + h // 2, par, par * 64:(par + 1) * 64], src, rd, ['VNC'])
                        elif not s.ctx:
                            tcopy('act', VNs[:, sub, 0:256], ps[:, b0 * 512 + 256:b0 * 512 + 512], [f'ps{b0}'], ['VNs'])
                            tcopy('dve', VNs[:, sub, 256:384], bank(b0 + 1, 128), [f'ps{b0 + 1}'], ['VNs'])
                        if 'g' in SK:
                            continue
                        tcopy('dve', VG[:, kt, 0, 0:64], ps[:, (b0 + 1) * 512 + 128:(b0 + 1) * 512 + 192], [f'ps{b0 + 1}'], ['VG'])
                        tcopy('dve', VG[:, kt, 1, 64:128], ps[:, (b0 + 1) * 512 + 192:(b0 + 1) * 512 + 256], [f'ps{b0 + 1}'], ['VG'])
                    ns = N // 128
                    if STOP == f'At{l}':
                        P.barrier()
                        stop(f'At{l}')
                    if need_q:
                        dma(s.pin[t0:t0 + N, :].rearrange("(j p) c -> p j c", p=128), PINs[:, 0:ns, :], ['PINs'], [], 'PINs')
                    if not s.ctx:
                        dma(s.vn[t0:t0 + N, :].rearrange("(j p) c -> p j c", p=128), VNs[:, 0:ns, :], ['VNs'], [], 'VNs')
                    if STOP == f'Ac{l}':
                        P.barrier()
                        stop(f'Ac{l}')
            P.barrier()
            ar.reset(m0)
            stop(f'A{l}')
            if last_layer:
                streams = [lat]

            m0 = ar.mark()
            PINL = [ar.alloc(f"PINL{i}", [128, 6, 256], BF16) for i in range(2)]
            PLT = ar.alloc("PLT", [128, 512], BF16)
            CATs = [ar.alloc(f"CATs{i}", [128, 3, 512], BF16) for i in range(2)]
            tcnt = 0
            d_tiles = [(s, t) for s in streams for t in range(s.L // s.T)]

            def d_load(idx):
                s, t = d_tiles[idx]
                NSUB = s.T // 128
                NS = s.L // 128
                j0 = t * NSUB
                sl = idx % 2
                jlo = max(j0 - 1, 0)
                jhi = min(j0 + NSUB + 1, NS)
                dma(PINL[sl][:, jlo - (j0 - 1):jhi - (j0 - 1), :],
                    s.pin[jlo * 128:jhi * 128, :].rearrange("(j p) c -> p j c", p=128), [], [f'PINL{sl}'], f'PINL{sl}')
            d_load(0)
            for d_idx, (s, t) in enumerate(d_tiles):
                N = s.T
                NSUB = N // 128
                NS = s.L // 128
                if True:
                    t0 = t * N
                    j0 = t * NSUB
                    sl = tcnt % 2
                    tcnt += 1
                    if d_idx + 1 < len(d_tiles):
                        d_load(d_idx + 1)
                    for i in range(2):
                        lst = []
                        for sub in range(NSUB):
                            j = j0 + sub
                            for gg in range(2):
                                g = 2 * i + gg
                                terms = []
                                if j > 0:
                                    terms.append((sub, g * 5 + 0))
                                terms.append((sub + 1, g * 5 + (3 if j == 0 else 4 if j == NS - 1 else 1)))
                                if j < NS - 1:
                                    terms.append((sub + 2, g * 5 + 2))
                                for ti, (pi, ai) in enumerate(terms):
                                    lst.append((ps[gg * 64:(gg + 1) * 64, i * 512 + sub * 128:i * 512 + (sub + 1) * 128],
                                                PINL[sl][:, pi, g * 64:(g + 1) * 64], ABb[:, ai, :],
                                                ti == 0, ti == len(terms) - 1, (0, gg * 64)))
                        mms(lst, [f'PINL{sl}', 'ABb'], [f'ps{i}'])
                        tcopy('dve', PLT[:, 0:N], bank(i, N), [f'ps{i}'], ['PLT'])
                        mms([(ps[gg * 64:(gg + 1) * 64, (2 + i + 2 * gg) * 512:(2 + i + 2 * gg) * 512 + N],
                              PWb[gg * 64:(gg + 1) * 64, i, :],
                              PLT[gg * 64:(gg + 1) * 64, 0:N], True, True, (gg * 64, gg * 64)) for gg in range(2)],
                            ['PLT', 'PWb'], [f'ps{2 + i}', f'ps{4 + i}'])
                        for gg in range(2):
                            bb = 2 + i + 2 * gg
                            ts('dve', CATs[sl][gg * 64:(gg + 1) * 64, i, 0:N], bank(bb, N, gg * 64, (gg + 1) * 64),
                               VEC[gg * 64:(gg + 1) * 64, 208 + i:209 + i], None, ALU.mult, None,
                               [f'ps{bb}', VECr], [f'CATs{sl}'])
                    dma(s.catT[:, 0:2, t0:t0 + N], CATs[sl][:, 0:2, 0:N], [f'CATs{sl}'], [], f'CATs{sl}')
            P.barrier()
            ar.reset(m0)

            def attn_phase(kind):
                m0 = ar.mark()
                Q = [ar.alloc(f"Q{i}", [128, 3, 512], BF16) for i in range(2)]
                PT = [ar.alloc(f"PT{i}", [128, 1024], BF16) for i in range(3)]
                Rr = ar.alloc("Rr", [128, 512], F32)
                CATs = [ar.alloc(f"CATa{i}", [128, 3, 512], BF16) for i in range(2)]
                if kind == 'na':
                    KNR = ar.alloc("KNR", [128, 3, RING * 128], BF16)
                    VNR = ar.alloc("VNR", [128, RING * 3, 2, 128], BF16)
                    BIAS = ar.alloc("BIAS", [128, 6, 64, 64], BF16)
                    BT = ar.alloc("BT", [64, 6, 15, 64], BF16)
                    memset('pool', VNR[:], 1.0, [f'VNR{r}' for r in range(RING)])
                    for h in range(6):
                        q = h % 3
                        stv = WST[q][0:64].rearrange("p a b -> p (a b)")[:, 0:960]
                        dma(stv, nab_d[l][:, h].rearrange("p a b -> p (a b)"), [], [f'WST{q}'], f'WST{q}')
                        tcopy('dve', BT[:, h].rearrange("p a b -> p (a b)"), stv, [f'WST{q}'], ['BT'])
                cur_sig = [None]
                loaded = [0]
                ocnt = 0
                scnt = 0
                pcnt = 0
                coff = 2 if kind == 'na' else 5
                tiles = [(s, t) for s in streams for t in range(s.L // s.T)]
                items = []

                def load_q(idx):
                    s, t = tiles[idx]
                    N = s.T
                    qs = idx % 2
                    dma(Q[qs][:, :, 0:N], (s.qnT if kind == 'na' else s.qgT)[:, :, t * N:(t + 1) * N], [], [f'Q{qs}'], f'Q{qs}')

                for idx, (s, t) in enumerate(tiles):
                    N = s.T
                    t0 = t * N
                    qs = idx % 2
                    pre = []
                    if idx == 0:
                        pre.append(lambda: load_q(0))
                    if idx + 1 < len(tiles):
                        pre.append((lambda idx=idx: lambda: load_q(idx + 1))())
                    lat_slots = []
                    if kind == 'na' and not s.ctx:
                        R0 = 8 * t

                        def rs(r):
                            return min(max(r - 4, 0), ROWS - 8)
                        kt_lo = rs(R0) // 2
                        kt_hi = (rs(R0 + 7) + 8 + 1) // 2
                        sig = []
                        for sl_, kt in enumerate(range(kt_lo, kt_hi)):
                            for a in range(2):
                                kr = 2 * kt + a
                                bs = [b for b in range(8) if rs(R0 + b) <= kr < rs(R0 + b) + 8]
                                if bs:
                                    assert bs == list(range(bs[0], bs[-1] + 1))
                                    i0_ = 7 - kr + R0 + bs[0]
                                    assert 0 <= i0_ and i0_ + len(bs) <= 15
                                    sig.append((sl_, a, bs[0], len(bs), i0_))
                        sig = tuple(sig)
                        if sig != cur_sig[0]:
                            cur_sig[0] = sig

                            def asm(sig=sig):
                                memset('pool', BIAS[:], NEG, ['BIAS'])
                                for (sl_, a, b0_, nb, i0_) in sig:
                                    dma(BIAS[a * 64:(a + 1) * 64, :, sl_ * 8 + b0_:sl_ * 8 + b0_ + nb, :],
                                        BT[0:64, :, i0_:i0_ + nb, :], ['BT'], ['BIAS'], 'BIAS')
                            pre.append(asm)

                        def ldk(k0=loaded[0], k1=kt_hi):
                            for kt in range(k0, k1):
                                rsl = kt % RING
                                dma(KNR[:, :, rsl * 128:(rsl + 1) * 128], lat.knT[:, :, kt * 128:(kt + 1) * 128],
                                    [], [f'KNR{rsl}'], f'KNR{rsl}')
                                vsrc = lat.vn[kt * 128:(kt + 1) * 128, :].rearrange("p (i a d) -> p i a d", i=3, a=2)
                                dma(VNR[:, rsl * 3:rsl * 3 + 3, 0, 0:64], vsrc[:, :, 0, :], [], [f'VNR{rsl}'], f'VNR{rsl}')
                                dma(VNR[:, rsl * 3:rsl * 3 + 3, 1, 64:128], vsrc[:, :, 1, :], [], [f'VNR{rsl}'], f'VNR{rsl}')
                        pre.append(ldk)
                        loaded[0] = max(loaded[0], kt_hi)
                        lat_slots = list(enumerate(range(kt_lo, kt_hi)))
                    if kind == 'na':
                        keys = [('l', sl_, kt) for sl_, kt in lat_slots] + [('c', 0, 0), ('c', 0, 1)]
                    else:
                        nk = 2 if s.ctx else NKT
                        keys = [('g', 0, kt) for kt in range(nk)]
                    for i in range(3):
                        ob = 4 + 2 * (ocnt % 2)
                        ocnt += 1
                        for ki, (kk, sl_, kt) in enumerate(keys):
                            sb = 2 * (scnt % 2)
                            scnt += 1
                            pp = pcnt % 3
                            pcnt += 1
                            if kk == 'l':
                                rsl = kt % RING
                                kA = KNR[0:64, i, rsl * 128:(rsl + 1) * 128]
                                kB = KNR[64:128, i, rsl * 128:(rsl + 1) * 128]
                                vA = VNR[:, rsl * 3 + i, 0, :]
                                vB = VNR[:, rsl * 3 + i, 1, :]
                                kres = [f'KNR{rsl}', f'VNR{rsl}']
                            elif kk == 'c':
                                kA = KNC[0:64, i, kt * 128:(kt + 1) * 128]
                                kB = KNC[64:128, i, kt * 128:(kt + 1) * 128]
                                vA = VNC[:, kt * 3 + i, 0, :]
                                vB = VNC[:, kt * 3 + i, 1, :]
                                kres = ['KNC', 'VNC']
                            else:
                                kA = KTg[0:64, kt * 128:(kt + 1) * 128]
                                kB = KTg[64:128, kt * 128:(kt + 1) * 128]
                                vA = VG[:, kt, 0, :]
                                vB = VG[:, kt, 1, :]
                                kres = ['KTg', 'VG']
                            hasb = kk == 'l'
                            lst = [(bank(sb, N), kA, Q[qs][0:64, i, 0:N], True, not hasb, None),
                                   (bank(sb + 1, N), kB, Q[qs][64:128, i, 0:N], True, not hasb, None)]
                            rdl = [f'Q{qs}'] + kres
                            if hasb:
                                for par in range(2):
                                    lst.append((bank(sb + par, N), identb[:],
                                                BIAS[:, 2 * i + par, sl_ * 8:(sl_ + 1) * 8, :].rearrange("p a b -> p (a b)"),
                                                False, True, None))
                                rdl += ['BIAS', 'identb']
                            first_of_tile = (i == 0 and ki == 0)

                            def qk(lst=lst, rdl=rdl, sb=sb, pre=pre, first_of_tile=first_of_tile):
                                if first_of_tile:
                                    for f_ in pre:
                                        f_()
                                mms(lst, rdl, [f'ps{sb}', f'ps{sb + 1}'])

                            def rest(sb=sb, pp=pp, N=N, ob=ob, vA=vA, vB=vB, kres=kres, ki=ki, nk_=len(keys), qs=qs, i=i,
                                     s=s, t0=t0):
                                sc_ = 1.0 if kind == 'na' else 0.125
                                if N == 512:
                                    act(PT[pp][:], ps[:, sb * 512:sb * 512 + 1024], AF.Exp, [f'ps{sb}', f'ps{sb + 1}'],
                                        [f'PT{pp}'], scale=sc_)
                                else:
                                    act(PT[pp][:].rearrange("p (a n) -> p a n", a=2)[:, :, 0:N],
                                        ps[:, sb * 512:sb * 512 + 1024].rearrange("p (a n) -> p a n", a=2)[:, :, 0:N],
                                        AF.Exp, [f'ps{sb}', f'ps{sb + 1}'], [f'PT{pp}'], scale=sc_)
                                mms([(bank(ob, N), vA, PT[pp][:, 0:N], ki == 0, ki == nk_ - 1, None),
                                     (bank(ob + 1, N), vB, PT[pp][:, 512:512 + N], ki == 0, ki == nk_ - 1, None)],
                                    [f'PT{pp}'] + kres, [f'ps{ob}', f'ps{ob + 1}'])
                                if ki == nk_ - 1:
                                    recip(Rr[0:64, 0:N], bank(ob, N, 64, 128), [f'ps{ob}'], ['Rr'])
                                    recip(Rr[64:128, 0:N], bank(ob + 1, N, 0, 64), [f'ps{ob + 1}'], ['Rr'])
                                    tt('dve', CATs[qs][0:64, i, 0:N], bank(ob, N, 0, 64), Rr[0:64, 0:N], ALU.mult,
                                       [f'ps{ob}', 'Rr'], [f'CATa{qs}'])
                                    tt('dve', CATs[qs][64:128, i, 0:N], bank(ob + 1, N, 64, 128), Rr[64:128, 0:N], ALU.mult,
                                       [f'ps{ob + 1}', 'Rr'], [f'CATa{qs}'])
                                    if i == 2:
                                        dma(s.catT[:, coff:coff + 3, t0:t0 + N], CATs[qs][:, :, 0:N], [f'CATa{qs}'], [],
                                            f'CATa{qs}')
                            items.append((qk, rest))
                for n_ in range(len(items) + 1):
                    if n_ < len(items):
                        items[n_][0]()
                    if n_ >= 1:
                        items[n_ - 1][1]()
                P.barrier()
                ar.reset(m0)
            stop(f'D{l}')
            attn_phase('na')
            stop(f'C{l}')
            attn_phase('gqa')
            stop(f'B{l}')
            ar.reset(mL)

            m0 = ar.mark()
            WOUT = ar.alloc("WOUT", [128, 8, D], BF16)
            CT = [ar.alloc(f"CT{i}", [128, 8, 512], BF16) for i in range(3)]
            XR = [ar.alloc(f"XR{i}", [128, 8, 512], F32) for i in range(3)]
            tmp = alloc_ln_tmp()
            woutv = wout_d[l].rearrange("(k p) n -> p k n", p=128)
            load_weight(lambda ci: WOUT[:, :, ci * 256:(ci + 1) * 256], lambda ci: woutv[:, :, ci * 256:(ci + 1) * 256],
                        4, 'WOUT')
            e_tiles = [(s, t) for s in streams for t in range(s.L // s.T)]

            def e1_load(idx):
                s, t = e_tiles[idx]
                N = s.T
                t0 = t * N
                sl = idx % 3
                dma(CT[sl][:, :, 0:N], s.catT[:, :, t0:t0 + N], [], [f'CT{sl}'], f'CT{sl}')
                dma(XR[sl][:, :, 0:N], s.xT[:, :, t0:t0 + N], [], [f'XR{sl}c{c}' for c in range(8)], f'XR{sl}')

            def e1_make(idx):
                s, t = e_tiles[idx]
                N = s.T
                sl = idx % 3

                def mmf(c, b):
                    mm(bank(b, N), [(WOUT[:, k, c * 128:(c + 1) * 128], CT[sl][:, k, 0:N]) for k in range(8)],
                       [f'CT{sl}'] + [f'WOUT{c // 2}'], [f'ps{b}'])
                return ln_make(s, XR[sl], f'XR{sl}', N, 2, 0, 8, mmf, tmp, 4 + 2 * (idx % 2))

            def e1_store(idx):
                s, t = e_tiles[idx]
                N = s.T
                sl = idx % 3
                dma(s.x1T[:, :, t * N:(t + 1) * N], XR[sl][:, :, 0:N], [f'XR{sl}c{c}' for c in range(8)], [], f'XRst{sl}')
            ln_pipeline(e_tiles, e1_make, e1_load, e1_store)
            P.barrier()
            ar.reset(m0)

            stop(f'E1{l}')
            m0 = ar.mark()
            WUP = ar.alloc("WUP", [128, 8, 2 * DFF], BF16)
            X1B = [ar.alloc(f"X1B{i}", [128, 8, 512], F32) for i in range(2)]
            H2 = [ar.alloc(f"H2{i}", [128, 8, 512], BF16) for i in range(2)]
            TAc = [ar.alloc(f"TAc{i}", [128, 512], F32) for i in range(2)]
            TGc = [ar.alloc(f"TGc{i}", [128, 512], F32) for i in range(2)]
            SGc = [ar.alloc(f"SGc{i}", [128, 512], F32) for i in range(2)]
            AOs = [ar.alloc(f"AOs{i}", [128, 512], BF16) for i in range(3)]
            wupv = wup_d[l].rearrange("(k p) n -> p k n", p=128)
            load_weight(lambda ci: WUP[:, :, ci * 256:(ci + 1) * 256], lambda ci: wupv[:, :, ci * 256:(ci + 1) * 256],
                        22, 'WUP')
            jcnt = 0
            f_tiles = []
            for s in streams:
                nf = (s.L + 509) // 510
                for ti in range(nf):
                    f_tiles.append((s, ti))

            def f_geom(idx):
                s, ti = f_tiles[idx]
                out_lo = 510 * ti
                out_hi = min(s.L, out_lo + 510)
                n_out = out_hi - out_lo
                N = n_out + 2
                tok_lo = max(out_lo - 1, 0)
                tok_hi = min(out_hi + 1, s.L)
                col_lo = tok_lo - (out_lo - 1)
                ncol = tok_hi - tok_lo
                return s, out_lo, out_hi, n_out, N, tok_lo, tok_hi, col_lo, ncol

            def f_load(idx):
                s, out_lo, out_hi, n_out, N, tok_lo, tok_hi, col_lo, ncol = f_geom(idx)
                xs = idx % 2
                dma(X1B[xs][:, :, col_lo:col_lo + ncol], s.x1T[:, :, tok_lo:tok_hi], [], [f'X1B{xs}'], f'X1B{xs}')

            def f_h2(idx):
                s, out_lo, out_hi, n_out, N, tok_lo, tok_hi, col_lo, ncol = f_geom(idx)
                xs = idx % 2
                for c in range(8):
                    act(H2[xs][:, c, col_lo:col_lo + ncol], X1B[xs][:, c, col_lo:col_lo + ncol], AF.Identity,
                        [f'X1B{xs}', MODr], [f'H2{xs}'], scale=MODap(4, c, s), bias=MODap(3, c, s))
                if col_lo == 1:
                    memset('pool', H2[xs][:, :, 0:1], 0.0, [f'H2{xs}'])
                if col_lo + ncol < N:
                    memset('pool', H2[xs][:, :, N - 1:N], 0.0, [f'H2{xs}'])
            f_load(0)
            f_h2(0)
            pend_m = m_steps(l + 1) if l + 1 < DEPTH else []
            for f_idx in range(len(f_tiles)):
                s, out_lo, out_hi, n_out, N, tok_lo, tok_hi, col_lo, ncol = f_geom(f_idx)
                xs = f_idx % 2
                if f_idx + 1 < len(f_tiles):
                    f_load(f_idx + 1)
                for j in range(NJ):
                    if j == 10 and f_idx + 1 < len(f_tiles):
                        f_h2(f_idx + 1)
                    if j in (3, 14) and pend_m and f_idx >= 1:
                        pend_m.pop(0)()
                    bs_ = 2 * (jcnt % 3)
                    cs_ = jcnt % 2
                    ao = jcnt % 3
                    jcnt += 1
                    halves = ((0, bs_, TAc[cs_], 'TAc'), (1, bs_ + 1, TGc[cs_], 'TGc'))
                    for half, bb, Tc, tn in halves:
                        col = half * DFF + j * 128
                        mm(bank(bb, N), [(WUP[:, k, col:col + 128], H2[xs][:, k, 0:N]) for k in range(8)],
                           [f'H2{xs}', f'WUP{col // 256}'], [f'ps{bb}'])
                    for half, bb, Tc, tn in halves:
                        vm = half * NJ + j
                        act(Tc[:, 0:n_out], ps[:, bb * 512 + 1:bb * 512 + 1 + n_out], AF.Identity, [f'ps{bb}', VECr],
                            [f'{tn}{cs_}'], scale=VEC[:, 32 + 44 + vm:32 + 44 + vm + 1], bias=VEC[:, 164 + vm:164 + vm + 1])
                    for tap, off in ((0, 0), (2, 2)):
                        for half, bb, Tc, tn in halves:
                            vm = half * NJ + j
                            stt('dve', Tc[:, 0:n_out], ps[:, bb * 512 + off:bb * 512 + off + n_out],
                                VEC[:, 32 + tap * 44 + vm:32 + tap * 44 + vm + 1], Tc[:, 0:n_out], ALU.mult, ALU.add,
                                [f'ps{bb}', f'{tn}{cs_}', VECr], [f'{tn}{cs_}'])
                    act(SGc[cs_][:, 0:n_out], TGc[cs_][:, 0:n_out], AF.Silu, [f'TGc{cs_}'], [f'SGc{cs_}'])
                    tt('pool', AOs[ao][:, 0:n_out], TAc[cs_][:, 0:n_out], SGc[cs_][:, 0:n_out], ALU.mult,
                       [f'TAc{cs_}', f'SGc{cs_}'], [f'AOs{ao}'])
                    dma(s.actT[:, j, out_lo:out_hi], AOs[ao][:, 0:n_out], [f'AOs{ao}'], [], f'AOs{ao}')
            while pend_m:
                pend_m.pop(0)()
            P.barrier()
            ar.reset(m0)

            stop(f'E2a{l}')
            m0 = ar.mark()
            WDN = ar.alloc("WDN", [128, NJ, D], BF16)
            AT = [ar.alloc(f"AT{i}", [128, NJ, 512], BF16) for i in range(2)]
            XR = [ar.alloc(f"XR2{i}", [128, 8, 512], F32) for i in range(3)]
            tmp = alloc_ln_tmp()
            wdnv = wdn_d[l].rearrange("(j p) n -> p j n", p=128)
            load_weight(lambda ci: WDN[:, 2 * (ci // 4):2 * (ci // 4) + 2, (ci % 4) * 256:(ci % 4 + 1) * 256],
                        lambda ci: wdnv[:, 2 * (ci // 4):2 * (ci // 4) + 2, (ci % 4) * 256:(ci % 4 + 1) * 256], 44, 'WDN')
            wdn_res = [f'WDN{ci}' for ci in range(44)]
            e_tiles = [(s, t) for s in streams for t in range(s.L // s.T)]

            def e2b_load(idx):
                s, t = e_tiles[idx]
                N = s.T
                t0 = t * N
                sl = idx % 3
                al = idx % 2
                dma(AT[al][:, :, 0:N], s.actT[:, :, t0:t0 + N], [], [f'AT{al}'], f'AT{al}')
                dma(XR[sl][:, :, 0:N], s.x1T[:, :, t0:t0 + N], [], [f'XR{sl}c{c}' for c in range(8)], f'XR{sl}')

            def e2b_make(idx):
                s, t = e_tiles[idx]
                N = s.T
                sl = idx % 3
                al = idx % 2

                def mmf(c, b):
                    mm(bank(b, N), [(WDN[:, j, c * 128:(c + 1) * 128], AT[al][:, j, 0:N]) for j in range(NJ)],
                       [f'AT{al}'] + wdn_res, [f'ps{b}'])
                return ln_make(s, XR[sl], f'XR{sl}', N, 5, 16, 24, mmf, tmp, 4 + 2 * (idx % 2))

            def e2b_store(idx):
                s, t = e_tiles[idx]
                N = s.T
                sl = idx % 3
                dma(s.xT[:, :, t * N:(t + 1) * N], XR[sl][:, :, 0:N], [f'XR{sl}c{c}' for c in range(8)], [], f'XRst{sl}')
            ln_pipeline(e_tiles, e2b_make, e2b_load, e2b_store)
            P.barrier()
            ar.reset(m0)
            stop(f'E2b{l}')

        XE = [ar.alloc(f"XE{i}", [128, 8, 512], F32) for i in range(2)]
        OT = [ar.alloc(f"OT{i}", [128, D], F32) for i in range(2)]
        ocnt = 0
        for t in range(L // 512):
            xs = t % 2
            dma(XE[xs][:], lat.xT[:, :, t * 512:(t + 1) * 512], [], [f'XE{xs}'], f'XE{xs}')
            for sub in range(4):
                osl = ocnt % 2
                b0 = 2 * (ocnt % 2)
                ocnt += 1
                P.add('pe', (lambda xs=xs, sub=sub, b0=b0: lambda e: [
                    e.transpose(ps[:, b0 * 512 + c * 128:b0 * 512 + (c + 1) * 128], XE[xs][:, c, sub * 128:(sub + 1) * 128], identf[:])
                    for c in range(8)][-1])(), reads=[f'XE{xs}', 'identf'], writes=[f'ps{b0}', f'ps{b0 + 1}'])
                tcopy('act' if sub % 2 == 0 else 'dve', OT[osl][:], ps[:, b0 * 512:b0 * 512 + 1024],
                      [f'ps{b0}', f'ps{b0 + 1}'], [f'OT{osl}'])
                r0 = t * 512 + sub * 128
                dma(out_d[r0:r0 + 128, :], OT[osl][:], [f'OT{osl}'], [], f'OT{osl}')
    try:
        _layers()
    except _Stop:
        P.emit(dummies, final_chans=())
        return nc
    P.emit(dummies, final_chans=('OT0', 'OT1'))
    return nc


_CACHE = {}


def kernel(x, c, ctx, c_ctx, w_mod, b_mod, w_in, pool_w, pool_scale, na_rpb, q_norm, k_norm,
           w_out, ln1_g, ln1_b, w_up, conv_w, conv_b, w_down, ln2_g, ln2_b):
    x = np.asarray(x, np.float32)
    B, L, _ = x.shape
    DEPTH = int(np.asarray(w_mod).shape[0])
    inp = dict(w_mod=w_mod, b_mod=b_mod, w_in=w_in, pool_w=pool_w, pool_scale=pool_scale, na_rpb=na_rpb,
               q_norm=q_norm, k_norm=k_norm, w_out=w_out, ln1_g=ln1_g, ln1_b=ln1_b, w_up=w_up, conv_w=conv_w,
               conv_b=conv_b, w_down=w_down, ln2_g=ln2_g, ln2_b=ln2_b)
    inp = {k: np.asarray(v, np.float32) for k, v in inp.items()}
    sh = _prep_shared(inp, L, DEPTH)
    key = (L, DEPTH)
    if key not in _CACHE:
        _CACHE[key] = build(L, DEPTH)
    nc = _CACHE[key]
    c = np.asarray(c, np.float32)
    ctx = np.asarray(ctx, np.float32)
    cc = _chunk(np.asarray(c_ctx, np.float32), 8)
    in_maps = []
    for b in range(B):
        cv = np.stack([_chunk(c[b], 8), cc], axis=2).reshape(128, 16)
        m = dict(sh)
        m['x'] = np.ascontiguousarray(x[b])
        m['ctx'] = np.ascontiguousarray(ctx[b])
        m['cvec'] = np.ascontiguousarray(cv)
        in_maps.append(m)
    res = run_bass_kernel_spmd(nc, in_maps, core_ids=list(range(B)))
    return np.stack([np.asarray(r['out'], np.float32) for r in res.results], 0)
```

```python
import numpy as np
from contextlib import ExitStack
import concourse.bass as bass
import concourse.mybir as mybir
from concourse.bass_utils import run_bass_kernel_spmd

F32 = mybir.dt.float32
BF16 = mybir.dt.bfloat16
AF = mybir.ActivationFunctionType
ALU = mybir.AluOpType

D = 1024
CTX = 256
DFF = 2816
NJ = 22
GRID_W = 64
NEG = -30000.0
LN_EPS = 1e-6
NVEC = 214
NWIN = 2560
RING = 12


class Prog:
    CE = ('pe', 'act', 'dve', 'pool')

    def __init__(self, nc):
        self.nc = nc
        self.segs = []
        self.chans = {}
        self.nbar = 0
        self._new_seg()

    def _new_seg(self):
        self.ops = []
        self.lastw = {}
        self.rd = {}
        self.chan_cnt = {}
        self.segs.append((self.ops, self.chan_cnt))

    def add(self, eng, fn, reads=(), writes=(), chan=None):
        i = len(self.ops)
        dma = eng == 'sp'
        deps = {}

        def dep(j, kind):
            o = self.ops[j]
            if not dma and not o['dma'] and o['eng'] == eng and kind != 'raw':
                return
            if dma and o['dma'] and kind == 'waw' and o['chan'] == chan:
                return
            deps[j] = True
        for r in reads:
            if r in self.lastw:
                dep(self.lastw[r], 'raw')
            if r.startswith('ps'):
                for j in self.rd.get(r, ()):
                    if self.ops[j]['eng'] != eng:
                        deps[j] = True
        for r in writes:
            if r in self.lastw:
                dep(self.lastw[r], 'waw')
            for j in self.rd.get(r, ()):
                dep(j, 'war')
        for r in reads:
            self.rd.setdefault(r, []).append(i)
        for r in writes:
            self.lastw[r] = i
            self.rd[r] = []
        op = dict(eng=eng, fn=fn, deps=list(deps), dma=dma, chan=chan, flag=False, ticket=None)
        if dma:
            assert chan is not None
            self.chans[chan] = True
            self.chan_cnt[chan] = self.chan_cnt.get(chan, 0) + 16
            op['ticket'] = self.chan_cnt[chan]
            op['flag'] = True
        for j in deps:
            self.ops[j]['flag'] = True
        self.ops.append(op)
        return i

    def barrier(self):
        self._new_seg()

    def emit(self, dummies, final_chans=()):
        nc = self.nc
        for ops, _ in self.segs:
            cnt = {e: 0 for e in self.CE}
            for o in ops:
                if not o['dma'] and o['flag']:
                    cnt[o['eng']] += 1
                    o['ticket'] = cnt[o['eng']]
                    assert cnt[o['eng']] < 30000
            for c, v in _.items():
                assert v < 30000, (c, v)
        with ExitStack() as st:
            sems = {e: st.enter_context(nc.semaphore('S_' + e)) for e in self.CE}
            bsem = {e: st.enter_context(nc.semaphore('B_' + e)) for e in self.CE}
            barc = st.enter_context(nc.semaphore('C_bar'))
            for c in self.chans:
                sems['c:' + c] = st.enter_context(nc.semaphore('C_' + c))
            block = st.enter_context(nc.Block())
            segs = self.segs
            nseg = len(segs)

            def run(engname):
                def body(e):
                    for si, (ops, chan_cnt) in enumerate(segs):
                        known = {}
                        for o in ops:
                            if o['eng'] != engname:
                                continue
                            need = {}
                            for j in o['deps']:
                                d = ops[j]
                                key = ('c:' + d['chan']) if d['dma'] else d['eng']
                                need[key] = max(need.get(key, 0), d['ticket'])
                            for key, v in need.items():
                                if known.get(key, 0) < v:
                                    e.wait_ge(sems[key], v)
                                    known[key] = v
                            ins = o['fn'](e)
                            if o['flag']:
                                if o['dma']:
                                    ins.then_inc(sems['c:' + o['chan']], 16)
                                else:
                                    ins.then_inc(sems[engname], 1)
                        last = si == nseg - 1
                        if engname == 'sp':
                            for c, v in chan_cnt.items():
                                if last and c not in final_chans:
                                    continue
                                e.wait_ge(sems['c:' + c], v)
                            if last:
                                continue
                            for ce in self.CE:
                                e.wait_ge(bsem[ce], 2 * si + 1)
                            for c in chan_cnt:
                                e.sem_clear(sems['c:' + c])
                            dummies['sp'](e).then_inc(barc, 16)
                            for ce in self.CE:
                                e.wait_ge(bsem[ce], 2 * si + 2)
                            dummies['sp'](e).then_inc(barc, 16)
                        else:
                            if last:
                                continue
                            if engname == 'pe':
                                for ce in ('act', 'dve', 'pool'):
                                    e.wait_ge(bsem[ce], 2 * si + 1)
                            dummies[engname](e).then_inc(bsem[engname], 1)
                            e.wait_ge(barc, 16 * (2 * si + 1))
                            e.sem_clear(sems[engname])
                            dummies[engname](e).then_inc(bsem[engname], 1)
                            e.wait_ge(barc, 16 * (2 * si + 2))
                return body
            block.tensor(run('pe'))
            block.scalar(run('act'))
            block.vector(run('dve'))
            block.gpsimd(run('pool'))
            block.sync(run('sp'))


class Arena:
    def __init__(self, nc, base=20480, limit=229344):
        self.nc = nc
        self.off = base
        self.limit = limit
        self.n = 0

    def alloc(self, name, shape, dtype):
        esz = 4 if dtype == F32 else 2
        nbytes = int(np.prod(shape[1:])) * esz
        nbytes = (nbytes + 63) // 64 * 64
        assert self.off + nbytes <= self.limit, (name, self.off, nbytes)
        self.n += 1
        t = self.nc.alloc_sbuf_tensor_at(f"{name}_{self.n}", list(shape), dtype, offset=self.off)
        self.off += nbytes
        return t

    def mark(self):
        return self.off

    def reset(self, m):
        self.off = m


def _partner_sign():
    partner = np.zeros(64, np.int64)
    sign = np.zeros(64, np.float32)
    for i in range(64):
        blk, r = divmod(i, 32)
        if r < 16:
            partner[i] = blk * 32 + r + 16
            sign[i] = -1.0
        else:
            partner[i] = blk * 32 + r - 16
            sign[i] = 1.0
    return partner, sign


def _rope_tables(L):
    partner, sign = _partner_sign()
    t = np.arange(L, dtype=np.int32)
    inv = (np.float32(10000.0) ** (-np.arange(0, 32, 2, dtype=np.float32) / np.float32(32))).astype(np.float32)
    ang_r = (t // GRID_W).astype(np.float32)[:, None] * inv
    ang_c = (t % GRID_W).astype(np.float32)[:, None] * inv
    cos = np.zeros((64, L), np.float32)
    sin = np.zeros((64, L), np.float32)
    for i in range(64):
        blk, r = divmod(i, 32)
        a = ang_r if blk == 0 else ang_c
        cos[i] = np.cos(a[:, r % 16])
        sin[i] = np.sin(a[:, r % 16]) * sign[i]
    return np.concatenate([cos, cos], 0), np.concatenate([sin, sin], 0)


def _band_tables():
    ab = np.zeros((128, 20, 128), np.float32)
    a = np.arange(128)[:, None]
    b = np.arange(128)[None, :]
    for g, w in enumerate((2, 4, 8, 16)):
        h = w // 2
        ab[:, g * 5 + 0, :] = np.where(a - 128 >= b - h, 1.0 / w, 0.0)
        cur = np.where((a >= b - h) & (a <= b + h - 1), 1.0 / w, 0.0)
        ab[:, g * 5 + 1, :] = cur - (a == b)
        ab[:, g * 5 + 2, :] = np.where(128 + a <= b + h - 1, 1.0 / w, 0.0)
        lo = np.maximum(b - h, 0)
        hi = b + h - 1
        cnt = (hi - lo + 1).astype(np.float32)
        ab[:, g * 5 + 3, :] = np.where((a >= lo) & (a <= hi), 1.0 / cnt, 0.0) - (a == b)
        lo = b - h
        hi = np.minimum(b + h - 1, 127)
        cnt = (hi - lo + 1).astype(np.float32)
        ab[:, g * 5 + 4, :] = np.where((a >= lo) & (a <= hi), 1.0 / cnt, 0.0) - (a == b)
    return ab


def _chunk(v, n):
    return np.ascontiguousarray(np.asarray(v, np.float32).reshape(n, 128).T)


def _prep_shared(inp, L, DEPTH):
    partner, _ = _partner_sign()
    a128 = np.arange(128)
    a64 = np.arange(64)
    cols = []
    for i in range(3):
        cols.append(256 + i * 128 + a128)
    for i in range(3):
        cols.append(640 + i * 128 + a128)
    for i in range(3):
        cols.append(np.concatenate([1408 + i * 64 + a64, 1408 + (i + 3) * 64 + a64]))
    cols.append(1792 + a128)
    for i in range(3):
        cols.append(np.concatenate([1408 + i * 64 + partner, 1408 + (i + 3) * 64 + partner]))
    cols.append(np.concatenate([1792 + partner, 1856 + partner]))
    cols.append(np.arange(0, 256))
    cols.append(1024 + np.arange(384))
    cols.append(1920 + a128)
    cols = np.concatenate(cols)
    assert cols.shape[0] == NWIN
    rows = [np.arange(0, 640)]
    for i in range(3):
        rows.append(640 + i * 64 + a64)
        rows.append(640 + (i + 3) * 64 + a64)
    rows = np.concatenate(rows)
    sh = {}
    sh['w_mod'] = np.ascontiguousarray(inp['w_mod'], np.float32)
    bm = np.stack([_chunk(inp['b_mod'][l], 48) for l in range(DEPTH)], 0)
    sh['bmod'] = np.ascontiguousarray(np.repeat(bm, 2, axis=2))
    sh['w_in'] = np.ascontiguousarray(np.asarray(inp['w_in'], np.float32)[:, :, cols])
    sh['w_out'] = np.ascontiguousarray(np.asarray(inp['w_out'], np.float32)[:, rows, :])
    sh['w_up'] = np.ascontiguousarray(inp['w_up'], np.float32)
    sh['w_down'] = np.ascontiguousarray(inp['w_down'], np.float32)
    sh['pw'] = np.ascontiguousarray(np.asarray(inp['pool_w'], np.float32).reshape(DEPTH, 2, 128, 64))
    vecs = np.zeros((DEPTH, 128, NVEC), np.float32)
    for l in range(DEPTH):
        vecs[l, :, 0:8] = _chunk(inp['ln1_g'][l], 8)
        vecs[l, :, 8:16] = _chunk(inp['ln1_b'][l], 8)
        vecs[l, :, 16:24] = _chunk(inp['ln2_g'][l], 8)
        vecs[l, :, 24:32] = _chunk(inp['ln2_b'][l], 8)
        for j in range(3):
            vecs[l, :, 32 + j * 44:32 + (j + 1) * 44] = _chunk(inp['conv_w'][l, j], 44)
        vecs[l, :, 164:208] = _chunk(inp['conv_b'][l], 44)
        vecs[l, :, 208:210] = _chunk(inp['pool_scale'][l], 2)
        qn = np.asarray(inp['q_norm'][l], np.float32)
        kn = np.asarray(inp['k_norm'][l], np.float32)
        vecs[l, :, 210] = np.concatenate([qn, qn])
        vecs[l, :, 211] = np.concatenate([qn[partner], qn[partner]])
        vecs[l, :, 212] = np.concatenate([kn, kn])
        vecs[l, :, 213] = np.concatenate([kn[partner], kn[partner]])
    sh['vecs'] = vecs
    rpb = np.asarray(inp['na_rpb'], np.float32)
    kc = np.arange(64)[:, None]
    qc = np.arange(64)[None, :]
    cs = np.clip(qc - 8, 0, 48)
    valid = (kc >= cs) & (kc < cs + 16)
    dc = np.clip(kc - qc + 15, 0, 30)
    nab = np.full((DEPTH, 64, 6, 15, 64), NEG, np.float32)
    for l in range(DEPTH):
        for h in range(6):
            for idx in range(15):
                g = rpb[l, h, 14 - idx][dc]
                nab[l, :, h, idx, :] = np.where(valid, g, np.float32(NEG))
    sh['nab'] = nab
    sh['ab'] = _band_tables()
    cos, sinp = _rope_tables(L)
    sh['cos'] = cos
    sh['sinp'] = sinp
    return sh


class _S:
    pass


DEBUG = False
STOP = None


class _Stop(Exception):
    pass


def build(L, DEPTH):
    assert L % 512 == 0
    ROWS = L // GRID_W
    alpha = float((2 * DEPTH) ** 0.25)
    eps_ln = float(LN_EPS / (alpha * alpha))
    nc = bass.Bass("TRN2", target_bir_lowering=False)

    def din(name, shape, dt=F32):
        return nc.dram_tensor(name, list(shape), dt, kind="ExternalInput").ap()

    def dscr(name, shape, dt):
        return nc.dram_tensor(name, list(shape), dt, kind=("ExternalOutput" if DEBUG else "Internal")).ap()

    def stop(tag):
        if STOP == tag:
            raise _Stop()
    x_d = din("x", [L, D])
    ctx_d = din("ctx", [CTX, D])
    cvec_d = din("cvec", [128, 16])
    wmod_d = din("w_mod", [DEPTH, D, 6 * D])
    bmod_d = din("bmod", [DEPTH, 128, 96])
    win_d = din("w_in", [DEPTH, D, NWIN])
    wout_d = din("w_out", [DEPTH, D, D])
    wup_d = din("w_up", [DEPTH, D, 2 * DFF])
    wdn_d = din("w_down", [DEPTH, DFF, D])
    pw_d = din("pw", [DEPTH, 2, 128, 64])
    vecs_d = din("vecs", [DEPTH, 128, NVEC])
    nab_d = din("nab", [DEPTH, 64, 6, 15, 64])
    ab_d = din("ab", [128, 20, 128])
    cos_d = din("cos", [128, L])
    sinp_d = din("sinp", [128, L])
    out_d = nc.dram_tensor("out", [L, D], F32, kind="ExternalOutput").ap()
    dum_d = dscr("dum", [128, 8], F32)

    lat = _S()
    cx = _S()
    for s, n, LL, T in ((lat, "l", L, 512), (cx, "c", CTX, 256)):
        s.L = LL
        s.T = T
        s.ctx = n == "c"
        s.n = n
        s.xT = dscr("xT" + n, [128, 8, LL], F32)
        s.x1T = dscr("x1T" + n, [128, 8, LL], F32)
        s.qnT = dscr("qnT" + n, [128, 3, LL], BF16)
        s.qgT = dscr("qgT" + n, [128, 3, LL], BF16)
        s.pin = dscr("pin" + n, [LL, 256], BF16)
        s.catT = dscr("catT" + n, [128, 8, LL], BF16)
        s.actT = dscr("actT" + n, [128, NJ, LL], BF16)
    lat.knT = dscr("knTl", [128, 3, L], BF16)
    lat.vn = dscr("vnl", [L, 384], BF16)

    P = Prog(nc)
    ar = Arena(nc)
    ps = nc.alloc_psum_tensor("ps", [128, 4096], F32)

    def bank(b, n=512, p0=0, p1=128):
        return ps[p0:p1, b * 512:b * 512 + n]

    identf = ar.alloc("identf", [128, 128], F32)
    identb = ar.alloc("identb", [128, 128], BF16)
    onesb = ar.alloc("onesb", [128, 128], BF16)
    blkb = ar.alloc("blkb", [128, 128], BF16)
    ABb = ar.alloc("ABb", [128, 20, 128], BF16)
    MODTs = [ar.alloc(f"MODT{i}", [128, 6, 8, 2], F32) for i in range(2)]
    VECs = [ar.alloc(f"VEC{i}", [128, NVEC], F32) for i in range(2)]
    cur_l = [0]
    ACT2 = ar.alloc("ACT2", [128, 8, 2], F32)
    PWb = ar.alloc("PWb", [128, 2, 64], BF16)
    BMp = ar.alloc("BMp", [128, 96], F32)
    PWs = ar.alloc("PWs", [128, 2, 64], F32)
    M2c = [ar.alloc(f"M2c{i}", [128, 256], F32) for i in range(2)]
    dumt = {e: ar.alloc("dum" + e, [128, 8], F32) for e in ('act', 'dve', 'pool', 'sp')}
    WST = [ar.alloc(f"WST{i}", [128, 8, 256], F32) for i in range(3)]
    NKT = 2 + L // 128
    base_mark = ar.mark()

    dummies = {
        'pe': lambda e: e.matmul(ps[:, 0:1], lhsT=identb[:, 0:128], rhs=identb[:, 0:1], start=True, stop=True),
        'act': lambda e: e.activation(out=dumt['act'][:, 0:1], in_=dumt['act'][:, 1:2], func=AF.Copy),
        'dve': lambda e: e.memset(dumt['dve'][:, 0:1], 0.0),
        'pool': lambda e: e.memset(dumt['pool'][:, 0:1], 0.0),
        'sp': lambda e: e.dma_start(out=dumt['sp'][:, :], in_=dum_d[:, :]),
    }

    def dma(out, in_, reads, writes, chan):
        P.add('sp', lambda e: e.dma_start(out=out, in_=in_), reads=reads, writes=writes, chan=chan)

    def mm(out, pairs, reads, writes, tps=None):
        def fn(e):
            n = len(pairs)
            ins = None
            for i, (lt, r) in enumerate(pairs):
                kw = {}
                if tps is not None:
                    kw['tile_position'] = tps[i]
                ins = e.matmul(out, lhsT=lt, rhs=r, start=(i == 0), stop=(i == n - 1), **kw)
            return ins
        P.add('pe', fn, reads=reads, writes=writes)

    def mms(lst, reads, writes):
        def fn(e):
            ins = None
            for (o, lt, r, s0, s1, tp) in lst:
                kw = {}
                if tp is not None:
                    kw['tile_position'] = tp
                ins = e.matmul(o, lhsT=lt, rhs=r, start=s0, stop=s1, **kw)
            return ins
        P.add('pe', fn, reads=reads, writes=writes)

    def act(out, in_, func, reads, writes, scale=None, bias=None):
        kw = {}
        if scale is not None:
            kw['scale'] = scale
        if bias is not None:
            kw['bias'] = bias
        P.add('act', lambda e: e.activation(out=out, in_=in_, func=func, **kw), reads=reads, writes=writes)

    def tcopy(eng, out, in_, reads, writes):
        if eng == 'act':
            act(out, in_, AF.Copy, reads, writes)
        else:
            P.add(eng, lambda e: e.tensor_copy(out=out, in_=in_), reads=reads, writes=writes)

    def tt(eng, out, in0, in1, op, reads, writes):
        P.add(eng, lambda e: e.tensor_tensor(out=out, in0=in0, in1=in1, op=op), reads=reads, writes=writes)

    def ts(eng, out, in0, s1, s2, op0, op1, reads, writes):
        if s2 is None:
            P.add(eng, lambda e: e.tensor_scalar(out=out, in0=in0, scalar1=s1, scalar2=None, op0=op0),
                  reads=reads, writes=writes)
        else:
            P.add(eng, lambda e: e.tensor_scalar(out=out, in0=in0, scalar1=s1, scalar2=s2, op0=op0, op1=op1),
                  reads=reads, writes=writes)

    def stt(eng, out, in0, sc, in1, op0, op1, reads, writes):
        P.add(eng, lambda e: e.scalar_tensor_tensor(out=out, in0=in0, scalar=sc, in1=in1, op0=op0, op1=op1),
              reads=reads, writes=writes)

    def recip(out, in_, reads, writes):
        P.add('dve', lambda e: e.reciprocal(out=out, in_=in_), reads=reads, writes=writes)

    def memset(eng, ap, v, writes):
        P.add(eng, lambda e: e.memset(ap, v), writes=writes)

    cast_rr = [0]

    def load_weight(dst_fn, src_fn, nchunks, wname, order=None):
        for ci in (order if order is not None else range(nchunks)):
            sl = cast_rr[0] % 3
            eng = ('dve', 'act')[cast_rr[0] % 2]
            cast_rr[0] += 1
            d = dst_fn(ci)
            s = src_fn(ci)
            shp = d.shape
            if len(shp) == 3:
                stv = WST[sl][:, 0:shp[1], 0:shp[2]]
            else:
                stv = WST[sl][:, 0, 0:shp[1]]
            stv = stv[0:shp[0]]
            dma(stv, s, [], [f'WST{sl}'], f'WST{sl}')
            tcopy(eng, d, stv, [f'WST{sl}'], [f'{wname}{ci}'])

    def MODap(part, c, s):
        return MODTs[cur_l[0] % 2][:, part, c, (1 if s.ctx else 0):(2 if s.ctx else 1)]

    memset('pool', identf[:], 0.0, ['identf'])
    P.add('pool', lambda e: e.affine_select(out=identf[:], in_=identf[:], pattern=[[-1, 128]],
                                            compare_op=ALU.not_equal, fill=1.0, base=0, channel_multiplier=1),
          reads=['identf'], writes=['identf'])
    tcopy('pool', identb[:], identf[:], ['identf'], ['identb'])
    memset('pool', onesb[:], 1.0, ['onesb'])
    memset('pool', blkb[:], 0.0, ['blkb'])
    memset('pool', blkb[0:64, 0:64], 1.0, ['blkb'])
    memset('pool', blkb[64:128, 64:128], 1.0, ['blkb'])
    for e_ in ('act', 'dve', 'pool'):
        memset('pool' if e_ == 'pool' else 'dve', dumt[e_][:], 0.0, ['dum' + e_])
    for q in range(3):
        n0, n1 = q * 7, min(20, q * 7 + 7)
        dma(WST[q][:, 0:n1 - n0, 0:128], ab_d[:, n0:n1, :], [], [f'WST{q}'], f'WST{q}')
        tcopy('dve', ABb[:, n0:n1, :], WST[q][:, 0:n1 - n0, 0:128], [f'WST{q}'], ['ABb'])
    dma(ACT2[:].rearrange("p a b -> p (a b)"), cvec_d[:, :], [], ['ACT2'], 'ACT2')
    act(ACT2[:], ACT2[:], AF.Silu, ['ACT2'], ['ACT2'])

    m0 = ar.mark()
    XIN = [ar.alloc(f"XIN{i}", [128, D], F32) for i in range(2)]
    XTS = [ar.alloc(f"XTS{i}", [128, 8, 512], F32) for i in range(2)]
    for s, src in ((cx, ctx_d), (lat, x_d)):
        for t in range(s.L // s.T):
            N = s.T
            xs = t % 2
            for sub in range(N // 128):
                j = t * (N // 128) + sub
                isl = j % 2
                dma(XIN[isl][:], src[j * 128:(j + 1) * 128, :], [], [f'XIN{isl}'], f'XIN{isl}')
                b0 = 2 * (j % 2)
                P.add('pe', (lambda isl=isl, b0=b0: lambda e: [e.transpose(ps[:, b0 * 512 + c * 128:b0 * 512 + (c + 1) * 128],
                                                                          XIN[isl][:, c * 128:(c + 1) * 128], identf[:])
                                                             for c in range(8)][-1])(),
                      reads=[f'XIN{isl}', 'identf'], writes=[f'ps{b0}', f'ps{b0 + 1}'])
                tcopy('act' if sub % 2 == 0 else 'dve', XTS[xs][:, :, sub * 128:(sub + 1) * 128],
                      ps[:, b0 * 512:b0 * 512 + 1024].rearrange("p (c n) -> p c n", c=8),
                      [f'ps{b0}', f'ps{b0 + 1}'], [f'XTS{xs}'])
            dma(s.xT[:, :, t * N:(t + 1) * N], XTS[xs][:, :, 0:N], [f'XTS{xs}'], [], f'XTS{xs}')
    P.barrier()
    ar.reset(m0)

    def ln_make(s, X, xres, N, gpart, gcol, bcol, mm_fn, tmp, sbk):
        YB, SQ, MEAN, MSQ, SD, RSTD, TT = tmp
        rr = (0, 1, 2)
        b6, b7 = sbk, sbk + 1

        def stats(c):
            mms([(bank(b6, N), onesb[:], YB[c % 2][:, 0:N], c == 0, c == 7, None),
                 (bank(b7, N), onesb[:], SQ[c % 2][:, 0:N], c == 0, c == 7, None)],
                [f'YB{c % 2}', f'SQ{c % 2}', 'onesb'], [f'ps{b6}', f'ps{b7}'])

        def head_chunk(c):
            b = rr[c % 3]
            mm_fn(c, b)
            if c >= 1:
                stats(c - 1)
            stt('dve', X[:, c, 0:N], bank(b, N), MODap(gpart, c, s), X[:, c, 0:N], ALU.mult, ALU.add,
                [f'ps{b}', xres + f'c{c}', f'MOD{cur_l[0] % 2}'], [xres + f'c{c}'])
            tcopy('act', YB[c % 2][:, 0:N], X[:, c, 0:N], [xres + f'c{c}'], [f'YB{c % 2}'])
            act(SQ[c % 2][:, 0:N], X[:, c, 0:N], AF.Square, [xres + f'c{c}'], [f'SQ{c % 2}'])

        def head_post():
            stats(7)

        def tail_pre():
            ts('dve', MEAN[:, 0:N], bank(b6, N), 1.0 / D, None, ALU.mult, None, [f'ps{b6}'], ['MEAN'])
            tt('pool', MSQ[:, 0:N], MEAN[:, 0:N], MEAN[:, 0:N], ALU.mult, ['MEAN'], ['MSQ'])
            stt('dve', SD[:, 0:N], bank(b7, N), 1.0 / D, MSQ[:, 0:N], ALU.mult, ALU.subtract, [f'ps{b7}', 'MSQ'], ['SD'])
            ts('dve', SD[:, 0:N], SD[:, 0:N], eps_ln, None, ALU.add, None, ['SD'], ['SD'])
            act(SD[:, 0:N], SD[:, 0:N], AF.Sqrt, ['SD'], ['SD'])
            recip(RSTD[:, 0:N], SD[:, 0:N], ['SD'], ['RSTD'])

        def tail_chunk(c):
            tt('pool', TT[c % 2][:, 0:N], X[:, c, 0:N], MEAN[:, 0:N], ALU.subtract,
               [xres + f'c{c}', 'MEAN'], [f'TT{c % 2}'])
            tt('dve', TT[c % 2][:, 0:N], TT[c % 2][:, 0:N], RSTD[:, 0:N], ALU.mult,
               [f'TT{c % 2}', 'RSTD'], [f'TT{c % 2}'])
            VEC_ = VECs[cur_l[0] % 2]
            act(X[:, c, 0:N], TT[c % 2][:, 0:N], AF.Identity, [f'TT{c % 2}', f'VEC{cur_l[0] % 2}'], [xres + f'c{c}'],
                scale=VEC_[:, gcol + c:gcol + c + 1], bias=VEC_[:, bcol + c:bcol + c + 1])
        return head_chunk, head_post, tail_pre, tail_chunk

    def ln_pipeline(tiles, make, load, store):
        n = len(tiles)
        load(0)
        prev = None
        for idx in range(n + 1):
            cur = make(idx) if idx < n else None
            if idx + 1 < n:
                load(idx + 1)
            if prev is not None:
                prev[2]()
            for c in range(8):
                if cur is not None:
                    cur[0](c)
                if prev is not None:
                    prev[3](c)
            if cur is not None:
                cur[1]()
            if prev is not None:
                store(idx - 1)
            prev = cur

    def alloc_ln_tmp():
        YB = [ar.alloc(f"YB{i}", [128, 512], BF16) for i in range(2)]
        SQ = [ar.alloc(f"SQ{i}", [128, 512], BF16) for i in range(2)]
        MEAN = ar.alloc("MEAN", [128, 512], F32)
        MSQ = ar.alloc("MSQ", [128, 512], F32)
        SD = ar.alloc("SD", [128, 512], F32)
        RSTD = ar.alloc("RSTD", [128, 512], F32)
        TT = [ar.alloc(f"TT{i}", [128, 512], F32) for i in range(2)]
        return (YB, SQ, MEAN, MSQ, SD, RSTD, TT)

    def _layers():
        stop('pro')
        for l in range(DEPTH):
            last_layer = l == DEPTH - 1
            streams = [cx, lat]
            mL = ar.mark()
            KTg = ar.alloc("KTg", [128, CTX + L], BF16)
            VG = ar.alloc("VG", [128, NKT, 2, 128], BF16)
            KNC = ar.alloc("KNC", [128, 3, CTX], BF16)
            VNC = ar.alloc("VNC", [128, 6, 2, 128], BF16)
            memset('pool', VG[:], 1.0, ['VG'])
            memset('pool', VNC[:], 1.0, ['VNC'])
            cur_l[0] = l
            VEC = VECs[l % 2]
            MODT = MODTs[l % 2]
            MODr = f'MOD{l % 2}'
            VECr = f'VEC{l % 2}'

            def m_steps(lm):
                MT = MODTs[lm % 2]
                mr = f'MOD{lm % 2}'
                steps = []

                def first():
                    dma(VECs[lm % 2][:], vecs_d[lm], [], [f'VEC{lm % 2}'], f'VEC{lm % 2}')
                    dma(BMp[:], bmod_d[lm], [], ['BMp'], 'BMp')
                steps.append(first)
                wmv = wmod_d[lm].rearrange("(k p) n -> p k n", p=128)

                def tr(q):
                    def fn(e):
                        ins = None
                        for jj in range(2):
                            j = q * 2 + jj
                            ins = e.transpose(ps[:, 3584 + 256 + j * 2:3584 + 256 + j * 2 + 2],
                                              M2c[q % 2][0:2, jj * 128:(jj + 1) * 128], identf[0:2, 0:2])
                        return ins
                    P.add('pe', fn, reads=[f'M2c{q % 2}', 'identf'], writes=['ps7t'])
                def cdma(q):
                    sl = q % 3
                    dma(WST[sl][:], wmv[:, :, q * 256:(q + 1) * 256], [], [f'WST{sl}'], f'WST{sl}')

                def first2():
                    cdma(0)
                    cdma(1)
                steps.append(first2)
                for q in range(24):
                    def chunk(q=q):
                        sl = q % 3
                        if q + 2 < 24:
                            cdma(q + 2)
                        mms([(ps[0:2, 3584:3584 + 256], ACT2[:, k, :], WST[sl][:, k, :], k == 0, k == 7, None)
                             for k in range(8)], [f'WST{sl}', 'ACT2'], ['ps7'])
                        if q >= 1:
                            tr(q - 1)
                        tcopy('dve', M2c[q % 2][0:2, :], ps[0:2, 3584:3584 + 256], ['ps7', 'ps7t'], [f'M2c{q % 2}'])
                    steps.append(chunk)

                def fin():
                    tr(23)
                    tt('dve', MT[:].rearrange("p a b c -> p (a b c)"), ps[:, 3584 + 256:3584 + 256 + 96], BMp[:], ALU.add,
                       ['ps7t', 'ps7', 'BMp'], [mr])
                    for part in (1, 4):
                        ts('dve', MT[:, part], MT[:, part], 1.0, None, ALU.add, None, [mr], [mr])
                    for part in (2, 5):
                        ts('dve', MT[:, part], MT[:, part], 1.0 / alpha, None, ALU.mult, None, [mr], [mr])
                    dma(PWs[:], pw_d[lm].rearrange("i p d -> p i d"), [], ['PWs'], 'PWs')
                    tcopy('dve', PWb[:], PWs[:], ['PWs'], ['PWb'])
                steps.append(fin)
                return steps
            if l == 0:
                for f_ in m_steps(0):
                    f_()
            if DEBUG and l == 0:
                modt_d = nc.dram_tensor("modt", [128, 96], F32, kind="ExternalOutput").ap()
                dma(modt_d[:, :], MODT[:].rearrange("p a b c -> p (a b c)"), [MODr], [], 'XT')
                act2_d = nc.dram_tensor("act2", [128, 16], F32, kind="ExternalOutput").ap()
                dma(act2_d[:, :], ACT2[:].rearrange("p a b -> p (a b)"), ['ACT2'], [], 'XT')
            if DEBUG:
                P.barrier()
                stop(f'M{l}')

            m0 = ar.mark()
            WIN = ar.alloc("WIN", [128, 8, NWIN], BF16)
            XT = ar.alloc("XT", [128, 8, 512], F32)
            H = [ar.alloc(f"H{i}", [128, 8, 512], BF16) for i in range(2)]
            QNs = ar.alloc("QNs", [128, 3, 512], BF16)
            KNs = ar.alloc("KNs", [128, 3, 512], BF16)
            QGs = ar.alloc("QGs", [128, 3, 512], BF16)
            PINs = ar.alloc("PINs", [128, 4, 256], BF16)
            VNs = ar.alloc("VNs", [128, 4, 384], BF16)
            CS = [ar.alloc(f"CS{i}", [128, 2, 512], F32) for i in range(2)]
            SQb = [ar.alloc(f"SQb{i}", [128, 512], BF16) for i in range(2)]
            Vt = ar.alloc("Vt", [128, 512], F32)
            RS = ar.alloc("RS", [128, 512], F32)
            TA = [ar.alloc(f"TA{i}", [128, 512], F32) for i in range(2)]
            TB = [ar.alloc(f"TB{i}", [128, 512], F32) for i in range(2)]
            winv = win_d[l].rearrange("(k p) n -> p k n", p=128)
            load_weight(lambda ci: WIN[:, :, ci * 256:(ci + 1) * 256], lambda ci: winv[:, :, ci * 256:(ci + 1) * 256],
                        NWIN // 256, 'WIN')

            def wres(c0, c1):
                return [f'WIN{i}' for i in range(c0 // 256, (c1 - 1) // 256 + 1)]
            if STOP == f'Aw{l}':
                P.barrier()
                stop(f'Aw{l}')
            fmrr = [0]
            tcnt = 0
            a_tiles = [(s, t) for s in streams for t in range(s.L // s.T)]

            def a_load(idx):
                s, t = a_tiles[idx]
                N = s.T
                t0 = t * N
                hs = idx % 2
                dma(XT[:, :, 0:N], s.xT[:, :, t0:t0 + N], [], ['XT'], 'XT')
                if not s.ctx:
                    dma(CS[hs][:, 0, :], cos_d[:, t0:t0 + N], [], [f'CS{hs}'], f'CS{hs}')
                    dma(CS[hs][:, 1, :], sinp_d[:, t0:t0 + N], [], [f'CS{hs}'], f'CS{hs}')
            def a_h(idx):
                s, t = a_tiles[idx]
                N = s.T
                hs = idx % 2
                for c in range(8):
                    act(H[hs][:, c, 0:N], XT[:, c, 0:N], AF.Identity, ['XT', MODr], [f'H{hs}'],
                        scale=MODap(1, c, s), bias=MODap(0, c, s))
            a_load(0)
            a_h(0)
            if len(a_tiles) > 1:
                a_load(1)
            rcnt = [0]
            for a_idx, (s, t) in enumerate(a_tiles):
                N = s.T
                if True:
                    t0 = t * N
                    hs = tcnt % 2
                    tcnt += 1

                    def fm(m, hs=hs, N=N):
                        b = fmrr[0] % 3
                        fmrr[0] += 1
                        mm(bank(b, N), [(WIN[:, k, m * 128:(m + 1) * 128], H[hs][:, k, 0:N]) for k in range(8)],
                           [f'H{hs}'] + wres(m * 128, (m + 1) * 128), [f'ps{b}'])
                        return b
                    need_q = not (s.ctx and last_layer)
                    if STOP == f'Ah{l}':
                        P.barrier()
                        stop(f'Ah{l}')
                    if need_q:
                        for m in range(3):
                            b = fm(m)
                            ts('dve', QNs[:, m, 0:N], bank(b, N), 0.125, None, ALU.mult, None, [f'ps{b}'], ['QNs'])
                        dma(s.qnT[:, :, t0:t0 + N], QNs[:, :, 0:N], ['QNs'], [], 'QNs')
                    for m in range(3, 6):
                        b = fm(m)
                        if s.ctx:
                            tcopy('act', KNC[:, m - 3, 0:N], bank(b, N), [f'ps{b}'], ['KNC'])
                        else:
                            tcopy('act', KNs[:, m - 3, 0:N], bank(b, N), [f'ps{b}'], ['KNs'])
                    if not s.ctx:
                        dma(s.knT[:, :, t0:t0 + N], KNs[:, :, 0:N], ['KNs'], [], 'KNs')
                    if STOP == f'Ak{l}':
                        P.barrier()
                        stop(f'Ak{l}')
                    rope_ms = ([6, 7, 8] if need_q else []) + [9]
                    deferred = None
                    for m in rope_ms:
                        bz = fm(m)
                        bp = None if s.ctx else fm(m + 4)
                        gcol = 210 if m < 9 else 212
                        rq = rcnt[0] % 2
                        rcnt[0] += 1
                        act(SQb[rq][:, 0:N], bank(bz, N), AF.Square, [f'ps{bz}'], [f'SQb{rq}'])
                        if m < 9:
                            dst, dres = QGs[:, m - 6, 0:N], 'QGs'
                        else:
                            koff = 0 if s.ctx else CTX + t0
                            dst, dres = KTg[:, koff:koff + N], 'KTg'
                        if not s.ctx:
                            stt('dve', TA[rq][:, 0:N], bank(bz, N), VEC[:, gcol:gcol + 1], CS[hs][:, 0, 0:N], ALU.mult, ALU.mult,
                                [f'ps{bz}', f'CS{hs}', VECr], [f'TA{rq}'])
                            stt('dve', TB[rq][:, 0:N], bank(bp, N), VEC[:, gcol + 1:gcol + 2], CS[hs][:, 1, 0:N], ALU.mult,
                                ALU.mult, [f'ps{bp}', f'CS{hs}', VECr], [f'TB{rq}'])
                        if deferred is not None:
                            deferred()

                        def deferred(rq=rq, bz=bz, gcol=gcol, dst=dst, dres=dres, s=s, N=N):
                            mm(bank(3, N), [(blkb[:], SQb[rq][:, 0:N])], [f'SQb{rq}', 'blkb'], ['ps3'])
                            ts('dve', Vt[:, 0:N], bank(3, N), 1.0 / 64, LN_EPS, ALU.mult, ALU.add, ['ps3'], ['Vt'])
                            act(Vt[:, 0:N], Vt[:, 0:N], AF.Sqrt, ['Vt'], ['Vt'])
                            recip(RS[:, 0:N], Vt[:, 0:N], ['Vt'], ['RS'])
                            if s.ctx:
                                stt('dve', dst, bank(bz, N), VEC[:, gcol:gcol + 1], RS[:, 0:N], ALU.mult, ALU.mult,
                                    [f'ps{bz}', 'RS', VECr], [dres])
                            else:
                                tt('pool', TA[rq][:, 0:N], TA[rq][:, 0:N], TB[rq][:, 0:N], ALU.add, [f'TA{rq}', f'TB{rq}'],
                                   [f'TA{rq}'])
                                tt('pool', dst, TA[rq][:, 0:N], RS[:, 0:N], ALU.mult, [f'TA{rq}', 'RS'], [dres])
                    if deferred is not None:
                        deferred()
                    if need_q:
                        dma(s.qgT[:, :, t0:t0 + N], QGs[:, :, 0:N], ['QGs'], [], 'QGs')
                    if STOP == f'Ar{l}':
                        P.barrier()
                        stop(f'Ar{l}')
                    if a_idx + 1 < len(a_tiles):
                        a_h(a_idx + 1)
                    if a_idx + 2 < len(a_tiles):
                        a_load(a_idx + 2)
                    for sub in range(N // 128):
                        b0 = 4 + 2 * (sub % 2)
                        hsl = H[hs]
                        mms([(bank(b0), hsl[:, k, sub * 128:(sub + 1) * 128], WIN[:, k, 1792:2304], k == 0, k == 7, None)
                             for k in range(8)] +
                            [(bank(b0 + 1, 256), hsl[:, k, sub * 128:(sub + 1) * 128], WIN[:, k, 2304:2560], k == 0, k == 7, None)
                             for k in range(8)],
                            [f'H{hs}'] + wres(1792, 2560), [f'ps{b0}', f'ps{b0 + 1}'])
                        rd = [f'ps{b0}', f'ps{b0 + 1}']
                        import os
                        SK = os.environ.get('SK', '')
                        if 'a' in SK:
                            continue
                        if need_q and 'p' not in SK:
                            tcopy('act', PINs[:, sub, :], bank(b0, 256), [f'ps{b0}'], ['PINs'])
                        kt = sub if s.ctx else 2 + t * 4 + sub
                        if 'v' in SK:
                            continue
                        if s.ctx and 'n' not in SK:
                            for h in range(6):
                                src = ps[:, b0 * 512 + 256 + h * 64:b0 * 512 + 320 + h * 64] if h < 4 else \
                                    ps[:, (b0 + 1) * 512 + (h - 4) * 64:(b0 + 1) * 512 + (h - 3) * 64]
                                par = h % 2
                                tcopy('act', VNC[:, sub * 3 + h // 2, par, par * 64:(par + 1) * 64], src, rd, ['VNC'])
                        elif not s.ctx:
                            tcopy('act', VNs[:, sub, 0:256], ps[:, b0 * 512 + 256:b0 * 512 + 512], [f'ps{b0}'], ['VNs'])
                            tcopy('dve', VNs[:, sub, 256:384], bank(b0 + 1, 128), [f'ps{b0 + 1}'], ['VNs'])
                        if 'g' in SK:
                            continue
                        tcopy('dve', VG[:, kt, 0, 0:64], ps[:, (b0 + 1) * 512 + 128:(b0 + 1) * 512 + 192], [f'ps{b0 + 1}'], ['VG'])
                        tcopy('dve', VG[:, kt, 1, 64:128], ps[:, (b0 + 1) * 512 + 192:(b0 + 1) * 512 + 256], [f'ps{b0 + 1}'], ['VG'])
                    ns = N // 128
                    if STOP == f'At{l}':
                        P.barrier()
                        stop(f'At{l}')
                    if need_q:
                        dma(s.pin[t0:t0 + N, :].rearrange("(j p) c -> p j c", p=128), PINs[:, 0:ns, :], ['PINs'], [], 'PINs')
                    if not s.ctx:
                        dma(s.vn[t0:t0 + N, :].rearrange("(j p) c -> p j c", p=128), VNs[:, 0:ns, :], ['VNs'], [], 'VNs')
                    if STOP == f'Ac{l}':
                        P.barrier()
                        stop(f'Ac{l}')
            P.barrier()
            ar.reset(m0)
            stop(f'A{l}')
            if last_layer:
                streams = [lat]

            m0 = ar.mark()
            PINL = [ar.alloc(f"PINL{i}", [128, 6, 256], BF16) for i in range(2)]
            PLT = ar.alloc("PLT", [128, 512], BF16)
            CATs = [ar.alloc(f"CATs{i}", [128, 3, 512], BF16) for i in range(2)]
            tcnt = 0
            d_tiles = [(s, t) for s in streams for t in range(s.L // s.T)]

            def d_load(idx):
                s, t = d_tiles[idx]
                NSUB = s.T // 128
                NS = s.L // 128
                j0 = t * NSUB
                sl = idx % 2
                jlo = max(j0 - 1, 0)
                jhi = min(j0 + NSUB + 1, NS)
                dma(PINL[sl][:, jlo - (j0 - 1):jhi - (j0 - 1), :],
                    s.pin[jlo * 128:jhi * 128, :].rearrange("(j p) c -> p j c", p=128), [], [f'PINL{sl}'], f'PINL{sl}')
            d_load(0)
            for d_idx, (s, t) in enumerate(d_tiles):
                N = s.T
                NSUB = N // 128
                NS = s.L // 128
                if True:
                    t0 = t * N
                    j0 = t * NSUB
                    sl = tcnt % 2
                    tcnt += 1
                    if d_idx + 1 < len(d_tiles):
                        d_load(d_idx + 1)
                    for i in range(2):
                        lst = []
                        for sub in range(NSUB):
                            j = j0 + sub
                            for gg in range(2):
                                g = 2 * i + gg
                                terms = []
                                if j > 0:
                                    terms.append((sub, g * 5 + 0))
                                terms.append((sub + 1, g * 5 + (3 if j == 0 else 4 if j == NS - 1 else 1)))
                                if j < NS - 1:
                                    terms.append((sub + 2, g * 5 + 2))
                                for ti, (pi, ai) in enumerate(terms):
                                    lst.append((ps[gg * 64:(gg + 1) * 64, i * 512 + sub * 128:i * 512 + (sub + 1) * 128],
                                                PINL[sl][:, pi, g * 64:(g + 1) * 64], ABb[:, ai, :],
                                                ti == 0, ti == len(terms) - 1, (0, gg * 64)))
                        mms(lst, [f'PINL{sl}', 'ABb'], [f'ps{i}'])
                        tcopy('dve', PLT[:, 0:N], bank(i, N), [f'ps{i}'], ['PLT'])
                        mms([(ps[gg * 64:(gg + 1) * 64, (2 + i + 2 * gg) * 512:(2 + i + 2 * gg) * 512 + N],
                              PWb[gg * 64:(gg + 1) * 64, i, :],
                              PLT[gg * 64:(gg + 1) * 64, 0:N], True, True, (gg * 64, gg * 64)) for gg in range(2)],
                            ['PLT', 'PWb'], [f'ps{2 + i}', f'ps{4 + i}'])
                        for gg in range(2):
                            bb = 2 + i + 2 * gg
                            ts('dve', CATs[sl][gg * 64:(gg + 1) * 64, i, 0:N], bank(bb, N, gg * 64, (gg + 1) * 64),
                               VEC[gg * 64:(gg + 1) * 64, 208 + i:209 + i], None, ALU.mult, None,
                               [f'ps{bb}', VECr], [f'CATs{sl}'])
                    dma(s.catT[:, 0:2, t0:t0 + N], CATs[sl][:, 0:2, 0:N], [f'CATs{sl}'], [], f'CATs{sl}')
            P.barrier()
            ar.reset(m0)

            def attn_phase(kind):
                m0 = ar.mark()
                Q = [ar.alloc(f"Q{i}", [128, 3, 512], BF16) for i in range(2)]
                PT = [ar.alloc(f"PT{i}", [128, 1024], BF16) for i in range(3)]
                Rr = ar.alloc("Rr", [128, 512], F32)
                CATs = [ar.alloc(f"CATa{i}", [128, 3, 512], BF16) for i in range(2)]
                if kind == 'na':
                    KNR = ar.alloc("KNR", [128, 3, RING * 128], BF16)
                    VNR = ar.alloc("VNR", [128, RING * 3, 2, 128], BF16)
                    BIAS = ar.alloc("BIAS", [128, 6, 64, 64], BF16)
                    BT = ar.alloc("BT", [64, 6, 15, 64], BF16)
                    memset('pool', VNR[:], 1.0, [f'VNR{r}' for r in range(RING)])
                    for h in range(6):
                        q = h % 3
                        stv = WST[q][0:64].rearrange("p a b -> p (a b)")[:, 0:960]
                        dma(stv, nab_d[l][:, h].rearrange("p a b -> p (a b)"), [], [f'WST{q}'], f'WST{q}')
                        tcopy('dve', BT[:, h].rearrange("p a b -> p (a b)"), stv, [f'WST{q}'], ['BT'])
                cur_sig = [None]
                loaded = [0]
                ocnt = 0
                scnt = 0
                pcnt = 0
                coff = 2 if kind == 'na' else 5
                tiles = [(s, t) for s in streams for t in range(s.L // s.T)]
                items = []

                def load_q(idx):
                    s, t = tiles[idx]
                    N = s.T
                    qs = idx % 2
                    dma(Q[qs][:, :, 0:N], (s.qnT if kind == 'na' else s.qgT)[:, :, t * N:(t + 1) * N], [], [f'Q{qs}'], f'Q{qs}')

                for idx, (s, t) in enumerate(tiles):
                    N = s.T
                    t0 = t * N
                    qs = idx % 2
                    pre = []
                    if idx == 0:
                        pre.append(lambda: load_q(0))
                    if idx + 1 < len(tiles):
                        pre.append((lambda idx=idx: lambda: load_q(idx + 1))())
                    lat_slots = []
                    if kind == 'na' and not s.ctx:
                        R0 = 8 * t

                        def rs(r):
                            return min(max(r - 4, 0), ROWS - 8)
                        kt_lo = rs(R0) // 2
                        kt_hi = (rs(R0 + 7) + 8 + 1) // 2
                        sig = []
                        for sl_, kt in enumerate(range(kt_lo, kt_hi)):
                            for a in range(2):
                                kr = 2 * kt + a
                                bs = [b for b in range(8) if rs(R0 + b) <= kr < rs(R0 + b) + 8]
                                if bs:
                                    assert bs == list(range(bs[0], bs[-1] + 1))
                                    i0_ = 7 - kr + R0 + bs[0]
                                    assert 0 <= i0_ and i0_ + len(bs) <= 15
                                    sig.append((sl_, a, bs[0], len(bs), i0_))
                        sig = tuple(sig)
                        if sig != cur_sig[0]:
                            cur_sig[0] = sig

                            def asm(sig=sig):
                                memset('pool', BIAS[:], NEG, ['BIAS'])
                                for (sl_, a, b0_, nb, i0_) in sig:
                                    dma(BIAS[a * 64:(a + 1) * 64, :, sl_ * 8 + b0_:sl_ * 8 + b0_ + nb, :],
                                        BT[0:64, :, i0_:i0_ + nb, :], ['BT'], ['BIAS'], 'BIAS')
                            pre.append(asm)

                        def ldk(k0=loaded[0], k1=kt_hi):
                            for kt in range(k0, k1):
                                rsl = kt % RING
                                dma(KNR[:, :, rsl * 128:(rsl + 1) * 128], lat.knT[:, :, kt * 128:(kt + 1) * 128],
                                    [], [f'KNR{rsl}'], f'KNR{rsl}')
                                vsrc = lat.vn[kt * 128:(kt + 1) * 128, :].rearrange("p (i a d) -> p i a d", i=3, a=2)
                                dma(VNR[:, rsl * 3:rsl * 3 + 3, 0, 0:64], vsrc[:, :, 0, :], [], [f'VNR{rsl}'], f'VNR{rsl}')
                                dma(VNR[:, rsl * 3:rsl * 3 + 3, 1, 64:128], vsrc[:, :, 1, :], [], [f'VNR{rsl}'], f'VNR{rsl}')
                        pre.append(ldk)
                        loaded[0] = max(loaded[0], kt_hi)
                        lat_slots = list(enumerate(range(kt_lo, kt_hi)))
                    if kind == 'na':
                        keys = [('l', sl_, kt) for sl_, kt in lat_slots] + [('c', 0, 0), ('c', 0, 1)]
                    else:
                        nk = 2 if s.ctx else NKT
                        keys = [('g', 0, kt) for kt in range(nk)]
                    for i in range(3):
                        ob = 4 + 2 * (ocnt % 2)
                        ocnt += 1
                        for ki, (kk, sl_, kt) in enumerate(keys):
                            sb = 2 * (scnt % 2)
                            scnt += 1
                            pp = pcnt % 3
                            pcnt += 1
                            if kk == 'l':
                                rsl = kt % RING
                                kA = KNR[0:64, i, rsl * 128:(rsl + 1) * 128]
                                kB = KNR[64:128, i, rsl * 128:(rsl + 1) * 128]
                                vA = VNR[:, rsl * 3 + i, 0, :]
                                vB = VNR[:, rsl * 3 + i, 1, :]
                                kres = [f'KNR{rsl}', f'VNR{rsl}']
                            elif kk == 'c':
                                kA = KNC[0:64, i, kt * 128:(kt + 1) * 128]
                                kB = KNC[64:128, i, kt * 128:(kt + 1) * 128]
                                vA = VNC[:, kt * 3 + i, 0, :]
                                vB = VNC[:, kt * 3 + i, 1, :]
                                kres = ['KNC', 'VNC']
                            else:
                                kA = KTg[0:64, kt * 128:(kt + 1) * 128]
                                kB = KTg[64:128, kt * 128:(kt + 1) * 128]
                                vA = VG[:, kt, 0, :]
                                vB = VG[:, kt, 1, :]
                                kres = ['KTg', 'VG']
                            hasb = kk == 'l'
                            lst = [(bank(sb, N), kA, Q[qs][0:64, i, 0:N], True, not hasb, None),
                                   (bank(sb + 1, N), kB, Q[qs][64:128, i, 0:N], True, not hasb, None)]
                            rdl = [f'Q{qs}'] + kres
                            if hasb:
                                for par in range(2):
                                    lst.append((bank(sb + par, N), identb[:],
                                                BIAS[:, 2 * i + par, sl_ * 8:(sl_ + 1) * 8, :].rearrange("p a b -> p (a b)"),
                                                False, True, None))
                                rdl += ['BIAS', 'identb']
                            first_of_tile = (i == 0 and ki == 0)

                            def qk(lst=lst, rdl=rdl, sb=sb, pre=pre, first_of_tile=first_of_tile):
                                if first_of_tile:
                                    for f_ in pre:
                                        f_()
                                mms(lst, rdl, [f'ps{sb}', f'ps{sb + 1}'])

                            def rest(sb=sb, pp=pp, N=N, ob=ob, vA=vA, vB=vB, kres=kres, ki=ki, nk_=len(keys), qs=qs, i=i,
                                     s=s, t0=t0):
                                sc_ = 1.0 if kind == 'na' else 0.125
                                if N == 512:
                                    act(PT[pp][:], ps[:, sb * 512:sb * 512 + 1024], AF.Exp, [f'ps{sb}', f'ps{sb + 1}'],
                                        [f'PT{pp}'], scale=sc_)
                                else:
                                    act(PT[pp][:].rearrange("p (a n) -> p a n", a=2)[:, :, 0:N],
                                        ps[:, sb * 512:sb * 512 + 1024].rearrange("p (a n) -> p a n", a=2)[:, :, 0:N],
                                        AF.Exp, [f'ps{sb}', f'ps{sb + 1}'], [f'PT{pp}'], scale=sc_)
                                mms([(bank(ob, N), vA, PT[pp][:, 0:N], ki == 0, ki == nk_ - 1, None),
                                     (bank(ob + 1, N), vB, PT[pp][:, 512:512 + N], ki == 0, ki == nk_ - 1, None)],
                                    [f'PT{pp}'] + kres, [f'ps{ob}', f'ps{ob + 1}'])
                                if ki == nk_ - 1:
                                    recip(Rr[0:64, 0:N], bank(ob, N, 64, 128), [f'ps{ob}'], ['Rr'])
                                    recip(Rr[64:128, 0:N], bank(ob + 1, N, 0, 64), [f'ps{ob + 1}'], ['Rr'])
                                    tt('dve', CATs[qs][0:64, i, 0:N], bank(ob, N, 0, 64), Rr[0:64, 0:N], ALU.mult,
                                       [f'ps{ob}', 'Rr'], [f'CATa{qs}'])
                                    tt('dve', CATs[qs][64:128, i, 0:N], bank(ob + 1, N, 64, 128), Rr[64:128, 0:N], ALU.mult,
                                       [f'ps{ob + 1}', 'Rr'], [f'CATa{qs}'])
                                    if i == 2:
                                        dma(s.catT[:, coff:coff + 3, t0:t0 + N], CATs[qs][:, :, 0:N], [f'CATa{qs}'], [],
                                            f'CATa{qs}')
                            items.append((qk, rest))
                for n_ in range(len(items) + 1):
                    if n_ < len(items):
                        items[n_][0]()
                    if n_ >= 1:
                        items[n_ - 1][1]()
                P.barrier()
                ar.reset(m0)
            stop(f'D{l}')
            attn_phase('na')
            stop(f'C{l}')
            attn_phase('gqa')
            stop(f'B{l}')
            ar.reset(mL)

            m0 = ar.mark()
            WOUT = ar.alloc("WOUT", [128, 8, D], BF16)
            CT = [ar.alloc(f"CT{i}", [128, 8, 512], BF16) for i in range(3)]
            XR = [ar.alloc(f"XR{i}", [128, 8, 512], F32) for i in range(3)]
            tmp = alloc_ln_tmp()
            woutv = wout_d[l].rearrange("(k p) n -> p k n", p=128)
            load_weight(lambda ci: WOUT[:, :, ci * 256:(ci + 1) * 256], lambda ci: woutv[:, :, ci * 256:(ci + 1) * 256],
                        4, 'WOUT')
            e_tiles = [(s, t) for s in streams for t in range(s.L // s.T)]

            def e1_load(idx):
                s, t = e_tiles[idx]
                N = s.T
                t0 = t * N
                sl = idx % 3
                dma(CT[sl][:, :, 0:N], s.catT[:, :, t0:t0 + N], [], [f'CT{sl}'], f'CT{sl}')
                dma(XR[sl][:, :, 0:N], s.xT[:, :, t0:t0 + N], [], [f'XR{sl}c{c}' for c in range(8)], f'XR{sl}')

            def e1_make(idx):
                s, t = e_tiles[idx]
                N = s.T
                sl = idx % 3

                def mmf(c, b):
                    mm(bank(b, N), [(WOUT[:, k, c * 128:(c + 1) * 128], CT[sl][:, k, 0:N]) for k in range(8)],
                       [f'CT{sl}'] + [f'WOUT{c // 2}'], [f'ps{b}'])
                return ln_make(s, XR[sl], f'XR{sl}', N, 2, 0, 8, mmf, tmp, 4 + 2 * (idx % 2))

            def e1_store(idx):
                s, t = e_tiles[idx]
                N = s.T
                sl = idx % 3
                dma(s.x1T[:, :, t * N:(t + 1) * N], XR[sl][:, :, 0:N], [f'XR{sl}c{c}' for c in range(8)], [], f'XRst{sl}')
            ln_pipeline(e_tiles, e1_make, e1_load, e1_store)
            P.barrier()
            ar.reset(m0)

            stop(f'E1{l}')
            m0 = ar.mark()
            WUP = ar.alloc("WUP", [128, 8, 2 * DFF], BF16)
            X1B = [ar.alloc(f"X1B{i}", [128, 8, 512], F32) for i in range(2)]
            H2 = [ar.alloc(f"H2{i}", [128, 8, 512], BF16) for i in range(2)]
            TAc = [ar.alloc(f"TAc{i}", [128, 512], F32) for i in range(2)]
            TGc = [ar.alloc(f"TGc{i}", [128, 512], F32) for i in range(2)]
            SGc = [ar.alloc(f"SGc{i}", [128, 512], F32) for i in range(2)]
            AOs = [ar.alloc(f"AOs{i}", [128, 512], BF16) for i in range(3)]
            wupv = wup_d[l].rearrange("(k p) n -> p k n", p=128)
            load_weight(lambda ci: WUP[:, :, ci * 256:(ci + 1) * 256], lambda ci: wupv[:, :, ci * 256:(ci + 1) * 256],
                        22, 'WUP', order=[x for p_ in range(11) for x in (p_, 11 + p_)])
            jcnt = 0
            f_tiles = []
            for s in streams:
                nf = (s.L + 509) // 510
                for ti in range(nf):
                    f_tiles.append((s, ti))

            def f_geom(idx):
                s, ti = f_tiles[idx]
                out_lo = 510 * ti
                out_hi = min(s.L, out_lo + 510)
                n_out = out_hi - out_lo
                N = n_out + 2
                tok_lo = max(out_lo - 1, 0)
                tok_hi = min(out_hi + 1, s.L)
                col_lo = tok_lo - (out_lo - 1)
                ncol = tok_hi - tok_lo
                return s, out_lo, out_hi, n_out, N, tok_lo, tok_hi, col_lo, ncol

            def f_load(idx):
                s, out_lo, out_hi, n_out, N, tok_lo, tok_hi, col_lo, ncol = f_geom(idx)
                xs = idx % 2
                dma(X1B[xs][:, :, col_lo:col_lo + ncol], s.x1T[:, :, tok_lo:tok_hi], [], [f'X1B{xs}'], f'X1B{xs}')

            def f_h2(idx):
                s, out_lo, out_hi, n_out, N, tok_lo, tok_hi, col_lo, ncol = f_geom(idx)
                xs = idx % 2
                for c in range(8):
                    act(H2[xs][:, c, col_lo:col_lo + ncol], X1B[xs][:, c, col_lo:col_lo + ncol], AF.Identity,
                        [f'X1B{xs}', MODr], [f'H2{xs}'], scale=MODap(4, c, s), bias=MODap(3, c, s))
                if col_lo == 1:
                    memset('pool', H2[xs][:, :, 0:1], 0.0, [f'H2{xs}'])
                if col_lo + ncol < N:
                    memset('pool', H2[xs][:, :, N - 1:N], 0.0, [f'H2{xs}'])
            f_load(0)
            f_h2(0)
            pend_m = m_steps(l + 1) if l + 1 < DEPTH else []
            for f_idx in range(len(f_tiles)):
                s, out_lo, out_hi, n_out, N, tok_lo, tok_hi, col_lo, ncol = f_geom(f_idx)
                xs = f_idx % 2
                if f_idx + 1 < len(f_tiles):
                    f_load(f_idx + 1)
                for j in range(NJ):
                    if j == 10 and f_idx + 1 < len(f_tiles):
                        f_h2(f_idx + 1)
                    if j in (3, 14) and pend_m and f_idx >= 1:
                        pend_m.pop(0)()
                    bs_ = 2 * (jcnt % 3)
                    cs_ = jcnt % 2
                    ao = jcnt % 3
                    jcnt += 1
                    halves = ((0, bs_, TAc[cs_], 'TAc'), (1, bs_ + 1, TGc[cs_], 'TGc'))
                    for half, bb, Tc, tn in halves:
                        col = half * DFF + j * 128
                        mm(bank(bb, N), [(WUP[:, k, col:col + 128], H2[xs][:, k, 0:N]) for k in range(8)],
                           [f'H2{xs}', f'WUP{col // 256}'], [f'ps{bb}'])
                    for half, bb, Tc, tn in halves:
                        vm = half * NJ + j
                        act(Tc[:, 0:n_out], ps[:, bb * 512 + 1:bb * 512 + 1 + n_out], AF.Identity, [f'ps{bb}', VECr],
                            [f'{tn}{cs_}'], scale=VEC[:, 32 + 44 + vm:32 + 44 + vm + 1], bias=VEC[:, 164 + vm:164 + vm + 1])
                    for tap, off in ((0, 0), (2, 2)):
                        for half, bb, Tc, tn in halves:
                            vm = half * NJ + j
                            stt('dve', Tc[:, 0:n_out], ps[:, bb * 512 + off:bb * 512 + off + n_out],
                                VEC[:, 32 + tap * 44 + vm:32 + tap * 44 + vm + 1], Tc[:, 0:n_out], ALU.mult, ALU.add,
                                [f'ps{bb}', f'{tn}{cs_}', VECr], [f'{tn}{cs_}'])
                    act(SGc[cs_][:, 0:n_out], TGc[cs_][:, 0:n_out], AF.Silu, [f'TGc{cs_}'], [f'SGc{cs_}'])
                    tt('pool', AOs[ao][:, 0:n_out], TAc[cs_][:, 0:n_out], SGc[cs_][:, 0:n_out], ALU.mult,
                       [f'TAc{cs_}', f'SGc{cs_}'], [f'AOs{ao}'])
                    dma(s.actT[:, j, out_lo:out_hi], AOs[ao][:, 0:n_out], [f'AOs{ao}'], [], f'AOs{ao}')
            while pend_m:
                pend_m.pop(0)()
            P.barrier()
            ar.reset(m0)

            stop(f'E2a{l}')
            m0 = ar.mark()
            WDN = ar.alloc("WDN", [128, NJ, D], BF16)
            AT = [ar.alloc(f"AT{i}", [128, NJ, 512], BF16) for i in range(2)]
            XR = [ar.alloc(f"XR2{i}", [128, 8, 512], F32) for i in range(3)]
            tmp = alloc_ln_tmp()
            wdnv = wdn_d[l].rearrange("(j p) n -> p j n", p=128)
            load_weight(lambda ci: WDN[:, 2 * (ci // 4):2 * (ci // 4) + 2, (ci % 4) * 256:(ci % 4 + 1) * 256],
                        lambda ci: wdnv[:, 2 * (ci // 4):2 * (ci // 4) + 2, (ci % 4) * 256:(ci % 4 + 1) * 256], 44, 'WDN',
                        order=[jp * 4 + cq for cq in range(4) for jp in range(11)])
            e_tiles = [(s, t) for s in streams for t in range(s.L // s.T)]

            def e2b_load(idx):
                s, t = e_tiles[idx]
                N = s.T
                t0 = t * N
                sl = idx % 3
                al = idx % 2
                dma(AT[al][:, :, 0:N], s.actT[:, :, t0:t0 + N], [], [f'AT{al}'], f'AT{al}')
                dma(XR[sl][:, :, 0:N], s.x1T[:, :, t0:t0 + N], [], [f'XR{sl}c{c}' for c in range(8)], f'XR{sl}')

            def e2b_make(idx):
                s, t = e_tiles[idx]
                N = s.T
                sl = idx % 3
                al = idx % 2

                def mmf(c, b):
                    mm(bank(b, N), [(WDN[:, j, c * 128:(c + 1) * 128], AT[al][:, j, 0:N]) for j in range(NJ)],
                       [f'AT{al}'] + [f'WDN{jp * 4 + c // 2}' for jp in range(11)], [f'ps{b}'])
                return ln_make(s, XR[sl], f'XR{sl}', N, 5, 16, 24, mmf, tmp, 4 + 2 * (idx % 2))

            def e2b_store(idx):
                s, t = e_tiles[idx]
                N = s.T
                sl = idx % 3
                dma(s.xT[:, :, t * N:(t + 1) * N], XR[sl][:, :, 0:N], [f'XR{sl}c{c}' for c in range(8)], [], f'XRst{sl}')
            ln_pipeline(e_tiles, e2b_make, e2b_load, e2b_store)
            P.barrier()
            ar.reset(m0)
            stop(f'E2b{l}')

        XE = [ar.alloc(f"XE{i}", [128, 8, 512], F32) for i in range(2)]
        OT = [ar.alloc(f"OT{i}", [128, D], F32) for i in range(2)]
        ocnt = 0
        for t in range(L // 512):
            xs = t % 2
            dma(XE[xs][:], lat.xT[:, :, t * 512:(t + 1) * 512], [], [f'XE{xs}'], f'XE{xs}')
            for sub in range(4):
                osl = ocnt % 2
                b0 = 2 * (ocnt % 2)
                ocnt += 1
                P.add('pe', (lambda xs=xs, sub=sub, b0=b0: lambda e: [
                    e.transpose(ps[:, b0 * 512 + c * 128:b0 * 512 + (c + 1) * 128], XE[xs][:, c, sub * 128:(sub + 1) * 128], identf[:])
                    for c in range(8)][-1])(), reads=[f'XE{xs}', 'identf'], writes=[f'ps{b0}', f'ps{b0 + 1}'])
                tcopy('act' if sub % 2 == 0 else 'dve', OT[osl][:], ps[:, b0 * 512:b0 * 512 + 1024],
                      [f'ps{b0}', f'ps{b0 + 1}'], [f'OT{osl}'])
                r0 = t * 512 + sub * 128
                dma(out_d[r0:r0 + 128, :], OT[osl][:], [f'OT{osl}'], [], f'OT{osl}')
    try:
        _layers()
    except _Stop:
        P.emit(dummies, final_chans=())
        return nc
    P.emit(dummies, final_chans=('OT0', 'OT1'))
    return nc


_CACHE = {}


def kernel(x, c, ctx, c_ctx, w_mod, b_mod, w_in, pool_w, pool_scale, na_rpb, q_norm, k_norm,
           w_out, ln1_g, ln1_b, w_up, conv_w, conv_b, w_down, ln2_g, ln2_b):
    x = np.asarray(x, np.float32)
    B, L, _ = x.shape
    DEPTH = int(np.asarray(w_mod).shape[0])
    inp = dict(w_mod=w_mod, b_mod=b_mod, w_in=w_in, pool_w=pool_w, pool_scale=pool_scale, na_rpb=na_rpb,
               q_norm=q_norm, k_norm=k_norm, w_out=w_out, ln1_g=ln1_g, ln1_b=ln1_b, w_up=w_up, conv_w=conv_w,
               conv_b=conv_b, w_down=w_down, ln2_g=ln2_g, ln2_b=ln2_b)
    inp = {k: np.asarray(v, np.float32) for k, v in inp.items()}
    sh = _prep_shared(inp, L, DEPTH)
    key = (L, DEPTH)
    if key not in _CACHE:
        _CACHE[key] = build(L, DEPTH)
    nc = _CACHE[key]
    c = np.asarray(c, np.float32)
    ctx = np.asarray(ctx, np.float32)
    cc = _chunk(np.asarray(c_ctx, np.float32), 8)
    in_maps = []
    for b in range(B):
        cv = np.stack([_chunk(c[b], 8), cc], axis=2).reshape(128, 16)
        m = dict(sh)
        m['x'] = np.ascontiguousarray(x[b])
        m['ctx'] = np.ascontiguousarray(ctx[b])
        m['cvec'] = np.ascontiguousarray(cv)
        in_maps.append(m)
    res = run_bass_kernel_spmd(nc, in_maps, core_ids=list(range(B)))
    return np.stack([np.asarray(r['out'], np.float32) for r in res.results], 0)
```

```python
import numpy as np
from contextlib import ExitStack
import concourse.bass as bass
import concourse.mybir as mybir
from concourse.bass_utils import run_bass_kernel_spmd

F32 = mybir.dt.float32
BF16 = mybir.dt.bfloat16
AF = mybir.ActivationFunctionType
ALU = mybir.AluOpType

D = 1024
CTX = 256
DFF = 2816
NJ = 22
GRID_W = 64
NEG = -30000.0
LN_EPS = 1e-6
NVEC = 214
NWIN = 2560
RING = 12


class Prog:
    CE = ('pe', 'act', 'dve', 'pool')

    def __init__(self, nc):
        self.nc = nc
        self.segs = []
        self.chans = {}
        self.nbar = 0
        self._new_seg()

    def _new_seg(self):
        self.ops = []
        self.lastw = {}
        self.rd = {}
        self.chan_cnt = {}
        self.segs.append((self.ops, self.chan_cnt))

    def add(self, eng, fn, reads=(), writes=(), chan=None):
        i = len(self.ops)
        dma = eng == 'sp'
        deps = {}

        def dep(j, kind):
            o = self.ops[j]
            if not dma and not o['dma'] and o['eng'] == eng and kind != 'raw':
                return
            if dma and o['dma'] and kind == 'waw' and o['chan'] == chan:
                return
            deps[j] = True
        for r in reads:
            if r in self.lastw:
                dep(self.lastw[r], 'raw')
            if r.startswith('ps'):
                for j in self.rd.get(r, ()):
                    if self.ops[j]['eng'] != eng:
                        deps[j] = True
        for r in writes:
            if r in self.lastw:
                dep(self.lastw[r], 'waw')
            for j in self.rd.get(r, ()):
                dep(j, 'war')
        for r in reads:
            self.rd.setdefault(r, []).append(i)
        for r in writes:
            self.lastw[r] = i
            self.rd[r] = []
        op = dict(eng=eng, fn=fn, deps=list(deps), dma=dma, chan=chan, flag=False, ticket=None)
        if dma:
            assert chan is not None
            self.chans[chan] = True
            self.chan_cnt[chan] = self.chan_cnt.get(chan, 0) + 16
            op['ticket'] = self.chan_cnt[chan]
            op['flag'] = True
        for j in deps:
            self.ops[j]['flag'] = True
        self.ops.append(op)
        return i

    def barrier(self):
        self._new_seg()

    def emit(self, dummies, final_chans=()):
        nc = self.nc
        for ops, _ in self.segs:
            cnt = {e: 0 for e in self.CE}
            for o in ops:
                if not o['dma'] and o['flag']:
                    cnt[o['eng']] += 1
                    o['ticket'] = cnt[o['eng']]
                    assert cnt[o['eng']] < 30000
            for c, v in _.items():
                assert v < 30000, (c, v)
        with ExitStack() as st:
            sems = {e: st.enter_context(nc.semaphore('S_' + e)) for e in self.CE}
            bsem = {e: st.enter_context(nc.semaphore('B_' + e)) for e in self.CE}
            barc = st.enter_context(nc.semaphore('C_bar'))
            for c in self.chans:
                sems['c:' + c] = st.enter_context(nc.semaphore('C_' + c))
            block = st.enter_context(nc.Block())
            segs = self.segs
            nseg = len(segs)

            def run(engname):
                def body(e):
                    for si, (ops, chan_cnt) in enumerate(segs):
                        known = {}
                        for o in ops:
                            if o['eng'] != engname:
                                continue
                            need = {}
                            for j in o['deps']:
                                d = ops[j]
                                key = ('c:' + d['chan']) if d['dma'] else d['eng']
                                need[key] = max(need.get(key, 0), d['ticket'])
                            for key, v in need.items():
                                if known.get(key, 0) < v:
                                    e.wait_ge(sems[key], v)
                                    known[key] = v
                            ins = o['fn'](e)
                            if o['flag']:
                                if o['dma']:
                                    ins.then_inc(sems['c:' + o['chan']], 16)
                                else:
                                    ins.then_inc(sems[engname], 1)
                        last = si == nseg - 1
                        if engname == 'sp':
                            for c, v in chan_cnt.items():
                                if last and c not in final_chans:
                                    continue
                                e.wait_ge(sems['c:' + c], v)
                            if last:
                                continue
                            for ce in self.CE:
                                e.wait_ge(bsem[ce], 2 * si + 1)
                            for c in chan_cnt:
                                e.sem_clear(sems['c:' + c])
                            dummies['sp'](e).then_inc(barc, 16)
                            for ce in self.CE:
                                e.wait_ge(bsem[ce], 2 * si + 2)
                            dummies['sp'](e).then_inc(barc, 16)
                        else:
                            if last:
                                continue
                            if engname == 'pe':
                                for ce in ('act', 'dve', 'pool'):
                                    e.wait_ge(bsem[ce], 2 * si + 1)
                            dummies[engname](e).then_inc(bsem[engname], 1)
                            e.wait_ge(barc, 16 * (2 * si + 1))
                            e.sem_clear(sems[engname])
                            dummies[engname](e).then_inc(bsem[engname], 1)
                            e.wait_ge(barc, 16 * (2 * si + 2))
                return body
            block.tensor(run('pe'))
            block.scalar(run('act'))
            block.vector(run('dve'))
            block.gpsimd(run('pool'))
            block.sync(run('sp'))


class Arena:
    def __init__(self, nc, base=20480, limit=229344):
        self.nc = nc
        self.off = base
        self.limit = limit
        self.n = 0

    def alloc(self, name, shape, dtype):
        esz = 4 if dtype == F32 else 2
        nbytes = int(np.prod(shape[1:])) * esz
        nbytes = (nbytes + 63) // 64 * 64
        assert self.off + nbytes <= self.limit, (name, self.off, nbytes)
        self.n += 1
        t = self.nc.alloc_sbuf_tensor_at(f"{name}_{self.n}", list(shape), dtype, offset=self.off)
        self.off += nbytes
        return t

    def mark(self):
        return self.off

    def reset(self, m):
        self.off = m


def _partner_sign():
    partner = np.zeros(64, np.int64)
    sign = np.zeros(64, np.float32)
    for i in range(64):
        blk, r = divmod(i, 32)
        if r < 16:
            partner[i] = blk * 32 + r + 16
            sign[i] = -1.0
        else:
            partner[i] = blk * 32 + r - 16
            sign[i] = 1.0
    return partner, sign


def _rope_tables(L):
    partner, sign = _partner_sign()
    t = np.arange(L, dtype=np.int32)
    inv = (np.float32(10000.0) ** (-np.arange(0, 32, 2, dtype=np.float32) / np.float32(32))).astype(np.float32)
    ang_r = (t // GRID_W).astype(np.float32)[:, None] * inv
    ang_c = (t % GRID_W).astype(np.float32)[:, None] * inv
    cos = np.zeros((64, L), np.float32)
    sin = np.zeros((64, L), np.float32)
    for i in range(64):
        blk, r = divmod(i, 32)
        a = ang_r if blk == 0 else ang_c
        cos[i] = np.cos(a[:, r % 16])
        sin[i] = np.sin(a[:, r % 16]) * sign[i]
    return np.concatenate([cos, cos], 0), np.concatenate([sin, sin], 0)


def _band_tables():
    ab = np.zeros((128, 20, 128), np.float32)
    a = np.arange(128)[:, None]
    b = np.arange(128)[None, :]
    for g, w in enumerate((2, 4, 8, 16)):
        h = w // 2
        ab[:, g * 5 + 0, :] = np.where(a - 128 >= b - h, 1.0 / w, 0.0)
        cur = np.where((a >= b - h) & (a <= b + h - 1), 1.0 / w, 0.0)
        ab[:, g * 5 + 1, :] = cur - (a == b)
        ab[:, g * 5 + 2, :] = np.where(128 + a <= b + h - 1, 1.0 / w, 0.0)
        lo = np.maximum(b - h, 0)
        hi = b + h - 1
        cnt = (hi - lo + 1).astype(np.float32)
        ab[:, g * 5 + 3, :] = np.where((a >= lo) & (a <= hi), 1.0 / cnt, 0.0) - (a == b)
        lo = b - h
        hi = np.minimum(b + h - 1, 127)
        cnt = (hi - lo + 1).astype(np.float32)
        ab[:, g * 5 + 4, :] = np.where((a >= lo) & (a <= hi), 1.0 / cnt, 0.0) - (a == b)
    return ab


def _chunk(v, n):
    return np.ascontiguousarray(np.asarray(v, np.float32).reshape(n, 128).T)


def _prep_shared(inp, L, DEPTH):
    partner, _ = _partner_sign()
    a128 = np.arange(128)
    a64 = np.arange(64)
    cols = []
    for i in range(3):
        cols.append(256 + i * 128 + a128)
    for i in range(3):
        cols.append(640 + i * 128 + a128)
    for i in range(3):
        cols.append(np.concatenate([1408 + i * 64 + a64, 1408 + (i + 3) * 64 + a64]))
    cols.append(1792 + a128)
    for i in range(3):
        cols.append(np.concatenate([1408 + i * 64 + partner, 1408 + (i + 3) * 64 + partner]))
    cols.append(np.concatenate([1792 + partner, 1856 + partner]))
    cols.append(np.arange(0, 256))
    cols.append(1024 + np.arange(384))
    cols.append(1920 + a128)
    cols = np.concatenate(cols)
    assert cols.shape[0] == NWIN
    rows = [np.arange(0, 640)]
    for i in range(3):
        rows.append(640 + i * 64 + a64)
        rows.append(640 + (i + 3) * 64 + a64)
    rows = np.concatenate(rows)
    sh = {}
    sh['w_mod'] = np.ascontiguousarray(inp['w_mod'], np.float32)
    bm = np.stack([_chunk(inp['b_mod'][l], 48) for l in range(DEPTH)], 0)
    sh['bmod'] = np.ascontiguousarray(np.repeat(bm, 2, axis=2))
    sh['w_in'] = np.ascontiguousarray(np.asarray(inp['w_in'], np.float32)[:, :, cols])
    sh['w_out'] = np.ascontiguousarray(np.asarray(inp['w_out'], np.float32)[:, rows, :])
    sh['w_up'] = np.ascontiguousarray(inp['w_up'], np.float32)
    sh['w_down'] = np.ascontiguousarray(inp['w_down'], np.float32)
    sh['pw'] = np.ascontiguousarray(np.asarray(inp['pool_w'], np.float32).reshape(DEPTH, 2, 128, 64))
    vecs = np.zeros((DEPTH, 128, NVEC), np.float32)
    for l in range(DEPTH):
        vecs[l, :, 0:8] = _chunk(inp['ln1_g'][l], 8)
        vecs[l, :, 8:16] = _chunk(inp['ln1_b'][l], 8)
        vecs[l, :, 16:24] = _chunk(inp['ln2_g'][l], 8)
        vecs[l, :, 24:32] = _chunk(inp['ln2_b'][l], 8)
        for j in range(3):
            vecs[l, :, 32 + j * 44:32 + (j + 1) * 44] = _chunk(inp['conv_w'][l, j], 44)
        vecs[l, :, 164:208] = _chunk(inp['conv_b'][l], 44)
        vecs[l, :, 208:210] = _chunk(inp['pool_scale'][l], 2)
        qn = np.asarray(inp['q_norm'][l], np.float32)
        kn = np.asarray(inp['k_norm'][l], np.float32)
        vecs[l, :, 210] = np.concatenate([qn, qn])
        vecs[l, :, 211] = np.concatenate([qn[partner], qn[partner]])
        vecs[l, :, 212] = np.concatenate([kn, kn])
        vecs[l, :, 213] = np.concatenate([kn[partner], kn[partner]])
    sh['vecs'] = vecs
    rpb = np.asarray(inp['na_rpb'], np.float32)
    kc = np.arange(64)[:, None]
    qc = np.arange(64)[None, :]
    cs = np.clip(qc - 8, 0, 48)
    valid = (kc >= cs) & (kc < cs + 16)
    dc = np.clip(kc - qc + 15, 0, 30)
    nab = np.full((DEPTH, 64, 6, 15, 64), NEG, np.float32)
    for l in range(DEPTH):
        for h in range(6):
            for idx in range(15):
                g = rpb[l, h, 14 - idx][dc]
                nab[l, :, h, idx, :] = np.where(valid, g, np.float32(NEG))
    sh['nab'] = nab
    sh['ab'] = _band_tables()
    cos, sinp = _rope_tables(L)
    sh['cos'] = cos
    sh['sinp'] = sinp
    return sh


class _S:
    pass


DEBUG = False
STOP = None


class _Stop(Exception):
    pass


def build(L, DEPTH):
    assert L % 512 == 0
    ROWS = L // GRID_W
    alpha = float((2 * DEPTH) ** 0.25)
    eps_ln = float(LN_EPS / (alpha * alpha))
    nc = bass.Bass("TRN2", target_bir_lowering=False)

    def din(name, shape, dt=F32):
        return nc.dram_tensor(name, list(shape), dt, kind="ExternalInput").ap()

    def dscr(name, shape, dt):
        return nc.dram_tensor(name, list(shape), dt, kind=("ExternalOutput" if DEBUG else "Internal")).ap()

    def stop(tag):
        if STOP == tag:
            raise _Stop()
    x_d = din("x", [L, D])
    ctx_d = din("ctx", [CTX, D])
    cvec_d = din("cvec", [128, 16])
    wmod_d = din("w_mod", [DEPTH, D, 6 * D])
    bmod_d = din("bmod", [DEPTH, 128, 96])
    win_d = din("w_in", [DEPTH, D, NWIN])
    wout_d = din("w_out", [DEPTH, D, D])
    wup_d = din("w_up", [DEPTH, D, 2 * DFF])
    wdn_d = din("w_down", [DEPTH, DFF, D])
    pw_d = din("pw", [DEPTH, 2, 128, 64])
    vecs_d = din("vecs", [DEPTH, 128, NVEC])
    nab_d = din("nab", [DEPTH, 64, 6, 15, 64])
    ab_d = din("ab", [128, 20, 128])
    cos_d = din("cos", [128, L])
    sinp_d = din("sinp", [128, L])
    out_d = nc.dram_tensor("out", [L, D], F32, kind="ExternalOutput").ap()
    dum_d = dscr("dum", [128, 8], F32)

    lat = _S()
    cx = _S()
    for s, n, LL, T in ((lat, "l", L, 512), (cx, "c", CTX, 256)):
        s.L = LL
        s.T = T
        s.ctx = n == "c"
        s.n = n
        s.xT = dscr("xT" + n, [128, 8, LL], F32)
        s.x1T = dscr("x1T" + n, [128, 8, LL], F32)
        s.qnT = dscr("qnT" + n, [128, 3, LL], BF16)
        s.qgT = dscr("qgT" + n, [128, 3, LL], BF16)
        s.pin = dscr("pin" + n, [LL, 256], BF16)
        s.catT = dscr("catT" + n, [128, 8, LL], BF16)
        s.actT = dscr("actT" + n, [128, NJ, LL], BF16)
    lat.knT = dscr("knTl", [128, 3, L], BF16)
    lat.vn = dscr("vnl", [L, 384], BF16)

    P = Prog(nc)
    ar = Arena(nc)
    ps = nc.alloc_psum_tensor("ps", [128, 4096], F32)

    def bank(b, n=512, p0=0, p1=128):
        return ps[p0:p1, b * 512:b * 512 + n]

    identf = ar.alloc("identf", [128, 128], F32)
    identb = ar.alloc("identb", [128, 128], BF16)
    onesb = ar.alloc("onesb", [128, 128], BF16)
    blkb = ar.alloc("blkb", [128, 128], BF16)
    ABb = ar.alloc("ABb", [128, 20, 128], BF16)
    MODTs = [ar.alloc(f"MODT{i}", [128, 6, 8, 2], F32) for i in range(2)]
    VECs = [ar.alloc(f"VEC{i}", [128, NVEC], F32) for i in range(2)]
    cur_l = [0]
    ACT2 = ar.alloc("ACT2", [128, 8, 2], F32)
    PWb = ar.alloc("PWb", [128, 2, 64], BF16)
    BMp = ar.alloc("BMp", [128, 96], F32)
    PWs = ar.alloc("PWs", [128, 2, 64], F32)
    M2c = [ar.alloc(f"M2c{i}", [128, 256], F32) for i in range(2)]
    dumt = {e: ar.alloc("dum" + e, [128, 8], F32) for e in ('act', 'dve', 'pool', 'sp')}
    WST = [ar.alloc(f"WST{i}", [128, 8, 256], F32) for i in range(3)]
    NKT = 2 + L // 128
    base_mark = ar.mark()

    dummies = {
        'pe': lambda e: e.matmul(ps[:, 0:1], lhsT=identb[:, 0:128], rhs=identb[:, 0:1], start=True, stop=True),
        'act': lambda e: e.activation(out=dumt['act'][:, 0:1], in_=dumt['act'][:, 1:2], func=AF.Copy),
        'dve': lambda e: e.memset(dumt['dve'][:, 0:1], 0.0),
        'pool': lambda e: e.memset(dumt['pool'][:, 0:1], 0.0),
        'sp': lambda e: e.dma_start(out=dumt['sp'][:, :], in_=dum_d[:, :]),
    }

    def dma(out, in_, reads, writes, chan):
        P.add('sp', lambda e: e.dma_start(out=out, in_=in_), reads=reads, writes=writes, chan=chan)

    def mm(out, pairs, reads, writes, tps=None):
        def fn(e):
            n = len(pairs)
            ins = None
            for i, (lt, r) in enumerate(pairs):
                kw = {}
                if tps is not None:
                    kw['tile_position'] = tps[i]
                ins = e.matmul(out, lhsT=lt, rhs=r, start=(i == 0), stop=(i == n - 1), **kw)
            return ins
        P.add('pe', fn, reads=reads, writes=writes)

    def mms(lst, reads, writes):
        def fn(e):
            ins = None
            for (o, lt, r, s0, s1, tp) in lst:
                kw = {}
                if tp is not None:
                    kw['tile_position'] = tp
                ins = e.matmul(o, lhsT=lt, rhs=r, start=s0, stop=s1, **kw)
            return ins
        P.add('pe', fn, reads=reads, writes=writes)

    def act(out, in_, func, reads, writes, scale=None, bias=None):
        kw = {}
        if scale is not None:
            kw['scale'] = scale
        if bias is not None:
            kw['bias'] = bias
        P.add('act', lambda e: e.activation(out=out, in_=in_, func=func, **kw), reads=reads, writes=writes)

    def tcopy(eng, out, in_, reads, writes):
        if eng == 'act':
            act(out, in_, AF.Copy, reads, writes)
        else:
            P.add(eng, lambda e: e.tensor_copy(out=out, in_=in_), reads=reads, writes=writes)

    def tt(eng, out, in0, in1, op, reads, writes):
        P.add(eng, lambda e: e.tensor_tensor(out=out, in0=in0, in1=in1, op=op), reads=reads, writes=writes)

    def ts(eng, out, in0, s1, s2, op0, op1, reads, writes):
        if s2 is None:
            P.add(eng, lambda e: e.tensor_scalar(out=out, in0=in0, scalar1=s1, scalar2=None, op0=op0),
                  reads=reads, writes=writes)
        else:
            P.add(eng, lambda e: e.tensor_scalar(out=out, in0=in0, scalar1=s1, scalar2=s2, op0=op0, op1=op1),
                  reads=reads, writes=writes)

    def stt(eng, out, in0, sc, in1, op0, op1, reads, writes):
        P.add(eng, lambda e: e.scalar_tensor_tensor(out=out, in0=in0, scalar=sc, in1=in1, op0=op0, op1=op1),
              reads=reads, writes=writes)

    def recip(out, in_, reads, writes):
        P.add('dve', lambda e: e.reciprocal(out=out, in_=in_), reads=reads, writes=writes)

    def memset(eng, ap, v, writes):
        P.add(eng, lambda e: e.memset(ap, v), writes=writes)

    cast_rr = [0]

    def load_weight(dst_fn, src_fn, nchunks, wname, order=None):
        for ci in (order if order is not None else range(nchunks)):
            sl = cast_rr[0] % 3
            eng = ('dve', 'act')[cast_rr[0] % 2]
            cast_rr[0] += 1
            d = dst_fn(ci)
            s = src_fn(ci)
            shp = d.shape
            if len(shp) == 3:
                stv = WST[sl][:, 0:shp[1], 0:shp[2]]
            else:
                stv = WST[sl][:, 0, 0:shp[1]]
            stv = stv[0:shp[0]]
            dma(stv, s, [], [f'WST{sl}'], f'WST{sl}')
            tcopy(eng, d, stv, [f'WST{sl}'], [f'{wname}{ci}'])

    def MODap(part, c, s):
        return MODTs[cur_l[0] % 2][:, part, c, (1 if s.ctx else 0):(2 if s.ctx else 1)]

    memset('pool', identf[:], 0.0, ['identf'])
    P.add('pool', lambda e: e.affine_select(out=identf[:], in_=identf[:], pattern=[[-1, 128]],
                                            compare_op=ALU.not_equal, fill=1.0, base=0, channel_multiplier=1),
          reads=['identf'], writes=['identf'])
    tcopy('pool', identb[:], identf[:], ['identf'], ['identb'])
    memset('pool', onesb[:], 1.0, ['onesb'])
    memset('pool', blkb[:], 0.0, ['blkb'])
    memset('pool', blkb[0:64, 0:64], 1.0, ['blkb'])
    memset('pool', blkb[64:128, 64:128], 1.0, ['blkb'])
    for e_ in ('act', 'dve', 'pool'):
        memset('pool' if e_ == 'pool' else 'dve', dumt[e_][:], 0.0, ['dum' + e_])
    for q in range(3):
        n0, n1 = q * 7, min(20, q * 7 + 7)
        dma(WST[q][:, 0:n1 - n0, 0:128], ab_d[:, n0:n1, :], [], [f'WST{q}'], f'WST{q}')
        tcopy('dve', ABb[:, n0:n1, :], WST[q][:, 0:n1 - n0, 0:128], [f'WST{q}'], ['ABb'])
    dma(ACT2[:].rearrange("p a b -> p (a b)"), cvec_d[:, :], [], ['ACT2'], 'ACT2')
    act(ACT2[:], ACT2[:], AF.Silu, ['ACT2'], ['ACT2'])

    m0 = ar.mark()
    XIN = [ar.alloc(f"XIN{i}", [128, D], F32) for i in range(2)]
    XTS = [ar.alloc(f"XTS{i}", [128, 8, 512], F32) for i in range(2)]
    for s, src in ((cx, ctx_d), (lat, x_d)):
        for t in range(s.L // s.T):
            N = s.T
            xs = t % 2
            for sub in range(N // 128):
                j = t * (N // 128) + sub
                isl = j % 2
                dma(XIN[isl][:], src[j * 128:(j + 1) * 128, :], [], [f'XIN{isl}'], f'XIN{isl}')
                b0 = 2 * (j % 2)
                P.add('pe', (lambda isl=isl, b0=b0: lambda e: [e.transpose(ps[:, b0 * 512 + c * 128:b0 * 512 + (c + 1) * 128],
                                                                          XIN[isl][:, c * 128:(c + 1) * 128], identf[:])
                                                             for c in range(8)][-1])(),
                      reads=[f'XIN{isl}', 'identf'], writes=[f'ps{b0}', f'ps{b0 + 1}'])
                tcopy('act' if sub % 2 == 0 else 'dve', XTS[xs][:, :, sub * 128:(sub + 1) * 128],
                      ps[:, b0 * 512:b0 * 512 + 1024].rearrange("p (c n) -> p c n", c=8),
                      [f'ps{b0}', f'ps{b0 + 1}'], [f'XTS{xs}'])
            dma(s.xT[:, :, t * N:(t + 1) * N], XTS[xs][:, :, 0:N], [f'XTS{xs}'], [], f'XTS{xs}')
    P.barrier()
    ar.reset(m0)

    def ln_make(s, X, xres, N, gpart, gcol, bcol, mm_fn, tmp, sbk):
        YB, SQ, MEAN, MSQ, SD, RSTD, TT = tmp
        rr = (0, 1, 2)
        b6, b7 = sbk, sbk + 1

        def stats(c):
            mms([(bank(b6, N), onesb[:], YB[c % 2][:, 0:N], c == 0, c == 7, None),
                 (bank(b7, N), onesb[:], SQ[c % 2][:, 0:N], c == 0, c == 7, None)],
                [f'YB{c % 2}', f'SQ{c % 2}', 'onesb'], [f'ps{b6}', f'ps{b7}'])

        def head_chunk(c):
            b = rr[c % 3]
            mm_fn(c, b)
            if c >= 1:
                stats(c - 1)
            stt('dve', X[:, c, 0:N], bank(b, N), MODap(gpart, c, s), X[:, c, 0:N], ALU.mult, ALU.add,
                [f'ps{b}', xres + f'c{c}', f'MOD{cur_l[0] % 2}'], [xres + f'c{c}'])
            tcopy('act', YB[c % 2][:, 0:N], X[:, c, 0:N], [xres + f'c{c}'], [f'YB{c % 2}'])
            act(SQ[c % 2][:, 0:N], X[:, c, 0:N], AF.Square, [xres + f'c{c}'], [f'SQ{c % 2}'])

        def head_post():
            stats(7)

        def tail_pre():
            ts('dve', MEAN[:, 0:N], bank(b6, N), 1.0 / D, None, ALU.mult, None, [f'ps{b6}'], ['MEAN'])
            tt('pool', MSQ[:, 0:N], MEAN[:, 0:N], MEAN[:, 0:N], ALU.mult, ['MEAN'], ['MSQ'])
            stt('dve', SD[:, 0:N], bank(b7, N), 1.0 / D, MSQ[:, 0:N], ALU.mult, ALU.subtract, [f'ps{b7}', 'MSQ'], ['SD'])
            ts('dve', SD[:, 0:N], SD[:, 0:N], eps_ln, None, ALU.add, None, ['SD'], ['SD'])
            act(SD[:, 0:N], SD[:, 0:N], AF.Sqrt, ['SD'], ['SD'])
            recip(RSTD[:, 0:N], SD[:, 0:N], ['SD'], ['RSTD'])

        def tail_chunk(c):
            tt('pool', TT[c % 2][:, 0:N], X[:, c, 0:N], MEAN[:, 0:N], ALU.subtract,
               [xres + f'c{c}', 'MEAN'], [f'TT{c % 2}'])
            tt('dve', TT[c % 2][:, 0:N], TT[c % 2][:, 0:N], RSTD[:, 0:N], ALU.mult,
               [f'TT{c % 2}', 'RSTD'], [f'TT{c % 2}'])
            VEC_ = VECs[cur_l[0] % 2]
            act(X[:, c, 0:N], TT[c % 2][:, 0:N], AF.Identity, [f'TT{c % 2}', f'VEC{cur_l[0] % 2}'], [xres + f'c{c}'],
                scale=VEC_[:, gcol + c:gcol + c + 1], bias=VEC_[:, bcol + c:bcol + c + 1])
        return head_chunk, head_post, tail_pre, tail_chunk

    def ln_pipeline(tiles, make, load, store):
        n = len(tiles)
        load(0)
        prev = None
        for idx in range(n + 1):
            cur = make(idx) if idx < n else None
            if idx + 1 < n:
                load(idx + 1)
            if prev is not None:
                prev[2]()
            for c in range(8):
                if cur is not None:
                    cur[0](c)
                if prev is not None:
                    prev[3](c)
            if cur is not None:
                cur[1]()
            if prev is not None:
                store(idx - 1)
            prev = cur

    def alloc_ln_tmp():
        YB = [ar.alloc(f"YB{i}", [128, 512], BF16) for i in range(2)]
        SQ = [ar.alloc(f"SQ{i}", [128, 512], BF16) for i in range(2)]
        MEAN = ar.alloc("MEAN", [128, 512], F32)
        MSQ = ar.alloc("MSQ", [128, 512], F32)
        SD = ar.alloc("SD", [128, 512], F32)
        RSTD = ar.alloc("RSTD", [128, 512], F32)
        TT = [ar.alloc(f"TT{i}", [128, 512], F32) for i in range(2)]
        return (YB, SQ, MEAN, MSQ, SD, RSTD, TT)

    def _layers():
        stop('pro')
        for l in range(DEPTH):
            last_layer = l == DEPTH - 1
            streams = [cx, lat]
            mL = ar.mark()
            KTg = ar.alloc("KTg", [128, CTX + L], BF16)
            VG = ar.alloc("VG", [128, NKT, 2, 128], BF16)
            KNC = ar.alloc("KNC", [128, 3, CTX], BF16)
            VNC = ar.alloc("VNC", [128, 6, 2, 128], BF16)
            memset('pool', VG[:], 1.0, ['VG'])
            memset('pool', VNC[:], 1.0, ['VNC'])
            cur_l[0] = l
            VEC = VECs[l % 2]
            MODT = MODTs[l % 2]
            MODr = f'MOD{l % 2}'
            VECr = f'VEC{l % 2}'

            def m_steps(lm):
                MT = MODTs[lm % 2]
                mr = f'MOD{lm % 2}'
                steps = []

                def first():
                    dma(VECs[lm % 2][:], vecs_d[lm], [], [f'VEC{lm % 2}'], f'VEC{lm % 2}')
                    dma(BMp[:], bmod_d[lm], [], ['BMp'], 'BMp')
                steps.append(first)
                wmv = wmod_d[lm].rearrange("(k p) n -> p k n", p=128)

                def tr(q):
                    def fn(e):
                        ins = None
                        for jj in range(2):
                            j = q * 2 + jj
                            ins = e.transpose(ps[:, 3584 + 256 + j * 2:3584 + 256 + j * 2 + 2],
                                              M2c[q % 2][0:2, jj * 128:(jj + 1) * 128], identf[0:2, 0:2])
                        return ins
                    P.add('pe', fn, reads=[f'M2c{q % 2}', 'identf'], writes=['ps7t'])
                def cdma(q):
                    sl = q % 3
                    dma(WST[sl][:], wmv[:, :, q * 256:(q + 1) * 256], [], [f'WST{sl}'], f'WST{sl}')

                def first2():
                    cdma(0)
                    cdma(1)
                steps.append(first2)
                for q in range(24):
                    def chunk(q=q):
                        sl = q % 3
                        if q + 2 < 24:
                            cdma(q + 2)
                        mms([(ps[0:2, 3584:3584 + 256], ACT2[:, k, :], WST[sl][:, k, :], k == 0, k == 7, None)
                             for k in range(8)], [f'WST{sl}', 'ACT2'], ['ps7'])
                        if q >= 1:
                            tr(q - 1)
                        tcopy('dve', M2c[q % 2][0:2, :], ps[0:2, 3584:3584 + 256], ['ps7', 'ps7t'], [f'M2c{q % 2}'])
                    steps.append(chunk)

                def fin():
                    tr(23)
                    tt('dve', MT[:].rearrange("p a b c -> p (a b c)"), ps[:, 3584 + 256:3584 + 256 + 96], BMp[:], ALU.add,
                       ['ps7t', 'ps7', 'BMp'], [mr])
                    for part in (1, 4):
                        ts('dve', MT[:, part], MT[:, part], 1.0, None, ALU.add, None, [mr], [mr])
                    for part in (2, 5):
                        ts('dve', MT[:, part], MT[:, part], 1.0 / alpha, None, ALU.mult, None, [mr], [mr])
                    dma(PWs[:], pw_d[lm].rearrange("i p d -> p i d"), [], ['PWs'], 'PWs')
                    tcopy('dve', PWb[:], PWs[:], ['PWs'], ['PWb'])
                steps.append(fin)
                return steps
            if l == 0:
                for f_ in m_steps(0):
                    f_()
            if DEBUG and l == 0:
                modt_d = nc.dram_tensor("modt", [128, 96], F32, kind="ExternalOutput").ap()
                dma(modt_d[:, :], MODT[:].rearrange("p a b c -> p (a b c)"), [MODr], [], 'XT')
                act2_d = nc.dram_tensor("act2", [128, 16], F32, kind="ExternalOutput").ap()
                dma(act2_d[:, :], ACT2[:].rearrange("p a b -> p (a b)"), ['ACT2'], [], 'XT')
            if DEBUG:
                P.barrier()
                stop(f'M{l}')

            m0 = ar.mark()
            WIN = ar.alloc("WIN", [128, 8, NWIN], BF16)
            XT = ar.alloc("XT", [128, 8, 512], F32)
            H = [ar.alloc(f"H{i}", [128, 8, 512], BF16) for i in range(2)]
            QNs = ar.alloc("QNs", [128, 3, 512], BF16)
            KNs = ar.alloc("KNs", [128, 3, 512], BF16)
            QGs = ar.alloc("QGs", [128, 3, 512], BF16)
            PINs = ar.alloc("PINs", [128, 4, 256], BF16)
            VNs = ar.alloc("VNs", [128, 4, 384], BF16)
            CS = [ar.alloc(f"CS{i}", [128, 2, 512], F32) for i in range(2)]
            SQb = [ar.alloc(f"SQb{i}", [128, 512], BF16) for i in range(2)]
            Vt = ar.alloc("Vt", [128, 512], F32)
            RS = ar.alloc("RS", [128, 512], F32)
            TA = [ar.alloc(f"TA{i}", [128, 512], F32) for i in range(2)]
            TB = [ar.alloc(f"TB{i}", [128, 512], F32) for i in range(2)]
            winv = win_d[l].rearrange("(k p) n -> p k n", p=128)
            load_weight(lambda ci: WIN[:, :, ci * 256:(ci + 1) * 256], lambda ci: winv[:, :, ci * 256:(ci + 1) * 256],
                        NWIN // 256, 'WIN')

            def wres(c0, c1):
                return [f'WIN{i}' for i in range(c0 // 256, (c1 - 1) // 256 + 1)]
            if STOP == f'Aw{l}':
                P.barrier()
                stop(f'Aw{l}')
            fmrr = [0]
            tcnt = 0
            a_tiles = [(s, t) for s in streams for t in range(s.L // s.T)]

            def a_load(idx):
                s, t = a_tiles[idx]
                N = s.T
                t0 = t * N
                hs = idx % 2
                dma(XT[:, :, 0:N], s.xT[:, :, t0:t0 + N], [], ['XT'], 'XT')
                if not s.ctx:
                    dma(CS[hs][:, 0, :], cos_d[:, t0:t0 + N], [], [f'CS{hs}'], f'CS{hs}')
                    dma(CS[hs][:, 1, :], sinp_d[:, t0:t0 + N], [], [f'CS{hs}'], f'CS{hs}')
            def a_h(idx):
                s, t = a_tiles[idx]
                N = s.T
                hs = idx % 2
                for c in range(8):
                    act(H[hs][:, c, 0:N], XT[:, c, 0:N], AF.Identity, ['XT', MODr], [f'H{hs}'],
                        scale=MODap(1, c, s), bias=MODap(0, c, s))
            a_load(0)
            a_h(0)
            if len(a_tiles) > 1:
                a_load(1)
            rcnt = [0]
            for a_idx, (s, t) in enumerate(a_tiles):
                N = s.T
                if True:
                    t0 = t * N
                    hs = tcnt % 2
                    tcnt += 1

                    def fm(m, hs=hs, N=N):
                        b = fmrr[0] % 3
                        fmrr[0] += 1
                        mm(bank(b, N), [(WIN[:, k, m * 128:(m + 1) * 128], H[hs][:, k, 0:N]) for k in range(8)],
                           [f'H{hs}'] + wres(m * 128, (m + 1) * 128), [f'ps{b}'])
                        return b
                    need_q = not (s.ctx and last_layer)
                    if STOP == f'Ah{l}':
                        P.barrier()
                        stop(f'Ah{l}')
                    if need_q:
                        for m in range(3):
                            b = fm(m)
                            ts('dve', QNs[:, m, 0:N], bank(b, N), 0.125, None, ALU.mult, None, [f'ps{b}'], ['QNs'])
                        dma(s.qnT[:, :, t0:t0 + N], QNs[:, :, 0:N], ['QNs'], [], 'QNs')
                    for m in range(3, 6):
                        b = fm(m)
                        if s.ctx:
                            tcopy('act', KNC[:, m - 3, 0:N], bank(b, N), [f'ps{b}'], ['KNC'])
                        else:
                            tcopy('act', KNs[:, m - 3, 0:N], bank(b, N), [f'ps{b}'], ['KNs'])
                    if not s.ctx:
                        dma(s.knT[:, :, t0:t0 + N], KNs[:, :, 0:N], ['KNs'], [], 'KNs')
                    if STOP == f'Ak{l}':
                        P.barrier()
                        stop(f'Ak{l}')
                    rope_ms = ([6, 7, 8] if need_q else []) + [9]
                    deferred = None
                    for m in rope_ms:
                        bz = fm(m)
                        bp = None if s.ctx else fm(m + 4)
                        gcol = 210 if m < 9 else 212
                        rq = rcnt[0] % 2
                        rcnt[0] += 1
                        act(SQb[rq][:, 0:N], bank(bz, N), AF.Square, [f'ps{bz}'], [f'SQb{rq}'])
                        if m < 9:
                            dst, dres = QGs[:, m - 6, 0:N], 'QGs'
                        else:
                            koff = 0 if s.ctx else CTX + t0
                            dst, dres = KTg[:, koff:koff + N], 'KTg'
                        if not s.ctx:
                            stt('dve', TA[rq][:, 0:N], bank(bz, N), VEC[:, gcol:gcol + 1], CS[hs][:, 0, 0:N], ALU.mult, ALU.mult,
                                [f'ps{bz}', f'CS{hs}', VECr], [f'TA{rq}'])
                            stt('dve', TB[rq][:, 0:N], bank(bp, N), VEC[:, gcol + 1:gcol + 2], CS[hs][:, 1, 0:N], ALU.mult,
                                ALU.mult, [f'ps{bp}', f'CS{hs}', VECr], [f'TB{rq}'])
                        if deferred is not None:
                            deferred()

                        def deferred(rq=rq, bz=bz, gcol=gcol, dst=dst, dres=dres, s=s, N=N):
                            mm(bank(3, N), [(blkb[:], SQb[rq][:, 0:N])], [f'SQb{rq}', 'blkb'], ['ps3'])
                            ts('dve', Vt[:, 0:N], bank(3, N), 1.0 / 64, LN_EPS, ALU.mult, ALU.add, ['ps3'], ['Vt'])
                            act(Vt[:, 0:N], Vt[:, 0:N], AF.Sqrt, ['Vt'], ['Vt'])
                            recip(RS[:, 0:N], Vt[:, 0:N], ['Vt'], ['RS'])
                            if s.ctx:
                                stt('dve', dst, bank(bz, N), VEC[:, gcol:gcol + 1], RS[:, 0:N], ALU.mult, ALU.mult,
                                    [f'ps{bz}', 'RS', VECr], [dres])
                            else:
                                tt('pool', TA[rq][:, 0:N], TA[rq][:, 0:N], TB[rq][:, 0:N], ALU.add, [f'TA{rq}', f'TB{rq}'],
                                   [f'TA{rq}'])
                                tt('pool', dst, TA[rq][:, 0:N], RS[:, 0:N], ALU.mult, [f'TA{rq}', 'RS'], [dres])
                    if deferred is not None:
                        deferred()
                    if need_q:
                        dma(s.qgT[:, :, t0:t0 + N], QGs[:, :, 0:N], ['QGs'], [], 'QGs')
                    if STOP == f'Ar{l}':
                        P.barrier()
                        stop(f'Ar{l}')
                    if a_idx + 1 < len(a_tiles):
                        a_h(a_idx + 1)
                    if a_idx + 2 < len(a_tiles):
                        a_load(a_idx + 2)
                    for sub in range(N // 128):
                        b0 = 4 + 2 * (sub % 2)
                        hsl = H[hs]
                        mms([(bank(b0), hsl[:, k, sub * 128:(sub + 1) * 128], WIN[:, k, 1792:2304], k == 0, k == 7, None)
                             for k in range(8)] +
                            [(bank(b0 + 1, 256), hsl[:, k, sub * 128:(sub + 1) * 128], WIN[:, k, 2304:2560], k == 0, k == 7, None)
                             for k in range(8)],
                            [f'H{hs}'] + wres(1792, 2560), [f'ps{b0}', f'ps{b0 + 1}'])
                        rd = [f'ps{b0}', f'ps{b0 + 1}']
                        import os
                        SK = os.environ.get('SK', '')
                        if 'a' in SK:
                            continue
                        if need_q and 'p' not in SK:
                            tcopy('act', PINs[:, sub, :], bank(b0, 256), [f'ps{b0}'], ['PINs'])
                        kt = sub if s.ctx else 2 + t * 4 + sub
                        if 'v' in SK:
                            continue
                        if s.ctx and 'n' not in SK:
                            for h in range(6):
                                src = ps[:, b0 * 512 + 256 + h * 64:b0 * 512 + 320 + h * 64] if h < 4 else \
                                    ps[:, (b0 + 1) * 512 + (h - 4) * 64:(b0 + 1) * 512 + (h - 3) * 64]
                                par = h % 2
                                tcopy('act', VNC[:, sub * 3 + h // 2, par, par * 64:(par + 1) * 64], src, rd, ['VNC'])
                        elif not s.ctx:
                            tcopy('act', VNs[:, sub, 0:256], ps[:, b0 * 512 + 256:b0 * 512 + 512], [f'ps{b0}'], ['VNs'])
                            tcopy('dve', VNs[:, sub, 256:384], bank(b0 + 1, 128), [f'ps{b0 + 1}'], ['VNs'])
                        if 'g' in SK:
                            continue
                        tcopy('dve', VG[:, kt, 0, 0:64], ps[:, (b0 + 1) * 512 + 128:(b0 + 1) * 512 + 192], [f'ps{b0 + 1}'], ['VG'])
                        tcopy('dve', VG[:, kt, 1, 64:128], ps[:, (b0 + 1) * 512 + 192:(b0 + 1) * 512 + 256], [f'ps{b0 + 1}'], ['VG'])
                    ns = N // 128
                    if STOP == f'At{l}':
                        P.barrier()
                        stop(f'At{l}')
                    if need_q:
                        dma(s.pin[t0:t0 + N, :].rearrange("(j p) c -> p j c", p=128), PINs[:, 0:ns, :], ['PINs'], [], 'PINs')
                    if not s.ctx:
                        dma(s.vn[t0:t0 + N, :].rearrange("(j p) c -> p j c", p=128), VNs[:, 0:ns, :], ['VNs'], [], 'VNs')
                    if STOP == f'Ac{l}':
                        P.barrier()
                        stop(f'Ac{l}')
            P.barrier()
            ar.reset(m0)
            stop(f'A{l}')
            if last_layer:
                streams = [lat]

            m0 = ar.mark()
            PINL = [ar.alloc(f"PINL{i}", [128, 6, 256], BF16) for i in range(2)]
            PLT = ar.alloc("PLT", [128, 512], BF16)
            CATs = [ar.alloc(f"CATs{i}", [128, 3, 512], BF16) for i in range(2)]
            tcnt = 0
            d_tiles = [(s, t) for s in streams for t in range(s.L // s.T)]

            def d_load(idx):
                s, t = d_tiles[idx]
                NSUB = s.T // 128
                NS = s.L // 128
                j0 = t * NSUB
                sl = idx % 2
                jlo = max(j0 - 1, 0)
                jhi = min(j0 + NSUB + 1, NS)
                dma(PINL[sl][:, jlo - (j0 - 1):jhi - (j0 - 1), :],
                    s.pin[jlo * 128:jhi * 128, :].rearrange("(j p) c -> p j c", p=128), [], [f'PINL{sl}'], f'PINL{sl}')
            d_load(0)
            for d_idx, (s, t) in enumerate(d_tiles):
                N = s.T
                NSUB = N // 128
                NS = s.L // 128
                if True:
                    t0 = t * N
                    j0 = t * NSUB
                    sl = tcnt % 2
                    tcnt += 1
                    if d_idx + 1 < len(d_tiles):
                        d_load(d_idx + 1)
                    for i in range(2):
                        lst = []
                        for sub in range(NSUB):
                            j = j0 + sub
                            for gg in range(2):
                                g = 2 * i + gg
                                terms = []
                                if j > 0:
                                    terms.append((sub, g * 5 + 0))
                                terms.append((sub + 1, g * 5 + (3 if j == 0 else 4 if j == NS - 1 else 1)))
                                if j < NS - 1:
                                    terms.append((sub + 2, g * 5 + 2))
                                for ti, (pi, ai) in enumerate(terms):
                                    lst.append((ps[gg * 64:(gg + 1) * 64, i * 512 + sub * 128:i * 512 + (sub + 1) * 128],
                                                PINL[sl][:, pi, g * 64:(g + 1) * 64], ABb[:, ai, :],
                                                ti == 0, ti == len(terms) - 1, (0, gg * 64)))
                        mms(lst, [f'PINL{sl}', 'ABb'], [f'ps{i}'])
                        tcopy('dve', PLT[:, 0:N], bank(i, N), [f'ps{i}'], ['PLT'])
                        mms([(ps[gg * 64:(gg + 1) * 64, (2 + i + 2 * gg) * 512:(2 + i + 2 * gg) * 512 + N],
                              PWb[gg * 64:(gg + 1) * 64, i, :],
                              PLT[gg * 64:(gg + 1) * 64, 0:N], True, True, (gg * 64, gg * 64)) for gg in range(2)],
                            ['PLT', 'PWb'], [f'ps{2 + i}', f'ps{4 + i}'])
                        for gg in range(2):
                            bb = 2 + i + 2 * gg
                            ts('dve', CATs[sl][gg * 64:(gg + 1) * 64, i, 0:N], bank(bb, N, gg * 64, (gg + 1) * 64),
                               VEC[gg * 64:(gg + 1) * 64, 208 + i:209 + i], None, ALU.mult, None,
                               [f'ps{bb}', VECr], [f'CATs{sl}'])
                    dma(s.catT[:, 0:2, t0:t0 + N], CATs[sl][:, 0:2, 0:N], [f'CATs{sl}'], [], f'CATs{sl}')
            P.barrier()
            ar.reset(m0)

            def attn_phase(kind):
                m0 = ar.mark()
                Q = [ar.alloc(f"Q{i}", [128, 3, 512], BF16) for i in range(2)]
                PT = [ar.alloc(f"PT{i}", [128, 1024], BF16) for i in range(3)]
                Rr = ar.alloc("Rr", [128, 512], F32)
                CATs = [ar.alloc(f"CATa{i}", [128, 3, 512], BF16) for i in range(2)]
                if kind == 'na':
                    KNR = ar.alloc("KNR", [128, 3, RING * 128], BF16)
                    VNR = ar.alloc("VNR", [128, RING * 3, 2, 128], BF16)
                    BIAS = ar.alloc("BIAS", [128, 6, 64, 64], BF16)
                    BT = ar.alloc("BT", [64, 6, 15, 64], BF16)
                    memset('pool', VNR[:], 1.0, [f'VNR{r}' for r in range(RING)])
                    for h in range(6):
                        q = h % 3
                        stv = WST[q][0:64].rearrange("p a b -> p (a b)")[:, 0:960]
                        dma(stv, nab_d[l][:, h].rearrange("p a b -> p (a b)"), [], [f'WST{q}'], f'WST{q}')
                        tcopy('dve', BT[:, h].rearrange("p a b -> p (a b)"), stv, [f'WST{q}'], ['BT'])
                cur_sig = [None]
                loaded = [0]
                ocnt = 0
                scnt = 0
                pcnt = 0
                coff = 2 if kind == 'na' else 5
                tiles = [(s, t) for s in streams for t in range(s.L // s.T)]
                items = []

                def load_q(idx):
                    s, t = tiles[idx]
                    N = s.T
                    qs = idx % 2
                    dma(Q[qs][:, :, 0:N], (s.qnT if kind == 'na' else s.qgT)[:, :, t * N:(t + 1) * N], [], [f'Q{qs}'], f'Q{qs}')

                for idx, (s, t) in enumerate(tiles):
                    N = s.T
                    t0 = t * N
                    qs = idx % 2
                    pre = []
                    if idx == 0:
                        pre.append(lambda: load_q(0))
                    if idx + 1 < len(tiles):
                        pre.append((lambda idx=idx: lambda: load_q(idx + 1))())
                    lat_slots = []
                    if kind == 'na' and not s.ctx:
                        R0 = 8 * t

                        def rs(r):
                            return min(max(r - 4, 0), ROWS - 8)
                        kt_lo = rs(R0) // 2
                        kt_hi = (rs(R0 + 7) + 8 + 1) // 2
                        sig = []
                        for sl_, kt in enumerate(range(kt_lo, kt_hi)):
                            for a in range(2):
                                kr = 2 * kt + a
                                bs = [b for b in range(8) if rs(R0 + b) <= kr < rs(R0 + b) + 8]
                                if bs:
                                    assert bs == list(range(bs[0], bs[-1] + 1))
                                    i0_ = 7 - kr + R0 + bs[0]
                                    assert 0 <= i0_ and i0_ + len(bs) <= 15
                                    sig.append((sl_, a, bs[0], len(bs), i0_))
                        sig = tuple(sig)
                        if sig != cur_sig[0]:
                            cur_sig[0] = sig

                            def asm(sig=sig):
                                memset('pool', BIAS[:], NEG, ['BIAS'])
                                for (sl_, a, b0_, nb, i0_) in sig:
                                    dma(BIAS[a * 64:(a + 1) * 64, :, sl_ * 8 + b0_:sl_ * 8 + b0_ + nb, :],
                                        BT[0:64, :, i0_:i0_ + nb, :], ['BT'], ['BIAS'], 'BIAS')
                            pre.append(asm)

                        def ldk(k0=loaded[0], k1=kt_hi):
                            for kt in range(k0, k1):
                                rsl = kt % RING
                                dma(KNR[:, :, rsl * 128:(rsl + 1) * 128], lat.knT[:, :, kt * 128:(kt + 1) * 128],
                                    [], [f'KNR{rsl}'], f'KNR{rsl}')
                                vsrc = lat.vn[kt * 128:(kt + 1) * 128, :].rearrange("p (i a d) -> p i a d", i=3, a=2)
                                dma(VNR[:, rsl * 3:rsl * 3 + 3, 0, 0:64], vsrc[:, :, 0, :], [], [f'VNR{rsl}'], f'VNR{rsl}')
                                dma(VNR[:, rsl * 3:rsl * 3 + 3, 1, 64:128], vsrc[:, :, 1, :], [], [f'VNR{rsl}'], f'VNR{rsl}')
                        pre.append(ldk)
                        loaded[0] = max(loaded[0], kt_hi)
                        lat_slots = list(enumerate(range(kt_lo, kt_hi)))
                    if kind == 'na':
                        keys = [('l', sl_, kt) for sl_, kt in lat_slots] + [('c', 0, 0), ('c', 0, 1)]
                    else:
                        nk = 2 if s.ctx else NKT
                        keys = [('g', 0, kt) for kt in range(nk)]
                    for i in range(3):
                        ob = 4 + 2 * (ocnt % 2)
                        ocnt += 1
                        for ki, (kk, sl_, kt) in enumerate(keys):
                            sb = 2 * (scnt % 2)
                            scnt += 1
                            pp = pcnt % 3
                            pcnt += 1
                            if kk == 'l':
                                rsl = kt % RING
                                kA = KNR[0:64, i, rsl * 128:(rsl + 1) * 128]
                                kB = KNR[64:128, i, rsl * 128:(rsl + 1) * 128]
                                vA = VNR[:, rsl * 3 + i, 0, :]
                                vB = VNR[:, rsl * 3 + i, 1, :]
                                kres = [f'KNR{rsl}', f'VNR{rsl}']
                            elif kk == 'c':
                                kA = KNC[0:64, i, kt * 128:(kt + 1) * 128]
                                kB = KNC[64:128, i, kt * 128:(kt + 1) * 128]
                                vA = VNC[:, kt * 3 + i, 0, :]
                                vB = VNC[:, kt * 3 + i, 1, :]
                                kres = ['KNC', 'VNC']
                            else:
                                kA = KTg[0:64, kt * 128:(kt + 1) * 128]
                                kB = KTg[64:128, kt * 128:(kt + 1) * 128]
                                vA = VG[:, kt, 0, :]
                                vB = VG[:, kt, 1, :]
                                kres = ['KTg', 'VG']
                            hasb = kk == 'l'
                            lst = [(bank(sb, N), kA, Q[qs][0:64, i, 0:N], True, not hasb, None),
                                   (bank(sb + 1, N), kB, Q[qs][64:128, i, 0:N], True, not hasb, None)]
                            rdl = [f'Q{qs}'] + kres
                            if hasb:
                                for par in range(2):
                                    lst.append((bank(sb + par, N), identb[:],
                                                BIAS[:, 2 * i + par, sl_ * 8:(sl_ + 1) * 8, :].rearrange("p a b -> p (a b)"),
                                                False, True, None))
                                rdl += ['BIAS', 'identb']
                            first_of_tile = (i == 0 and ki == 0)

                            def qk(lst=lst, rdl=rdl, sb=sb, pre=pre, first_of_tile=first_of_tile):
                                if first_of_tile:
                                    for f_ in pre:
                                        f_()
                                mms(lst, rdl, [f'ps{sb}', f'ps{sb + 1}'])

                            def rest(sb=sb, pp=pp, N=N, ob=ob, vA=vA, vB=vB, kres=kres, ki=ki, nk_=len(keys), qs=qs, i=i,
                                     s=s, t0=t0):
                                sc_ = 1.0 if kind == 'na' else 0.125
                                if N == 512:
                                    act(PT[pp][:], ps[:, sb * 512:sb * 512 + 1024], AF.Exp, [f'ps{sb}', f'ps{sb + 1}'],
                                        [f'PT{pp}'], scale=sc_)
                                else:
                                    act(PT[pp][:].rearrange("p (a n) -> p a n", a=2)[:, :, 0:N],
                                        ps[:, sb * 512:sb * 512 + 1024].rearrange("p (a n) -> p a n", a=2)[:, :, 0:N],
                                        AF.Exp, [f'ps{sb}', f'ps{sb + 1}'], [f'PT{pp}'], scale=sc_)
                                mms([(bank(ob, N), vA, PT[pp][:, 0:N], ki == 0, ki == nk_ - 1, None),
                                     (bank(ob + 1, N), vB, PT[pp][:, 512:512 + N], ki == 0, ki == nk_ - 1, None)],
                                    [f'PT{pp}'] + kres, [f'ps{ob}', f'ps{ob + 1}'])
                                if ki == nk_ - 1:
                                    recip(Rr[0:64, 0:N], bank(ob, N, 64, 128), [f'ps{ob}'], ['Rr'])
                                    recip(Rr[64:128, 0:N], bank(ob + 1, N, 0, 64), [f'ps{ob + 1}'], ['Rr'])
                                    tt('dve', CATs[qs][0:64, i, 0:N], bank(ob, N, 0, 64), Rr[0:64, 0:N], ALU.mult,
                                       [f'ps{ob}', 'Rr'], [f'CATa{qs}'])
                                    tt('dve', CATs[qs][64:128, i, 0:N], bank(ob + 1, N, 64, 128), Rr[64:128, 0:N], ALU.mult,
                                       [f'ps{ob + 1}', 'Rr'], [f'CATa{qs}'])
                                    if i == 2:
                                        dma(s.catT[:, coff:coff + 3, t0:t0 + N], CATs[qs][:, :, 0:N], [f'CATa{qs}'], [],
                                            f'CATa{qs}')
                            items.append((qk, rest))
                for n_ in range(len(items) + 1):
                    if n_ < len(items):
                        items[n_][0]()
                    if n_ >= 1:
                        items[n_ - 1][1]()
                P.barrier()
                ar.reset(m0)
            stop(f'D{l}')
            attn_phase('na')
            stop(f'C{l}')
            attn_phase('gqa')
            stop(f'B{l}')
            ar.reset(mL)

            m0 = ar.mark()
            WOUT = ar.alloc("WOUT", [128, 8, D], BF16)
            CT = [ar.alloc(f"CT{i}", [128, 8, 512], BF16) for i in range(3)]
            XR = [ar.alloc(f"XR{i}", [128, 8, 512], F32) for i in range(3)]
            tmp = alloc_ln_tmp()
            woutv = wout_d[l].rearrange("(k p) n -> p k n", p=128)
            load_weight(lambda ci: WOUT[:, :, ci * 256:(ci + 1) * 256], lambda ci: woutv[:, :, ci * 256:(ci + 1) * 256],
                        4, 'WOUT')
            e_tiles = [(s, t) for s in streams for t in range(s.L // s.T)]

            def e1_load(idx):
                s, t = e_tiles[idx]
                N = s.T
                t0 = t * N
                sl = idx % 3
                dma(CT[sl][:, :, 0:N], s.catT[:, :, t0:t0 + N], [], [f'CT{sl}'], f'CT{sl}')
                dma(XR[sl][:, :, 0:N], s.xT[:, :, t0:t0 + N], [], [f'XR{sl}c{c}' for c in range(8)], f'XR{sl}')

            def e1_make(idx):
                s, t = e_tiles[idx]
                N = s.T
                sl = idx % 3

                def mmf(c, b):
                    mm(bank(b, N), [(WOUT[:, k, c * 128:(c + 1) * 128], CT[sl][:, k, 0:N]) for k in range(8)],
                       [f'CT{sl}'] + [f'WOUT{c // 2}'], [f'ps{b}'])
                return ln_make(s, XR[sl], f'XR{sl}', N, 2, 0, 8, mmf, tmp, 4 + 2 * (idx % 2))

            def e1_store(idx):
                s, t = e_tiles[idx]
                N = s.T
                sl = idx % 3
                dma(s.x1T[:, :, t * N:(t + 1) * N], XR[sl][:, :, 0:N], [f'XR{sl}c{c}' for c in range(8)], [], f'XRst{sl}')
            ln_pipeline(e_tiles, e1_make, e1_load, e1_store)
            P.barrier()
            ar.reset(m0)

            stop(f'E1{l}')
            m0 = ar.mark()
            WUP = ar.alloc("WUP", [128, 8, 2 * DFF], BF16)
            X1B = [ar.alloc(f"X1B{i}", [128, 8, 512], F32) for i in range(2)]
            H2 = [ar.alloc(f"H2{i}", [128, 8, 512], BF16) for i in range(2)]
            TAc = [ar.alloc(f"TAc{i}", [128, 512], F32) for i in range(2)]
            TGc = [ar.alloc(f"TGc{i}", [128, 512], F32) for i in range(2)]
            SGc = [ar.alloc(f"SGc{i}", [128, 512], F32) for i in range(2)]
            AOs = [ar.alloc(f"AOs{i}", [128, 512], BF16) for i in range(3)]
            wupv = wup_d[l].rearrange("(k p) n -> p k n", p=128)
            load_weight(lambda ci: WUP[:, :, ci * 256:(ci + 1) * 256], lambda ci: wupv[:, :, ci * 256:(ci + 1) * 256],
                        22, 'WUP', order=[x for p_ in range(11) for x in (p_, 11 + p_)])
            jcnt = 0
            f_tiles = []
            for s in streams:
                nf = (s.L + 509) // 510
                for ti in range(nf):
                    f_tiles.append((s, ti))

            def f_geom(idx):
                s, ti = f_tiles[idx]
                out_lo = 510 * ti
                out_hi = min(s.L, out_lo + 510)
                n_out = out_hi - out_lo
                N = n_out + 2
                tok_lo = max(out_lo - 1, 0)
                tok_hi = min(out_hi + 1, s.L)
                col_lo = tok_lo - (out_lo - 1)
                ncol = tok_hi - tok_lo
                return s, out_lo, out_hi, n_out, N, tok_lo, tok_hi, col_lo, ncol

            def f_load(idx):
                s, out_lo, out_hi, n_out, N, tok_lo, tok_hi, col_lo, ncol = f_geom(idx)
                xs = idx % 2
                dma(X1B[xs][:, :, col_lo:col_lo + ncol], s.x1T[:, :, tok_lo:tok_hi], [], [f'X1B{xs}'], f'X1B{xs}')

            def f_h2(idx):
                s, out_lo, out_hi, n_out, N, tok_lo, tok_hi, col_lo, ncol = f_geom(idx)
                xs = idx % 2
                for c in range(8):
                    act(H2[xs][:, c, col_lo:col_lo + ncol], X1B[xs][:, c, col_lo:col_lo + ncol], AF.Identity,
                        [f'X1B{xs}', MODr], [f'H2{xs}'], scale=MODap(4, c, s), bias=MODap(3, c, s))
                if col_lo == 1:
                    memset('pool', H2[xs][:, :, 0:1], 0.0, [f'H2{xs}'])
                if col_lo + ncol < N:
                    memset('pool', H2[xs][:, :, N - 1:N], 0.0, [f'H2{xs}'])
            f_load(0)
            f_h2(0)
            pend_m = m_steps(l + 1) if l + 1 < DEPTH else []
            for f_idx in range(len(f_tiles)):
                s, out_lo, out_hi, n_out, N, tok_lo, tok_hi, col_lo, ncol = f_geom(f_idx)
                xs = f_idx % 2
                if f_idx + 1 < len(f_tiles):
                    f_load(f_idx + 1)
                for j in range(NJ):
                    if j == 10 and f_idx + 1 < len(f_tiles):
                        f_h2(f_idx + 1)
                    if j in (3, 14) and pend_m and f_idx >= 1:
                        pend_m.pop(0)()
                    bs_ = 2 * (jcnt % 3)
                    cs_ = jcnt % 2
                    ao = jcnt % 3
                    jcnt += 1
                    halves = ((0, bs_, TAc[cs_], 'TAc'), (1, bs_ + 1, TGc[cs_], 'TGc'))
                    for half, bb, Tc, tn in halves:
                        col = half * DFF + j * 128
                        mm(bank(bb, N), [(WUP[:, k, col:col + 128], H2[xs][:, k, 0:N]) for k in range(8)],
                           [f'H2{xs}', f'WUP{col // 256}'], [f'ps{bb}'])
                    for half, bb, Tc, tn in halves:
                        vm = half * NJ + j
                        act(Tc[:, 0:n_out], ps[:, bb * 512 + 1:bb * 512 + 1 + n_out], AF.Identity, [f'ps{bb}', VECr],
                            [f'{tn}{cs_}'], scale=VEC[:, 32 + 44 + vm:32 + 44 + vm + 1], bias=VEC[:, 164 + vm:164 + vm + 1])
                    for tap, off in ((0, 0), (2, 2)):
                        for half, bb, Tc, tn in halves:
                            vm = half * NJ + j
                            stt('dve', Tc[:, 0:n_out], ps[:, bb * 512 + off:bb * 512 + off + n_out],
                                VEC[:, 32 + tap * 44 + vm:32 + tap * 44 + vm + 1], Tc[:, 0:n_out], ALU.mult, ALU.add,
                                [f'ps{bb}', f'{tn}{cs_}', VECr], [f'{tn}{cs_}'])
                    act(SGc[cs_][:, 0:n_out], TGc[cs_][:, 0:n_out], AF.Silu, [f'TGc{cs_}'], [f'SGc{cs_}'])
                    tt('pool', AOs[ao][:, 0:n_out], TAc[cs_][:, 0:n_out], SGc[cs_][:, 0:n_out], ALU.mult,
                       [f'TAc{cs_}', f'SGc{cs_}'], [f'AOs{ao}'])
                    dma(s.actT[:, j, out_lo:out_hi], AOs[ao][:, 0:n_out], [f'AOs{ao}'], [], f'AOs{ao}')
            while pend_m:
                pend_m.pop(0)()
            P.barrier()
            ar.reset(m0)

            stop(f'E2a{l}')
            m0 = ar.mark()
            WDN = ar.alloc("WDN", [128, NJ, D], BF16)
            AT = [ar.alloc(f"AT{i}", [128, NJ, 512], BF16) for i in range(2)]
            XR = [ar.alloc(f"XR2{i}", [128, 8, 512], F32) for i in range(3)]
            tmp = alloc_ln_tmp()
            wdnv = wdn_d[l].rearrange("(j p) n -> p j n", p=128)
            def wdn_j(ci):
                jg = ci % 3
                return 8 * jg, min(8 * jg + 8, NJ)
            load_weight(lambda ci: WDN[:, wdn_j(ci)[0]:wdn_j(ci)[1], (ci // 3) * 256:(ci // 3 + 1) * 256],
                        lambda ci: wdnv[:, wdn_j(ci)[0]:wdn_j(ci)[1], (ci // 3) * 256:(ci // 3 + 1) * 256], 12, 'WDN')
            e_tiles = [(s, t) for s in streams for t in range(s.L // s.T)]

            def e2b_load(idx):
                s, t = e_tiles[idx]
                N = s.T
                t0 = t * N
                sl = idx % 3
                al = idx % 2
                dma(AT[al][:, :, 0:N], s.actT[:, :, t0:t0 + N], [], [f'AT{al}'], f'AT{al}')
                dma(XR[sl][:, :, 0:N], s.x1T[:, :, t0:t0 + N], [], [f'XR{sl}c{c}' for c in range(8)], f'XR{sl}')

            def e2b_make(idx):
                s, t = e_tiles[idx]
                N = s.T
                sl = idx % 3
                al = idx % 2

                def mmf(c, b):
                    mm(bank(b, N), [(WDN[:, j, c * 128:(c + 1) * 128], AT[al][:, j, 0:N]) for j in range(NJ)],
                       [f'AT{al}'] + [f'WDN{(c // 2) * 3 + jg}' for jg in range(3)], [f'ps{b}'])
                return ln_make(s, XR[sl], f'XR{sl}', N, 5, 16, 24, mmf, tmp, 4 + 2 * (idx % 2))

            def e2b_store(idx):
                s, t = e_tiles[idx]
                N = s.T
                sl = idx % 3
                dma(s.xT[:, :, t * N:(t + 1) * N], XR[sl][:, :, 0:N], [f'XR{sl}c{c}' for c in range(8)], [], f'XRst{sl}')
            ln_pipeline(e_tiles, e2b_make, e2b_load, e2b_store)
            P.barrier()
            ar.reset(m0)
            stop(f'E2b{l}')

        XE = [ar.alloc(f"XE{i}", [128, 8, 512], F32) for i in range(2)]
        OT = [ar.alloc(f"OT{i}", [128, D], F32) for i in range(2)]
        ocnt = 0
        for t in range(L // 512):
            xs = t % 2
            dma(XE[xs][:], lat.xT[:, :, t * 512:(t + 1) * 512], [], [f'XE{xs}'], f'XE{xs}')
            for sub in range(4):
                osl = ocnt % 2
                b0 = 2 * (ocnt % 2)
                ocnt += 1
                P.add('pe', (lambda xs=xs, sub=sub, b0=b0: lambda e: [
                    e.transpose(ps[:, b0 * 512 + c * 128:b0 * 512 + (c + 1) * 128], XE[xs][:, c, sub * 128:(sub + 1) * 128], identf[:])
                    for c in range(8)][-1])(), reads=[f'XE{xs}', 'identf'], writes=[f'ps{b0}', f'ps{b0 + 1}'])
                tcopy('act' if sub % 2 == 0 else 'dve', OT[osl][:], ps[:, b0 * 512:b0 * 512 + 1024],
                      [f'ps{b0}', f'ps{b0 + 1}'], [f'OT{osl}'])
                r0 = t * 512 + sub * 128
                dma(out_d[r0:r0 + 128, :], OT[osl][:], [f'OT{osl}'], [], f'OT{osl}')
    try:
        _layers()
    except _Stop:
        P.emit(dummies, final_chans=())
        return nc
    P.emit(dummies, final_chans=('OT0', 'OT1'))
    return nc


_CACHE = {}


def kernel(x, c, ctx, c_ctx, w_mod, b_mod, w_in, pool_w, pool_scale, na_rpb, q_norm, k_norm,
           w_out, ln1_g, ln1_b, w_up, conv_w, conv_b, w_down, ln2_g, ln2_b):
    x = np.asarray(x, np.float32)
    B, L, _ = x.shape
    DEPTH = int(np.asarray(w_mod).shape[0])
    inp = dict(w_mod=w_mod, b_mod=b_mod, w_in=w_in, pool_w=pool_w, pool_scale=pool_scale, na_rpb=na_rpb,
               q_norm=q_norm, k_norm=k_norm, w_out=w_out, ln1_g=ln1_g, ln1_b=ln1_b, w_up=w_up, conv_w=conv_w,
               conv_b=conv_b, w_down=w_down, ln2_g=ln2_g, ln2_b=ln2_b)
    inp = {k: np.asarray(v, np.float32) for k, v in inp.items()}
    sh = _prep_shared(inp, L, DEPTH)
    key = (L, DEPTH)
    if key not in _CACHE:
        _CACHE[key] = build(L, DEPTH)
    nc = _CACHE[key]
    c = np.asarray(c, np.float32)
    ctx = np.asarray(ctx, np.float32)
    cc = _chunk(np.asarray(c_ctx, np.float32), 8)
    in_maps = []
    for b in range(B):
        cv = np.stack([_chunk(c[b], 8), cc], axis=2).reshape(128, 16)
        m = dict(sh)
        m['x'] = np.ascontiguousarray(x[b])
        m['ctx'] = np.ascontiguousarray(ctx[b])
        m['cvec'] = np.ascontiguousarray(cv)
        in_maps.append(m)
    res = run_bass_kernel_spmd(nc, in_maps, core_ids=list(range(B)))
    return np.stack([np.asarray(r['out'], np.float32) for r in res.results], 0)
```

```python
import numpy as np
from contextlib import ExitStack
import concourse.bass as bass
import concourse.mybir as mybir
from concourse.bass_utils import run_bass_kernel_spmd

F32 = mybir.dt.float32
BF16 = mybir.dt.bfloat16
AF = mybir.ActivationFunctionType
ALU = mybir.AluOpType

D = 1024
CTX = 256
DFF = 2816
NJ = 22
GRID_W = 64
NEG = -30000.0
LN_EPS = 1e-6
NVEC = 214
NWIN = 2560
RING = 12


class Prog:
    CE = ('pe', 'act', 'dve', 'pool')

    def __init__(self, nc):
        self.nc = nc
        self.segs = []
        self.chans = {}
        self.nbar = 0
        self._new_seg()

    def _new_seg(self):
        self.ops = []
        self.lastw = {}
        self.rd = {}
        self.chan_cnt = {}
        self.segs.append((self.ops, self.chan_cnt))

    def add(self, eng, fn, reads=(), writes=(), chan=None):
        i = len(self.ops)
        dma = eng == 'sp'
        deps = {}

        def dep(j, kind):
            o = self.ops[j]
            if not dma and not o['dma'] and o['eng'] == eng and kind != 'raw':
                return
            if dma and o['dma'] and kind == 'waw' and o['chan'] == chan:
                return
            deps[j] = True
        for r in reads:
            if r in self.lastw:
                dep(self.lastw[r], 'raw')
            if r.startswith('ps'):
                for j in self.rd.get(r, ()):
                    if self.ops[j]['eng'] != eng:
                        deps[j] = True
        for r in writes:
            if r in self.lastw:
                dep(self.lastw[r], 'waw')
            for j in self.rd.get(r, ()):
                dep(j, 'war')
        for r in reads:
            self.rd.setdefault(r, []).append(i)
        for r in writes:
            self.lastw[r] = i
            self.rd[r] = []
        op = dict(eng=eng, fn=fn, deps=list(deps), dma=dma, chan=chan, flag=False, ticket=None)
        if dma:
            assert chan is not None
            self.chans[chan] = True
            self.chan_cnt[chan] = self.chan_cnt.get(chan, 0) + 16
            op['ticket'] = self.chan_cnt[chan]
            op['flag'] = True
        for j in deps:
            self.ops[j]['flag'] = True
        self.ops.append(op)
        return i

    def barrier(self):
        self._new_seg()

    def emit(self, dummies, final_chans=()):
        nc = self.nc
        for ops, _ in self.segs:
            cnt = {e: 0 for e in self.CE}
            for o in ops:
                if not o['dma'] and o['flag']:
                    cnt[o['eng']] += 1
                    o['ticket'] = cnt[o['eng']]
                    assert cnt[o['eng']] < 30000
            for c, v in _.items():
                assert v < 30000, (c, v)
        with ExitStack() as st:
            sems = {e: st.enter_context(nc.semaphore('S_' + e)) for e in self.CE}
            bsem = {e: st.enter_context(nc.semaphore('B_' + e)) for e in self.CE}
            barc = st.enter_context(nc.semaphore('C_bar'))
            for c in self.chans:
                sems['c:' + c] = st.enter_context(nc.semaphore('C_' + c))
            block = st.enter_context(nc.Block())
            segs = self.segs
            nseg = len(segs)

            def run(engname):
                def body(e):
                    for si, (ops, chan_cnt) in enumerate(segs):
                        known = {}
                        for o in ops:
                            if o['eng'] != engname:
                                continue
                            need = {}
                            for j in o['deps']:
                                d = ops[j]
                                key = ('c:' + d['chan']) if d['dma'] else d['eng']
                                need[key] = max(need.get(key, 0), d['ticket'])
                            for key, v in need.items():
                                if known.get(key, 0) < v:
                                    e.wait_ge(sems[key], v)
                                    known[key] = v
                            ins = o['fn'](e)
                            if o['flag']:
                                if o['dma']:
                                    ins.then_inc(sems['c:' + o['chan']], 16)
                                else:
                                    ins.then_inc(sems[engname], 1)
                        last = si == nseg - 1
                        if engname == 'sp':
                            for c, v in chan_cnt.items():
                                if last and c not in final_chans:
                                    continue
                                e.wait_ge(sems['c:' + c], v)
                            if last:
                                continue
                            for ce in self.CE:
                                e.wait_ge(bsem[ce], 2 * si + 1)
                            for c in chan_cnt:
                                e.sem_clear(sems['c:' + c])
                            dummies['sp'](e).then_inc(barc, 16)
                            for ce in self.CE:
                                e.wait_ge(bsem[ce], 2 * si + 2)
                            dummies['sp'](e).then_inc(barc, 16)
                        else:
                            if last:
                                continue
                            if engname == 'pe':
                                for ce in ('act', 'dve', 'pool'):
                                    e.wait_ge(bsem[ce], 2 * si + 1)
                            dummies[engname](e).then_inc(bsem[engname], 1)
                            e.wait_ge(barc, 16 * (2 * si + 1))
                            e.sem_clear(sems[engname])
                            dummies[engname](e).then_inc(bsem[engname], 1)
                            e.wait_ge(barc, 16 * (2 * si + 2))
                return body
            block.tensor(run('pe'))
            block.scalar(run('act'))
            block.vector(run('dve'))
            block.gpsimd(run('pool'))
            block.sync(run('sp'))


class Arena:
    def __init__(self, nc, base=20480, limit=229344):
        self.nc = nc
        self.off = base
        self.limit = limit
        self.n = 0

    def alloc(self, name, shape, dtype):
        esz = 4 if dtype == F32 else 2
        nbytes = int(np.prod(shape[1:])) * esz
        nbytes = (nbytes + 63) // 64 * 64
        assert self.off + nbytes <= self.limit, (name, self.off, nbytes)
        self.n += 1
        t = self.nc.alloc_sbuf_tensor_at(f"{name}_{self.n}", list(shape), dtype, offset=self.off)
        self.off += nbytes
        return t

    def mark(self):
        return self.off

    def reset(self, m):
        self.off = m


def _partner_sign():
    partner = np.zeros(64, np.int64)
    sign = np.zeros(64, np.float32)
    for i in range(64):
        blk, r = divmod(i, 32)
        if r < 16:
            partner[i] = blk * 32 + r + 16
            sign[i] = -1.0
        else:
            partner[i] = blk * 32 + r - 16
            sign[i] = 1.0
    return partner, sign


def _rope_tables(L):
    partner, sign = _partner_sign()
    t = np.arange(L, dtype=np.int32)
    inv = (np.float32(10000.0) ** (-np.arange(0, 32, 2, dtype=np.float32) / np.float32(32))).astype(np.float32)
    ang_r = (t // GRID_W).astype(np.float32)[:, None] * inv
    ang_c = (t % GRID_W).astype(np.float32)[:, None] * inv
    cos = np.zeros((64, L), np.float32)
    sin = np.zeros((64, L), np.float32)
    for i in range(64):
        blk, r = divmod(i, 32)
        a = ang_r if blk == 0 else ang_c
        cos[i] = np.cos(a[:, r % 16])
        sin[i] = np.sin(a[:, r % 16]) * sign[i]
    return np.concatenate([cos, cos], 0), np.concatenate([sin, sin], 0)


def _band_tables():
    ab = np.zeros((128, 20, 128), np.float32)
    a = np.arange(128)[:, None]
    b = np.arange(128)[None, :]
    for g, w in enumerate((2, 4, 8, 16)):
        h = w // 2
        ab[:, g * 5 + 0, :] = np.where(a - 128 >= b - h, 1.0 / w, 0.0)
        cur = np.where((a >= b - h) & (a <= b + h - 1), 1.0 / w, 0.0)
        ab[:, g * 5 + 1, :] = cur - (a == b)
        ab[:, g * 5 + 2, :] = np.where(128 + a <= b + h - 1, 1.0 / w, 0.0)
        lo = np.maximum(b - h, 0)
        hi = b + h - 1
        cnt = (hi - lo + 1).astype(np.float32)
        ab[:, g * 5 + 3, :] = np.where((a >= lo) & (a <= hi), 1.0 / cnt, 0.0) - (a == b)
        lo = b - h
        hi = np.minimum(b + h - 1, 127)
        cnt = (hi - lo + 1).astype(np.float32)
        ab[:, g * 5 + 4, :] = np.where((a >= lo) & (a <= hi), 1.0 / cnt, 0.0) - (a == b)
    return ab


def _chunk(v, n):
    return np.ascontiguousarray(np.asarray(v, np.float32).reshape(n, 128).T)


def _prep_shared(inp, L, DEPTH):
    partner, _ = _partner_sign()
    a128 = np.arange(128)
    a64 = np.arange(64)
    cols = []
    for i in range(3):
        cols.append(256 + i * 128 + a128)
    for i in range(3):
        cols.append(640 + i * 128 + a128)
    for i in range(3):
        cols.append(np.concatenate([1408 + i * 64 + a64, 1408 + (i + 3) * 64 + a64]))
    cols.append(1792 + a128)
    for i in range(3):
        cols.append(np.concatenate([1408 + i * 64 + partner, 1408 + (i + 3) * 64 + partner]))
    cols.append(np.concatenate([1792 + partner, 1856 + partner]))
    cols.append(np.arange(0, 256))
    cols.append(1024 + np.arange(384))
    cols.append(1920 + a128)
    cols = np.concatenate(cols)
    assert cols.shape[0] == NWIN
    rows = [np.arange(0, 640)]
    for i in range(3):
        rows.append(640 + i * 64 + a64)
        rows.append(640 + (i + 3) * 64 + a64)
    rows = np.concatenate(rows)
    sh = {}
    sh['w_mod'] = np.ascontiguousarray(inp['w_mod'], np.float32)
    bm = np.stack([_chunk(inp['b_mod'][l], 48) for l in range(DEPTH)], 0)
    sh['bmod'] = np.ascontiguousarray(np.repeat(bm, 2, axis=2))
    sh['w_in'] = np.ascontiguousarray(np.asarray(inp['w_in'], np.float32)[:, :, cols])
    sh['w_out'] = np.ascontiguousarray(np.asarray(inp['w_out'], np.float32)[:, rows, :])
    sh['w_up'] = np.ascontiguousarray(inp['w_up'], np.float32)
    sh['w_down'] = np.ascontiguousarray(inp['w_down'], np.float32)
    sh['pw'] = np.ascontiguousarray(np.asarray(inp['pool_w'], np.float32).reshape(DEPTH, 2, 128, 64))
    vecs = np.zeros((DEPTH, 128, NVEC), np.float32)
    for l in range(DEPTH):
        vecs[l, :, 0:8] = _chunk(inp['ln1_g'][l], 8)
        vecs[l, :, 8:16] = _chunk(inp['ln1_b'][l], 8)
        vecs[l, :, 16:24] = _chunk(inp['ln2_g'][l], 8)
        vecs[l, :, 24:32] = _chunk(inp['ln2_b'][l], 8)
        for j in range(3):
            vecs[l, :, 32 + j * 44:32 + (j + 1) * 44] = _chunk(inp['conv_w'][l, j], 44)
        vecs[l, :, 164:208] = _chunk(inp['conv_b'][l], 44)
        vecs[l, :, 208:210] = _chunk(inp['pool_scale'][l], 2)
        qn = np.asarray(inp['q_norm'][l], np.float32)
        kn = np.asarray(inp['k_norm'][l], np.float32)
        vecs[l, :, 210] = np.concatenate([qn, qn])
        vecs[l, :, 211] = np.concatenate([qn[partner], qn[partner]])
        vecs[l, :, 212] = np.concatenate([kn, kn])
        vecs[l, :, 213] = np.concatenate([kn[partner], kn[partner]])
    sh['vecs'] = vecs
    rpb = np.asarray(inp['na_rpb'], np.float32)
    kc = np.arange(64)[:, None]
    qc = np.arange(64)[None, :]
    cs = np.clip(qc - 8, 0, 48)
    valid = (kc >= cs) & (kc < cs + 16)
    dc = np.clip(kc - qc + 15, 0, 30)
    nab = np.full((DEPTH, 64, 6, 15, 64), NEG, np.float32)
    for l in range(DEPTH):
        for h in range(6):
            for idx in range(15):
                g = rpb[l, h, 14 - idx][dc]
                nab[l, :, h, idx, :] = np.where(valid, g, np.float32(NEG))
    sh['nab'] = nab
    sh['ab'] = _band_tables()
    cos, sinp = _rope_tables(L)
    sh['cos'] = cos
    sh['sinp'] = sinp
    return sh


class _S:
    pass


DEBUG = False
STOP = None


class _Stop(Exception):
    pass


def build(L, DEPTH):
    assert L % 512 == 0
    ROWS = L // GRID_W
    alpha = float((2 * DEPTH) ** 0.25)
    eps_ln = float(LN_EPS / (alpha * alpha))
    nc = bass.Bass("TRN2", target_bir_lowering=False)

    def din(name, shape, dt=F32):
        return nc.dram_tensor(name, list(shape), dt, kind="ExternalInput").ap()

    def dscr(name, shape, dt):
        return nc.dram_tensor(name, list(shape), dt, kind=("ExternalOutput" if DEBUG else "Internal")).ap()

    def stop(tag):
        if STOP == tag:
            raise _Stop()
    x_d = din("x", [L, D])
    ctx_d = din("ctx", [CTX, D])
    cvec_d = din("cvec", [128, 16])
    wmod_d = din("w_mod", [DEPTH, D, 6 * D])
    bmod_d = din("bmod", [DEPTH, 128, 96])
    win_d = din("w_in", [DEPTH, D, NWIN])
    wout_d = din("w_out", [DEPTH, D, D])
    wup_d = din("w_up", [DEPTH, D, 2 * DFF])
    wdn_d = din("w_down", [DEPTH, DFF, D])
    pw_d = din("pw", [DEPTH, 2, 128, 64])
    vecs_d = din("vecs", [DEPTH, 128, NVEC])
    nab_d = din("nab", [DEPTH, 64, 6, 15, 64])
    ab_d = din("ab", [128, 20, 128])
    cos_d = din("cos", [128, L])
    sinp_d = din("sinp", [128, L])
    out_d = nc.dram_tensor("out", [L, D], F32, kind="ExternalOutput").ap()
    dum_d = dscr("dum", [128, 8], F32)

    lat = _S()
    cx = _S()
    for s, n, LL, T in ((lat, "l", L, 512), (cx, "c", CTX, 256)):
        s.L = LL
        s.T = T
        s.ctx = n == "c"
        s.n = n
        s.xT = dscr("xT" + n, [128, 8, LL], F32)
        s.x1T = dscr("x1T" + n, [128, 8, LL], F32)
        s.qnT = dscr("qnT" + n, [128, 3, LL], BF16)
        s.qgT = dscr("qgT" + n, [128, 3, LL], BF16)
        s.pin = dscr("pin" + n, [LL, 256], BF16)
        s.catT = dscr("catT" + n, [128, 8, LL], BF16)
        s.actT = dscr("actT" + n, [128, NJ, LL], BF16)
    lat.knT = dscr("knTl", [128, 3, L], BF16)
    lat.vn = dscr("vnl", [L, 384], BF16)

    P = Prog(nc)
    ar = Arena(nc)
    ps = nc.alloc_psum_tensor("ps", [128, 4096], F32)

    def bank(b, n=512, p0=0, p1=128):
        return ps[p0:p1, b * 512:b * 512 + n]

    identf = ar.alloc("identf", [128, 128], F32)
    identb = ar.alloc("identb", [128, 128], BF16)
    onesb = ar.alloc("onesb", [128, 128], BF16)
    blkb = ar.alloc("blkb", [128, 128], BF16)
    ABb = ar.alloc("ABb", [128, 20, 128], BF16)
    MODTs = [ar.alloc(f"MODT{i}", [128, 6, 8, 2], F32) for i in range(2)]
    VECs = [ar.alloc(f"VEC{i}", [128, NVEC], F32) for i in range(2)]
    cur_l = [0]
    ACT2 = ar.alloc("ACT2", [128, 8, 2], F32)
    PWb = ar.alloc("PWb", [128, 2, 64], BF16)
    BMp = ar.alloc("BMp", [128, 96], F32)
    PWs = ar.alloc("PWs", [128, 2, 64], F32)
    M2c = [ar.alloc(f"M2c{i}", [128, 256], F32) for i in range(2)]
    dumt = {e: ar.alloc("dum" + e, [128, 8], F32) for e in ('act', 'dve', 'pool', 'sp')}
    WST = [ar.alloc(f"WST{i}", [128, 8, 256], F32) for i in range(3)]
    NKT = 2 + L // 128
    base_mark = ar.mark()

    dummies = {
        'pe': lambda e: e.matmul(ps[:, 0:1], lhsT=identb[:, 0:128], rhs=identb[:, 0:1], start=True, stop=True),
        'act': lambda e: e.activation(out=dumt['act'][:, 0:1], in_=dumt['act'][:, 1:2], func=AF.Copy),
        'dve': lambda e: e.memset(dumt['dve'][:, 0:1], 0.0),
        'pool': lambda e: e.memset(dumt['pool'][:, 0:1], 0.0),
        'sp': lambda e: e.dma_start(out=dumt['sp'][:, :], in_=dum_d[:, :]),
    }

    def dma(out, in_, reads, writes, chan):
        P.add('sp', lambda e: e.dma_start(out=out, in_=in_), reads=reads, writes=writes, chan=chan)

    def mm(out, pairs, reads, writes, tps=None):
        def fn(e):
            n = len(pairs)
            ins = None
            for i, (lt, r) in enumerate(pairs):
                kw = {}
                if tps is not None:
                    kw['tile_position'] = tps[i]
                ins = e.matmul(out, lhsT=lt, rhs=r, start=(i == 0), stop=(i == n - 1), **kw)
            return ins
        P.add('pe', fn, reads=reads, writes=writes)

    def mms(lst, reads, writes):
        def fn(e):
            ins = None
            for (o, lt, r, s0, s1, tp) in lst:
                kw = {}
                if tp is not None:
                    kw['tile_position'] = tp
                ins = e.matmul(o, lhsT=lt, rhs=r, start=s0, stop=s1, **kw)
            return ins
        P.add('pe', fn, reads=reads, writes=writes)

    def act(out, in_, func, reads, writes, scale=None, bias=None):
        kw = {}
        if scale is not None:
            kw['scale'] = scale
        if bias is not None:
            kw['bias'] = bias
        P.add('act', lambda e: e.activation(out=out, in_=in_, func=func, **kw), reads=reads, writes=writes)

    def tcopy(eng, out, in_, reads, writes):
        if eng == 'act':
            act(out, in_, AF.Copy, reads, writes)
        else:
            P.add(eng, lambda e: e.tensor_copy(out=out, in_=in_), reads=reads, writes=writes)

    def tt(eng, out, in0, in1, op, reads, writes):
        P.add(eng, lambda e: e.tensor_tensor(out=out, in0=in0, in1=in1, op=op), reads=reads, writes=writes)

    def ts(eng, out, in0, s1, s2, op0, op1, reads, writes):
        if s2 is None:
            P.add(eng, lambda e: e.tensor_scalar(out=out, in0=in0, scalar1=s1, scalar2=None, op0=op0),
                  reads=reads, writes=writes)
        else:
            P.add(eng, lambda e: e.tensor_scalar(out=out, in0=in0, scalar1=s1, scalar2=s2, op0=op0, op1=op1),
                  reads=reads, writes=writes)

    def stt(eng, out, in0, sc, in1, op0, op1, reads, writes):
        P.add(eng, lambda e: e.scalar_tensor_tensor(out=out, in0=in0, scalar=sc, in1=in1, op0=op0, op1=op1),
              reads=reads, writes=writes)

    def recip(out, in_, reads, writes):
        P.add('dve', lambda e: e.reciprocal(out=out, in_=in_), reads=reads, writes=writes)

    def memset(eng, ap, v, writes):
        P.add(eng, lambda e: e.memset(ap, v), writes=writes)

    cast_rr = [0]

    def load_weight(dst_fn, src_fn, nchunks, wname, order=None):
        for ci in (order if order is not None else range(nchunks)):
            sl = cast_rr[0] % 3
            eng = ('dve', 'act')[cast_rr[0] % 2]
            cast_rr[0] += 1
            d = dst_fn(ci)
            s = src_fn(ci)
            shp = d.shape
            if len(shp) == 3:
                stv = WST[sl][:, 0:shp[1], 0:shp[2]]
            else:
                stv = WST[sl][:, 0, 0:shp[1]]
            stv = stv[0:shp[0]]
            dma(stv, s, [], [f'WST{sl}'], f'WST{sl}')
            tcopy(eng, d, stv, [f'WST{sl}'], [f'{wname}{ci}'])

    def MODap(part, c, s):
        return MODTs[cur_l[0] % 2][:, part, c, (1 if s.ctx else 0):(2 if s.ctx else 1)]

    memset('pool', identf[:], 0.0, ['identf'])
    P.add('pool', lambda e: e.affine_select(out=identf[:], in_=identf[:], pattern=[[-1, 128]],
                                            compare_op=ALU.not_equal, fill=1.0, base=0, channel_multiplier=1),
          reads=['identf'], writes=['identf'])
    tcopy('pool', identb[:], identf[:], ['identf'], ['identb'])
    memset('pool', onesb[:], 1.0, ['onesb'])
    memset('pool', blkb[:], 0.0, ['blkb'])
    memset('pool', blkb[0:64, 0:64], 1.0, ['blkb'])
    memset('pool', blkb[64:128, 64:128], 1.0, ['blkb'])
    for e_ in ('act', 'dve', 'pool'):
        memset('pool' if e_ == 'pool' else 'dve', dumt[e_][:], 0.0, ['dum' + e_])
    for q in range(3):
        n0, n1 = q * 7, min(20, q * 7 + 7)
        dma(WST[q][:, 0:n1 - n0, 0:128], ab_d[:, n0:n1, :], [], [f'WST{q}'], f'WST{q}')
        tcopy('dve', ABb[:, n0:n1, :], WST[q][:, 0:n1 - n0, 0:128], [f'WST{q}'], ['ABb'])
    dma(ACT2[:].rearrange("p a b -> p (a b)"), cvec_d[:, :], [], ['ACT2'], 'ACT2')
    act(ACT2[:], ACT2[:], AF.Silu, ['ACT2'], ['ACT2'])

    m0 = ar.mark()
    XIN = [ar.alloc(f"XIN{i}", [128, D], F32) for i in range(2)]
    XTS = [ar.alloc(f"XTS{i}", [128, 8, 512], F32) for i in range(2)]
    for s, src in ((cx, ctx_d), (lat, x_d)):
        for t in range(s.L // s.T):
            N = s.T
            xs = t % 2
            for sub in range(N // 128):
                j = t * (N // 128) + sub
                isl = j % 2
                dma(XIN[isl][:], src[j * 128:(j + 1) * 128, :], [], [f'XIN{isl}'], f'XIN{isl}')
                b0 = 2 * (j % 2)
                P.add('pe', (lambda isl=isl, b0=b0: lambda e: [e.transpose(ps[:, b0 * 512 + c * 128:b0 * 512 + (c + 1) * 128],
                                                                          XIN[isl][:, c * 128:(c + 1) * 128], identf[:])
                                                             for c in range(8)][-1])(),
                      reads=[f'XIN{isl}', 'identf'], writes=[f'ps{b0}', f'ps{b0 + 1}'])
                tcopy('act' if sub % 2 == 0 else 'dve', XTS[xs][:, :, sub * 128:(sub + 1) * 128],
                      ps[:, b0 * 512:b0 * 512 + 1024].rearrange("p (c n) -> p c n", c=8),
                      [f'ps{b0}', f'ps{b0 + 1}'], [f'XTS{xs}'])
            dma(s.xT[:, :, t * N:(t + 1) * N], XTS[xs][:, :, 0:N], [f'XTS{xs}'], [], f'XTS{xs}')
    P.barrier()
    ar.reset(m0)

    def ln_make(s, X, xres, N, gpart, gcol, bcol, mm_fn, tmp, sbk):
        YB, SQ, MEAN, MSQ, SD, RSTD, TT = tmp
        rr = (0, 1, 2)
        b6, b7 = sbk, sbk + 1

        def stats(c):
            mms([(bank(b6, N), onesb[:], YB[c % 2][:, 0:N], c == 0, c == 7, None),
                 (bank(b7, N), onesb[:], SQ[c % 2][:, 0:N], c == 0, c == 7, None)],
                [f'YB{c % 2}', f'SQ{c % 2}', 'onesb'], [f'ps{b6}', f'ps{b7}'])

        def head_chunk(c):
            b = rr[c % 3]
            mm_fn(c, b)
            if c >= 1:
                stats(c - 1)
            stt('dve', X[:, c, 0:N], bank(b, N), MODap(gpart, c, s), X[:, c, 0:N], ALU.mult, ALU.add,
                [f'ps{b}', xres + f'c{c}', f'MOD{cur_l[0] % 2}'], [xres + f'c{c}'])
            tcopy('act', YB[c % 2][:, 0:N], X[:, c, 0:N], [xres + f'c{c}'], [f'YB{c % 2}'])
            act(SQ[c % 2][:, 0:N], X[:, c, 0:N], AF.Square, [xres + f'c{c}'], [f'SQ{c % 2}'])

        def head_post():
            stats(7)

        def tail_pre():
            ts('dve', MEAN[:, 0:N], bank(b6, N), 1.0 / D, None, ALU.mult, None, [f'ps{b6}'], ['MEAN'])
            tt('pool', MSQ[:, 0:N], MEAN[:, 0:N], MEAN[:, 0:N], ALU.mult, ['MEAN'], ['MSQ'])
            stt('dve', SD[:, 0:N], bank(b7, N), 1.0 / D, MSQ[:, 0:N], ALU.mult, ALU.subtract, [f'ps{b7}', 'MSQ'], ['SD'])
            ts('dve', SD[:, 0:N], SD[:, 0:N], eps_ln, None, ALU.add, None, ['SD'], ['SD'])
            act(SD[:, 0:N], SD[:, 0:N], AF.Sqrt, ['SD'], ['SD'])
            recip(RSTD[:, 0:N], SD[:, 0:N], ['SD'], ['RSTD'])

        def tail_chunk(c):
            tt('pool', TT[c % 2][:, 0:N], X[:, c, 0:N], MEAN[:, 0:N], ALU.subtract,
               [xres + f'c{c}', 'MEAN'], [f'TT{c % 2}'])
            tt('dve', TT[c % 2][:, 0:N], TT[c % 2][:, 0:N], RSTD[:, 0:N], ALU.mult,
               [f'TT{c % 2}', 'RSTD'], [f'TT{c % 2}'])
            VEC_ = VECs[cur_l[0] % 2]
            act(X[:, c, 0:N], TT[c % 2][:, 0:N], AF.Identity, [f'TT{c % 2}', f'VEC{cur_l[0] % 2}'], [xres + f'c{c}'],
                scale=VEC_[:, gcol + c:gcol + c + 1], bias=VEC_[:, bcol + c:bcol + c + 1])
        return head_chunk, head_post, tail_pre, tail_chunk

    def ln_pipeline(tiles, make, load, store):
        n = len(tiles)
        load(0)
        prev = None
        for idx in range(n + 1):
            cur = make(idx) if idx < n else None
            if idx + 1 < n:
                load(idx + 1)
            if prev is not None:
                prev[2]()
            for c in range(8):
                if cur is not None:
                    cur[0](c)
                if prev is not None:
                    prev[3](c)
            if cur is not None:
                cur[1]()
            if prev is not None:
                store(idx - 1)
            prev = cur

    def alloc_ln_tmp():
        YB = [ar.alloc(f"YB{i}", [128, 512], BF16) for i in range(2)]
        SQ = [ar.alloc(f"SQ{i}", [128, 512], BF16) for i in range(2)]
        MEAN = ar.alloc("MEAN", [128, 512], F32)
        MSQ = ar.alloc("MSQ", [128, 512], F32)
        SD = ar.alloc("SD", [128, 512], F32)
        RSTD = ar.alloc("RSTD", [128, 512], F32)
        TT = [ar.alloc(f"TT{i}", [128, 512], F32) for i in range(2)]
        return (YB, SQ, MEAN, MSQ, SD, RSTD, TT)

    def _layers():
        stop('pro')
        for l in range(DEPTH):
            last_layer = l == DEPTH - 1
            streams = [cx, lat]
            mL = ar.mark()
            KTg = ar.alloc("KTg", [128, CTX + L], BF16)
            VG = ar.alloc("VG", [128, NKT, 2, 128], BF16)
            KNC = ar.alloc("KNC", [128, 3, CTX], BF16)
            VNC = ar.alloc("VNC", [128, 6, 2, 128], BF16)
            memset('pool', VG[:], 1.0, ['VG'])
            memset('pool', VNC[:], 1.0, ['VNC'])
            cur_l[0] = l
            VEC = VECs[l % 2]
            MODT = MODTs[l % 2]
            MODr = f'MOD{l % 2}'
            VECr = f'VEC{l % 2}'

            def m_steps(lm):
                MT = MODTs[lm % 2]
                mr = f'MOD{lm % 2}'
                steps = []

                def first():
                    dma(VECs[lm % 2][:], vecs_d[lm], [], [f'VEC{lm % 2}'], f'VEC{lm % 2}')
                    dma(BMp[:], bmod_d[lm], [], ['BMp'], 'BMp')
                steps.append(first)
                wmv = wmod_d[lm].rearrange("(k p) n -> p k n", p=128)

                def tr(q):
                    def fn(e):
                        ins = None
                        for jj in range(2):
                            j = q * 2 + jj
                            ins = e.transpose(ps[:, 3584 + 256 + j * 2:3584 + 256 + j * 2 + 2],
                                              M2c[q % 2][0:2, jj * 128:(jj + 1) * 128], identf[0:2, 0:2])
                        return ins
                    P.add('pe', fn, reads=[f'M2c{q % 2}', 'identf'], writes=['ps7t'])
                def cdma(q):
                    sl = q % 3
                    dma(WST[sl][:], wmv[:, :, q * 256:(q + 1) * 256], [], [f'WST{sl}'], f'WST{sl}')

                def first2():
                    cdma(0)
                    cdma(1)
                steps.append(first2)
                for q in range(24):
                    def chunk(q=q):
                        sl = q % 3
                        if q + 2 < 24:
                            cdma(q + 2)
                        mms([(ps[0:2, 3584:3584 + 256], ACT2[:, k, :], WST[sl][:, k, :], k == 0, k == 7, None)
                             for k in range(8)], [f'WST{sl}', 'ACT2'], ['ps7'])
                        if q >= 1:
                            tr(q - 1)
                        tcopy('dve', M2c[q % 2][0:2, :], ps[0:2, 3584:3584 + 256], ['ps7', 'ps7t'], [f'M2c{q % 2}'])
                    steps.append(chunk)

                def fin():
                    tr(23)
                    tt('dve', MT[:].rearrange("p a b c -> p (a b c)"), ps[:, 3584 + 256:3584 + 256 + 96], BMp[:], ALU.add,
                       ['ps7t', 'ps7', 'BMp'], [mr])
                    for part in (1, 4):
                        ts('dve', MT[:, part], MT[:, part], 1.0, None, ALU.add, None, [mr], [mr])
                    for part in (2, 5):
                        ts('dve', MT[:, part], MT[:, part], 1.0 / alpha, None, ALU.mult, None, [mr], [mr])
                    dma(PWs[:], pw_d[lm].rearrange("i p d -> p i d"), [], ['PWs'], 'PWs')
                    tcopy('dve', PWb[:], PWs[:], ['PWs'], ['PWb'])
                steps.append(fin)
                return steps
            if l == 0:
                for f_ in m_steps(0):
                    f_()
            if DEBUG and l == 0:
                modt_d = nc.dram_tensor("modt", [128, 96], F32, kind="ExternalOutput").ap()
                dma(modt_d[:, :], MODT[:].rearrange("p a b c -> p (a b c)"), [MODr], [], 'XT')
                act2_d = nc.dram_tensor("act2", [128, 16], F32, kind="ExternalOutput").ap()
                dma(act2_d[:, :], ACT2[:].rearrange("p a b -> p (a b)"), ['ACT2'], [], 'XT')
            if DEBUG:
                P.barrier()
                stop(f'M{l}')

            m0 = ar.mark()
            WIN = ar.alloc("WIN", [128, 8, NWIN], BF16)
            XT = ar.alloc("XT", [128, 8, 512], F32)
            H = [ar.alloc(f"H{i}", [128, 8, 512], BF16) for i in range(2)]
            QNs = ar.alloc("QNs", [128, 3, 512], BF16)
            KNs = ar.alloc("KNs", [128, 3, 512], BF16)
            QGs = ar.alloc("QGs", [128, 3, 512], BF16)
            PINs = ar.alloc("PINs", [128, 4, 256], BF16)
            VNs = ar.alloc("VNs", [128, 4, 384], BF16)
            CS = [ar.alloc(f"CS{i}", [128, 2, 512], F32) for i in range(2)]
            SQb = [ar.alloc(f"SQb{i}", [128, 512], BF16) for i in range(2)]
            Vt = ar.alloc("Vt", [128, 512], F32)
            RS = ar.alloc("RS", [128, 512], F32)
            TA = [ar.alloc(f"TA{i}", [128, 512], F32) for i in range(2)]
            TB = [ar.alloc(f"TB{i}", [128, 512], F32) for i in range(2)]
            winv = win_d[l].rearrange("(k p) n -> p k n", p=128)
            load_weight(lambda ci: WIN[:, :, ci * 256:(ci + 1) * 256], lambda ci: winv[:, :, ci * 256:(ci + 1) * 256],
                        NWIN // 256, 'WIN')

            def wres(c0, c1):
                return [f'WIN{i}' for i in range(c0 // 256, (c1 - 1) // 256 + 1)]
            if STOP == f'Aw{l}':
                P.barrier()
                stop(f'Aw{l}')
            fmrr = [0]
            tcnt = 0
            a_tiles = [(s, t) for s in streams for t in range(s.L // s.T)]

            def a_load(idx):
                s, t = a_tiles[idx]
                N = s.T
                t0 = t * N
                hs = idx % 2
                dma(XT[:, :, 0:N], s.xT[:, :, t0:t0 + N], [], ['XT'], 'XT')
                if not s.ctx:
                    dma(CS[hs][:, 0, :], cos_d[:, t0:t0 + N], [], [f'CS{hs}'], f'CS{hs}')
                    dma(CS[hs][:, 1, :], sinp_d[:, t0:t0 + N], [], [f'CS{hs}'], f'CS{hs}')
            def a_h(idx):
                s, t = a_tiles[idx]
                N = s.T
                hs = idx % 2
                for c in range(8):
                    act(H[hs][:, c, 0:N], XT[:, c, 0:N], AF.Identity, ['XT', MODr], [f'H{hs}'],
                        scale=MODap(1, c, s), bias=MODap(0, c, s))
            a_load(0)
            a_h(0)
            if len(a_tiles) > 1:
                a_load(1)
            rcnt = [0]
            for a_idx, (s, t) in enumerate(a_tiles):
                N = s.T
                if True:
                    t0 = t * N
                    hs = tcnt % 2
                    tcnt += 1

                    def fm(m, hs=hs, N=N):
                        b = fmrr[0] % 3
                        fmrr[0] += 1
                        mm(bank(b, N), [(WIN[:, k, m * 128:(m + 1) * 128], H[hs][:, k, 0:N]) for k in range(8)],
                           [f'H{hs}'] + wres(m * 128, (m + 1) * 128), [f'ps{b}'])
                        return b
                    need_q = not (s.ctx and last_layer)
                    if STOP == f'Ah{l}':
                        P.barrier()
                        stop(f'Ah{l}')
                    if need_q:
                        for m in range(3):
                            b = fm(m)
                            ts('dve', QNs[:, m, 0:N], bank(b, N), 0.125, None, ALU.mult, None, [f'ps{b}'], ['QNs'])
                        dma(s.qnT[:, :, t0:t0 + N], QNs[:, :, 0:N], ['QNs'], [], 'QNs')
                    for m in range(3, 6):
                        b = fm(m)
                        if s.ctx:
                            tcopy('act', KNC[:, m - 3, 0:N], bank(b, N), [f'ps{b}'], ['KNC'])
                        else:
                            tcopy('act', KNs[:, m - 3, 0:N], bank(b, N), [f'ps{b}'], ['KNs'])
                    if not s.ctx:
                        dma(s.knT[:, :, t0:t0 + N], KNs[:, :, 0:N], ['KNs'], [], 'KNs')
                    if STOP == f'Ak{l}':
                        P.barrier()
                        stop(f'Ak{l}')
                    rope_ms = ([6, 7, 8] if need_q else []) + [9]
                    deferred = None
                    for m in rope_ms:
                        bz = fm(m)
                        bp = None if s.ctx else fm(m + 4)
                        gcol = 210 if m < 9 else 212
                        rq = rcnt[0] % 2
                        rcnt[0] += 1
                        act(SQb[rq][:, 0:N], bank(bz, N), AF.Square, [f'ps{bz}'], [f'SQb{rq}'])
                        if m < 9:
                            dst, dres = QGs[:, m - 6, 0:N], 'QGs'
                        else:
                            koff = 0 if s.ctx else CTX + t0
                            dst, dres = KTg[:, koff:koff + N], 'KTg'
                        if not s.ctx:
                            stt('dve', TA[rq][:, 0:N], bank(bz, N), VEC[:, gcol:gcol + 1], CS[hs][:, 0, 0:N], ALU.mult, ALU.mult,
                                [f'ps{bz}', f'CS{hs}', VECr], [f'TA{rq}'])
                            stt('dve', TB[rq][:, 0:N], bank(bp, N), VEC[:, gcol + 1:gcol + 2], CS[hs][:, 1, 0:N], ALU.mult,
                                ALU.mult, [f'ps{bp}', f'CS{hs}', VECr], [f'TB{rq}'])
                        if deferred is not None:
                            deferred()

                        def deferred(rq=rq, bz=bz, gcol=gcol, dst=dst, dres=dres, s=s, N=N):
                            mm(bank(3, N), [(blkb[:], SQb[rq][:, 0:N])], [f'SQb{rq}', 'blkb'], ['ps3'])
                            ts('dve', Vt[:, 0:N], bank(3, N), 1.0 / 64, LN_EPS, ALU.mult, ALU.add, ['ps3'], ['Vt'])
                            act(Vt[:, 0:N], Vt[:, 0:N], AF.Sqrt, ['Vt'], ['Vt'])
                            recip(RS[:, 0:N], Vt[:, 0:N], ['Vt'], ['RS'])
                            if s.ctx:
                                stt('dve', dst, bank(bz, N), VEC[:, gcol:gcol + 1], RS[:, 0:N], ALU.mult, ALU.mult,
                                    [f'ps{bz}', 'RS', VECr], [dres])
                            else:
                                tt('pool', TA[rq][:, 0:N], TA[rq][:, 0:N], TB[rq][:, 0:N], ALU.add, [f'TA{rq}', f'TB{rq}'],
                                   [f'TA{rq}'])
                                tt('pool', dst, TA[rq][:, 0:N], RS[:, 0:N], ALU.mult, [f'TA{rq}', 'RS'], [dres])
                    if deferred is not None:
                        deferred()
                    if need_q:
                        dma(s.qgT[:, :, t0:t0 + N], QGs[:, :, 0:N], ['QGs'], [], 'QGs')
                    if STOP == f'Ar{l}':
                        P.barrier()
                        stop(f'Ar{l}')
                    if a_idx + 1 < len(a_tiles):
                        a_h(a_idx + 1)
                    if a_idx + 2 < len(a_tiles):
                        a_load(a_idx + 2)
                    for sub in range(N // 128):
                        b0 = 4 + 2 * (sub % 2)
                        hsl = H[hs]
                        mms([(bank(b0), hsl[:, k, sub * 128:(sub + 1) * 128], WIN[:, k, 1792:2304], k == 0, k == 7, None)
                             for k in range(8)] +
                            [(bank(b0 + 1, 256), hsl[:, k, sub * 128:(sub + 1) * 128], WIN[:, k, 2304:2560], k == 0, k == 7, None)
                             for k in range(8)],
                            [f'H{hs}'] + wres(1792, 2560), [f'ps{b0}', f'ps{b0 + 1}'])
                        rd = [f'ps{b0}', f'ps{b0 + 1}']
                        import os
                        SK = os.environ.get('SK', '')
                        if 'a' in SK:
                            continue
                        if need_q and 'p' not in SK:
                            tcopy('act', PINs[:, sub, :], bank(b0, 256), [f'ps{b0}'], ['PINs'])
                        kt = sub if s.ctx else 2 + t * 4 + sub
                        if 'v' in SK:
                            continue
                        if s.ctx and 'n' not in SK:
                            for h in range(6):
                                src = ps[:, b0 * 512 + 256 + h * 64:b0 * 512 + 320 + h * 64] if h < 4 else \
                                    ps[:, (b0 + 1) * 512 + (h - 4) * 64:(b0 + 1) * 512 + (h - 3) * 64]
                                par = h % 2
                                tcopy('act', VNC[:, sub * 3 + h // 2, par, par * 64:(par + 1) * 64], src, rd, ['VNC'])
                        elif not s.ctx:
                            tcopy('act', VNs[:, sub, 0:256], ps[:, b0 * 512 + 256:b0 * 512 + 512], [f'ps{b0}'], ['VNs'])
                            tcopy('dve', VNs[:, sub, 256:384], bank(b0 + 1, 128), [f'ps{b0 + 1}'], ['VNs'])
                        if 'g' in SK:
                            continue
                        tcopy('dve', VG[:, kt, 0, 0:64], ps[:, (b0 + 1) * 512 + 128:(b0 + 1) * 512 + 192], [f'ps{b0 + 1}'], ['VG'])
                        tcopy('dve', VG[:, kt, 1, 64:128], ps[:, (b0 + 1) * 512 + 192:(b0 + 1) * 512 + 256], [f'ps{b0 + 1}'], ['VG'])
                    ns = N // 128
                    if STOP == f'At{l}':
                        P.barrier()
                        stop(f'At{l}')
                    if need_q:
                        dma(s.pin[t0:t0 + N, :].rearrange("(j p) c -> p j c", p=128), PINs[:, 0:ns, :], ['PINs'], [], 'PINs')
                    if not s.ctx:
                        dma(s.vn[t0:t0 + N, :].rearrange("(j p) c -> p j c", p=128), VNs[:, 0:ns, :], ['VNs'], [], 'VNs')
                    if STOP == f'Ac{l}':
                        P.barrier()
                        stop(f'Ac{l}')
            P.barrier()
            ar.reset(m0)
            stop(f'A{l}')
            if last_layer:
                streams = [lat]

            m0 = ar.mark()
            PINL = [ar.alloc(f"PINL{i}", [128, 6, 256], BF16) for i in range(2)]
            PLT = ar.alloc("PLT", [128, 512], BF16)
            CATs = [ar.alloc(f"CATs{i}", [128, 3, 512], BF16) for i in range(2)]
            tcnt = 0
            d_tiles = [(s, t) for s in streams for t in range(s.L // s.T)]

            def d_load(idx):
                s, t = d_tiles[idx]
                NSUB = s.T // 128
                NS = s.L // 128
                j0 = t * NSUB
                sl = idx % 2
                jlo = max(j0 - 1, 0)
                jhi = min(j0 + NSUB + 1, NS)
                dma(PINL[sl][:, jlo - (j0 - 1):jhi - (j0 - 1), :],
                    s.pin[jlo * 128:jhi * 128, :].rearrange("(j p) c -> p j c", p=128), [], [f'PINL{sl}'], f'PINL{sl}')
            d_load(0)
            for d_idx, (s, t) in enumerate(d_tiles):
                N = s.T
                NSUB = N // 128
                NS = s.L // 128
                if True:
                    t0 = t * N
                    j0 = t * NSUB
                    sl = tcnt % 2
                    tcnt += 1
                    if d_idx + 1 < len(d_tiles):
                        d_load(d_idx + 1)
                    for i in range(2):
                        lst = []
                        for sub in range(NSUB):
                            j = j0 + sub
                            for gg in range(2):
                                g = 2 * i + gg
                                terms = []
                                if j > 0:
                                    terms.append((sub, g * 5 + 0))
                                terms.append((sub + 1, g * 5 + (3 if j == 0 else 4 if j == NS - 1 else 1)))
                                if j < NS - 1:
                                    terms.append((sub + 2, g * 5 + 2))
                                for ti, (pi, ai) in enumerate(terms):
                                    lst.append((ps[gg * 64:(gg + 1) * 64, i * 512 + sub * 128:i * 512 + (sub + 1) * 128],
                                                PINL[sl][:, pi, g * 64:(g + 1) * 64], ABb[:, ai, :],
                                                ti == 0, ti == len(terms) - 1, (0, gg * 64)))
                        mms(lst, [f'PINL{sl}', 'ABb'], [f'ps{i}'])
                        tcopy('dve', PLT[:, 0:N], bank(i, N), [f'ps{i}'], ['PLT'])
                        mms([(ps[gg * 64:(gg + 1) * 64, (2 + i + 2 * gg) * 512:(2 + i + 2 * gg) * 512 + N],
                              PWb[gg * 64:(gg + 1) * 64, i, :],
                              PLT[gg * 64:(gg + 1) * 64, 0:N], True, True, (gg * 64, gg * 64)) for gg in range(2)],
                            ['PLT', 'PWb'], [f'ps{2 + i}', f'ps{4 + i}'])
                        for gg in range(2):
                            bb = 2 + i + 2 * gg
                            ts('dve', CATs[sl][gg * 64:(gg + 1) * 64, i, 0:N], bank(bb, N, gg * 64, (gg + 1) * 64),
                               VEC[gg * 64:(gg + 1) * 64, 208 + i:209 + i], None, ALU.mult, None,
                               [f'ps{bb}', VECr], [f'CATs{sl}'])
                    dma(s.catT[:, 0:2, t0:t0 + N], CATs[sl][:, 0:2, 0:N], [f'CATs{sl}'], [], f'CATs{sl}')
            P.barrier()
            ar.reset(m0)

            def attn_phase(kind):
                m0 = ar.mark()
                Q = [ar.alloc(f"Q{i}", [128, 3, 512], BF16) for i in range(2)]
                PT = [ar.alloc(f"PT{i}", [128, 1024], BF16) for i in range(3)]
                Rr = ar.alloc("Rr", [128, 512], F32)
                CATs = [ar.alloc(f"CATa{i}", [128, 3, 512], BF16) for i in range(2)]
                if kind == 'na':
                    KNR = ar.alloc("KNR", [128, 3, RING * 128], BF16)
                    VNR = ar.alloc("VNR", [128, RING * 3, 2, 128], BF16)
                    BIAS = ar.alloc("BIAS", [128, 6, 64, 64], BF16)
                    BT = ar.alloc("BT", [64, 6, 15, 64], BF16)
                    memset('pool', VNR[:], 1.0, [f'VNR{r}' for r in range(RING)])
                    for h in range(6):
                        q = h % 3
                        stv = WST[q][0:64].rearrange("p a b -> p (a b)")[:, 0:960]
                        dma(stv, nab_d[l][:, h].rearrange("p a b -> p (a b)"), [], [f'WST{q}'], f'WST{q}')
                        tcopy('dve', BT[:, h].rearrange("p a b -> p (a b)"), stv, [f'WST{q}'], ['BT'])
                cur_sig = [None]
                loaded = [0]
                ocnt = 0
                scnt = 0
                pcnt = 0
                coff = 2 if kind == 'na' else 5
                tiles = [(s, t) for s in streams for t in range(s.L // s.T)]
                items = []

                def load_q(idx):
                    s, t = tiles[idx]
                    N = s.T
                    qs = idx % 2
                    dma(Q[qs][:, :, 0:N], (s.qnT if kind == 'na' else s.qgT)[:, :, t * N:(t + 1) * N], [], [f'Q{qs}'], f'Q{qs}')

                for idx, (s, t) in enumerate(tiles):
                    N = s.T
                    t0 = t * N
                    qs = idx % 2
                    pre = []
                    if idx == 0:
                        pre.append(lambda: load_q(0))
                    if idx + 1 < len(tiles):
                        pre.append((lambda idx=idx: lambda: load_q(idx + 1))())
                    lat_slots = []
                    if kind == 'na' and not s.ctx:
                        R0 = 8 * t

                        def rs(r):
                            return min(max(r - 4, 0), ROWS - 8)
                        kt_lo = rs(R0) // 2
                        kt_hi = (rs(R0 + 7) + 8 + 1) // 2
                        sig = []
                        for sl_, kt in enumerate(range(kt_lo, kt_hi)):
                            for a in range(2):
                                kr = 2 * kt + a
                                bs = [b for b in range(8) if rs(R0 + b) <= kr < rs(R0 + b) + 8]
                                if bs:
                                    assert bs == list(range(bs[0], bs[-1] + 1))
                                    i0_ = 7 - kr + R0 + bs[0]
                                    assert 0 <= i0_ and i0_ + len(bs) <= 15
                                    sig.append((sl_, a, bs[0], len(bs), i0_))
                        sig = tuple(sig)
                        if sig != cur_sig[0]:
                            cur_sig[0] = sig

                            def asm(sig=sig):
                                memset('pool', BIAS[:], NEG, ['BIAS'])
                                for (sl_, a, b0_, nb, i0_) in sig:
                                    dma(BIAS[a * 64:(a + 1) * 64, :, sl_ * 8 + b0_:sl_ * 8 + b0_ + nb, :],
                                        BT[0:64, :, i0_:i0_ + nb, :], ['BT'], ['BIAS'], 'BIAS')
                            pre.append(asm)

                        def ldk(k0=loaded[0], k1=kt_hi):
                            for kt in range(k0, k1):
                                rsl = kt % RING
                                dma(KNR[:, :, rsl * 128:(rsl + 1) * 128], lat.knT[:, :, kt * 128:(kt + 1) * 128],
                                    [], [f'KNR{rsl}'], f'KNR{rsl}')
                                vsrc = lat.vn[kt * 128:(kt + 1) * 128, :].rearrange("p (i a d) -> p i a d", i=3, a=2)
                                dma(VNR[:, rsl * 3:rsl * 3 + 3, 0, 0:64], vsrc[:, :, 0, :], [], [f'VNR{rsl}'], f'VNR{rsl}')
                                dma(VNR[:, rsl * 3:rsl * 3 + 3, 1, 64:128], vsrc[:, :, 1, :], [], [f'VNR{rsl}'], f'VNR{rsl}')
                        pre.append(ldk)
                        loaded[0] = max(loaded[0], kt_hi)
                        lat_slots = list(enumerate(range(kt_lo, kt_hi)))
                    if kind == 'na':
                        keys = [('l', sl_, kt) for sl_, kt in lat_slots] + [('c', 0, 0), ('c', 0, 1)]
                    else:
                        nk = 2 if s.ctx else NKT
                        keys = [('g', 0, kt) for kt in range(nk)]
                    for i in range(3):
                        ob = 4 + 2 * (ocnt % 2)
                        ocnt += 1
                        for ki, (kk, sl_, kt) in enumerate(keys):
                            sb = 2 * (scnt % 2)
                            scnt += 1
                            pp = pcnt % 3
                            pcnt += 1
                            if kk == 'l':
                                rsl = kt % RING
                                kA = KNR[0:64, i, rsl * 128:(rsl + 1) * 128]
                                kB = KNR[64:128, i, rsl * 128:(rsl + 1) * 128]
                                vA = VNR[:, rsl * 3 + i, 0, :]
                                vB = VNR[:, rsl * 3 + i, 1, :]
                                kres = [f'KNR{rsl}', f'VNR{rsl}']
                            elif kk == 'c':
                                kA = KNC[0:64, i, kt * 128:(kt + 1) * 128]
                                kB = KNC[64:128, i, kt * 128:(kt + 1) * 128]
                                vA = VNC[:, kt * 3 + i, 0, :]
                                vB = VNC[:, kt * 3 + i, 1, :]
                                kres = ['KNC', 'VNC']
                            else:
                                kA = KTg[0:64, kt * 128:(kt + 1) * 128]
                                kB = KTg[64:128, kt * 128:(kt + 1) * 128]
                                vA = VG[:, kt, 0, :]
                                vB = VG[:, kt, 1, :]
                                kres = ['KTg', 'VG']
                            hasb = kk == 'l'
                            lst = [(bank(sb, N), kA, Q[qs][0:64, i, 0:N], True, not hasb, None),
                                   (bank(sb + 1, N), kB, Q[qs][64:128, i, 0:N], True, not hasb, None)]
                            rdl = [f'Q{qs}'] + kres
                            if hasb:
                                for par in range(2):
                                    lst.append((bank(sb + par, N), identb[:],
                                                BIAS[:, 2 * i + par, sl_ * 8:(sl_ + 1) * 8, :].rearrange("p a b -> p (a b)"),
                                                False, True, None))
                                rdl += ['BIAS', 'identb']
                            first_of_tile = (i == 0 and ki == 0)

                            def qk(lst=lst, rdl=rdl, sb=sb, pre=pre, first_of_tile=first_of_tile):
                                if first_of_tile:
                                    for f_ in pre:
                                        f_()
                                mms(lst, rdl, [f'ps{sb}', f'ps{sb + 1}'])

                            def rest(sb=sb, pp=pp, N=N, ob=ob, vA=vA, vB=vB, kres=kres, ki=ki, nk_=len(keys), qs=qs, i=i,
                                     s=s, t0=t0):
                                sc_ = 1.0 if kind == 'na' else 0.125
                                if N == 512:
                                    act(PT[pp][:], ps[:, sb * 512:sb * 512 + 1024], AF.Exp, [f'ps{sb}', f'ps{sb + 1}'],
                                        [f'PT{pp}'], scale=sc_)
                                else:
                                    act(PT[pp][:].rearrange("p (a n) -> p a n", a=2)[:, :, 0:N],
                                        ps[:, sb * 512:sb * 512 + 1024].rearrange("p (a n) -> p a n", a=2)[:, :, 0:N],
                                        AF.Exp, [f'ps{sb}', f'ps{sb + 1}'], [f'PT{pp}'], scale=sc_)
                                mms([(bank(ob, N), vA, PT[pp][:, 0:N], ki == 0, ki == nk_ - 1, None),
                                     (bank(ob + 1, N), vB, PT[pp][:, 512:512 + N], ki == 0, ki == nk_ - 1, None)],
                                    [f'PT{pp}'] + kres, [f'ps{ob}', f'ps{ob + 1}'])
                                if ki == nk_ - 1:
                                    recip(Rr[0:64, 0:N], bank(ob, N, 64, 128), [f'ps{ob}'], ['Rr'])
                                    recip(Rr[64:128, 0:N], bank(ob + 1, N, 0, 64), [f'ps{ob + 1}'], ['Rr'])
                                    tt('dve', CATs[qs][0:64, i, 0:N], bank(ob, N, 0, 64), Rr[0:64, 0:N], ALU.mult,
                                       [f'ps{ob}', 'Rr'], [f'CATa{qs}'])
                                    tt('dve', CATs[qs][64:128, i, 0:N], bank(ob + 1, N, 64, 128), Rr[64:128, 0:N], ALU.mult,
                                       [f'ps{ob + 1}', 'Rr'], [f'CATa{qs}'])
                                    if i == 2:
                                        dma(s.catT[:, coff:coff + 3, t0:t0 + N], CATs[qs][:, :, 0:N], [f'CATa{qs}'], [],
                                            f'CATa{qs}')
                            items.append((qk, rest))
                for n_ in range(len(items) + 1):
                    if n_ < len(items):
                        items[n_][0]()
                    if n_ >= 1:
                        items[n_ - 1][1]()
                P.barrier()
                ar.reset(m0)
            stop(f'D{l}')
            attn_phase('na')
            stop(f'C{l}')
            attn_phase('gqa')
            stop(f'B{l}')
            ar.reset(mL)

            m0 = ar.mark()
            WOUT = ar.alloc("WOUT", [128, 8, D], BF16)
            CT = [ar.alloc(f"CT{i}", [128, 8, 512], BF16) for i in range(3)]
            XR = [ar.alloc(f"XR{i}", [128, 8, 512], F32) for i in range(3)]
            tmp = alloc_ln_tmp()
            woutv = wout_d[l].rearrange("(k p) n -> p k n", p=128)
            load_weight(lambda ci: WOUT[:, :, ci * 256:(ci + 1) * 256], lambda ci: woutv[:, :, ci * 256:(ci + 1) * 256],
                        4, 'WOUT')
            e_tiles = [(s, t) for s in streams for t in range(s.L // s.T)]

            def e1_load(idx):
                s, t = e_tiles[idx]
                N = s.T
                t0 = t * N
                sl = idx % 3
                dma(CT[sl][:, :, 0:N], s.catT[:, :, t0:t0 + N], [], [f'CT{sl}'], f'CT{sl}')
                dma(XR[sl][:, :, 0:N], s.xT[:, :, t0:t0 + N], [], [f'XR{sl}c{c}' for c in range(8)], f'XR{sl}')

            def e1_make(idx):
                s, t = e_tiles[idx]
                N = s.T
                sl = idx % 3

                def mmf(c, b):
                    mm(bank(b, N), [(WOUT[:, k, c * 128:(c + 1) * 128], CT[sl][:, k, 0:N]) for k in range(8)],
                       [f'CT{sl}'] + [f'WOUT{c // 2}'], [f'ps{b}'])
                return ln_make(s, XR[sl], f'XR{sl}', N, 2, 0, 8, mmf, tmp, 4 + 2 * (idx % 2))

            def e1_store(idx):
                s, t = e_tiles[idx]
                N = s.T
                sl = idx % 3
                dma(s.x1T[:, :, t * N:(t + 1) * N], XR[sl][:, :, 0:N], [f'XR{sl}c{c}' for c in range(8)], [], f'XRst{sl}')
            ln_pipeline(e_tiles, e1_make, e1_load, e1_store)
            P.barrier()
            ar.reset(m0)

            stop(f'E1{l}')
            m0 = ar.mark()
            WUP = ar.alloc("WUP", [128, 8, 2 * DFF], BF16)
            X1B = [ar.alloc(f"X1B{i}", [128, 8, 512], F32) for i in range(2)]
            H2 = [ar.alloc(f"H2{i}", [128, 8, 512], BF16) for i in range(2)]
            TAc = [ar.alloc(f"TAc{i}", [128, 512], F32) for i in range(2)]
            TGc = [ar.alloc(f"TGc{i}", [128, 512], F32) for i in range(2)]
            SGc = [ar.alloc(f"SGc{i}", [128, 512], F32) for i in range(2)]
            AOs = [ar.alloc(f"AOs{i}", [128, 512], BF16) for i in range(3)]
            wupv = wup_d[l].rearrange("(k p) n -> p k n", p=128)
            load_weight(lambda ci: WUP[:, :, ci * 256:(ci + 1) * 256], lambda ci: wupv[:, :, ci * 256:(ci + 1) * 256],
                        22, 'WUP', order=[x for p_ in range(11) for x in (p_, 11 + p_)])
            jcnt = 0
            f_tiles = []
            for s in streams:
                nf = (s.L + 509) // 510
                for ti in range(nf):
                    f_tiles.append((s, ti))

            def f_geom(idx):
                s, ti = f_tiles[idx]
                out_lo = 510 * ti
                out_hi = min(s.L, out_lo + 510)
                n_out = out_hi - out_lo
                N = n_out + 2
                tok_lo = max(out_lo - 1, 0)
                tok_hi = min(out_hi + 1, s.L)
                col_lo = tok_lo - (out_lo - 1)
                ncol = tok_hi - tok_lo
                return s, out_lo, out_hi, n_out, N, tok_lo, tok_hi, col_lo, ncol

            def f_load(idx):
                s, out_lo, out_hi, n_out, N, tok_lo, tok_hi, col_lo, ncol = f_geom(idx)
                xs = idx % 2
                dma(X1B[xs][:, :, col_lo:col_lo + ncol], s.x1T[:, :, tok_lo:tok_hi], [], [f'X1B{xs}'], f'X1B{xs}')

            def f_h2(idx):
                s, out_lo, out_hi, n_out, N, tok_lo, tok_hi, col_lo, ncol = f_geom(idx)
                xs = idx % 2
                for c in range(8):
                    act(H2[xs][:, c, col_lo:col_lo + ncol], X1B[xs][:, c, col_lo:col_lo + ncol], AF.Identity,
                        [f'X1B{xs}', MODr], [f'H2{xs}'], scale=MODap(4, c, s), bias=MODap(3, c, s))
                if col_lo == 1:
                    memset('pool', H2[xs][:, :, 0:1], 0.0, [f'H2{xs}'])
                if col_lo + ncol < N:
                    memset('pool', H2[xs][:, :, N - 1:N], 0.0, [f'H2{xs}'])
            f_load(0)
            f_h2(0)
            pend_m = m_steps(l + 1) if l + 1 < DEPTH else []
            for f_idx in range(len(f_tiles)):
                s, out_lo, out_hi, n_out, N, tok_lo, tok_hi, col_lo, ncol = f_geom(f_idx)
                xs = f_idx % 2
                if f_idx + 1 < len(f_tiles):
                    f_load(f_idx + 1)
                for j in range(NJ):
                    if j == 10 and f_idx + 1 < len(f_tiles):
                        f_h2(f_idx + 1)
                    if j in (3, 14) and pend_m and f_idx >= 1:
                        pend_m.pop(0)()
                    bs_ = 2 * (jcnt % 3)
                    cs_ = jcnt % 2
                    ao = jcnt % 3
                    jcnt += 1
                    halves = ((0, bs_, TAc[cs_], 'TAc'), (1, bs_ + 1, TGc[cs_], 'TGc'))
                    for half, bb, Tc, tn in halves:
                        col = half * DFF + j * 128
                        mm(bank(bb, N), [(WUP[:, k, col:col + 128], H2[xs][:, k, 0:N]) for k in range(8)],
                           [f'H2{xs}', f'WUP{col // 256}'], [f'ps{bb}'])
                    for half, bb, Tc, tn in halves:
                        vm = half * NJ + j
                        act(Tc[:, 0:n_out], ps[:, bb * 512 + 1:bb * 512 + 1 + n_out], AF.Identity, [f'ps{bb}', VECr],
                            [f'{tn}{cs_}'], scale=VEC[:, 32 + 44 + vm:32 + 44 + vm + 1], bias=VEC[:, 164 + vm:164 + vm + 1])
                    for tap, off in ((0, 0), (2, 2)):
                        for half, bb, Tc, tn in halves:
                            vm = half * NJ + j
                            stt('dve', Tc[:, 0:n_out], ps[:, bb * 512 + off:bb * 512 + off + n_out],
                                VEC[:, 32 + tap * 44 + vm:32 + tap * 44 + vm + 1], Tc[:, 0:n_out], ALU.mult, ALU.add,
                                [f'ps{bb}', f'{tn}{cs_}', VECr], [f'{tn}{cs_}'])
                    act(SGc[cs_][:, 0:n_out], TGc[cs_][:, 0:n_out], AF.Silu, [f'TGc{cs_}'], [f'SGc{cs_}'])
                    tt('pool', AOs[ao][:, 0:n_out], TAc[cs_][:, 0:n_out], SGc[cs_][:, 0:n_out], ALU.mult,
                       [f'TAc{cs_}', f'SGc{cs_}'], [f'AOs{ao}'])
                    dma(s.actT[:, j, out_lo:out_hi], AOs[ao][:, 0:n_out], [f'AOs{ao}'], [], f'AOs{ao}')
            while pend_m:
                pend_m.pop(0)()
            P.barrier()
            ar.reset(m0)

            stop(f'E2a{l}')
            m0 = ar.mark()
            WDN = ar.alloc("WDN", [128, NJ, D], BF16)
            AT = [ar.alloc(f"AT{i}", [128, NJ, 512], BF16) for i in range(2)]
            XR = [ar.alloc(f"XR2{i}", [128, 8, 512], F32) for i in range(3)]
            tmp = alloc_ln_tmp()
            wdnv = wdn_d[l].rearrange("(j p) n -> p j n", p=128)
            load_weight(lambda ci: WDN[:, 2 * (ci // 4):2 * (ci // 4) + 2, (ci % 4) * 256:(ci % 4 + 1) * 256],
                        lambda ci: wdnv[:, 2 * (ci // 4):2 * (ci // 4) + 2, (ci % 4) * 256:(ci % 4 + 1) * 256], 44, 'WDN',
                        order=[jp * 4 + cq for cq in range(4) for jp in range(11)])
            e_tiles = [(s, t) for s in streams for t in range(s.L // s.T)]

            def e2b_load(idx):
                s, t = e_tiles[idx]
                N = s.T
                t0 = t * N
                sl = idx % 3
                al = idx % 2
                dma(AT[al][:, :, 0:N], s.actT[:, :, t0:t0 + N], [], [f'AT{al}'], f'AT{al}')
                dma(XR[sl][:, :, 0:N], s.x1T[:, :, t0:t0 + N], [], [f'XR{sl}c{c}' for c in range(8)], f'XR{sl}')

            def e2b_make(idx):
                s, t = e_tiles[idx]
                N = s.T
                sl = idx % 3
                al = idx % 2

                def mmf(c, b):
                    mm(bank(b, N), [(WDN[:, j, c * 128:(c + 1) * 128], AT[al][:, j, 0:N]) for j in range(NJ)],
                       [f'AT{al}'] + [f'WDN{jp * 4 + c // 2}' for jp in range(11)], [f'ps{b}'])
                return ln_make(s, XR[sl], f'XR{sl}', N, 5, 16, 24, mmf, tmp, 4 + 2 * (idx % 2))

            def e2b_store(idx):
                s, t = e_tiles[idx]
                N = s.T
                sl = idx % 3
                dma(s.xT[:, :, t * N:(t + 1) * N], XR[sl][:, :, 0:N], [f'XR{sl}c{c}' for c in range(8)], [], f'XRst{sl}')
            ln_pipeline(e_tiles, e2b_make, e2b_load, e2b_store)
            P.barrier()
            ar.reset(m0)
            stop(f'E2b{l}')

        XE = [ar.alloc(f"XE{i}", [128, 8, 512], F32) for i in range(2)]
        OT = [ar.alloc(f"OT{i}", [128, D], F32) for i in range(2)]
        ocnt = 0
        for t in range(L // 512):
            xs = t % 2
            dma(XE[xs][:], lat.xT[:, :, t * 512:(t + 1) * 512], [], [f'XE{xs}'], f'XE{xs}')
            for sub in range(4):
                osl = ocnt % 2
                b0 = 2 * (ocnt % 2)
                ocnt += 1
                P.add('pe', (lambda xs=xs, sub=sub, b0=b0: lambda e: [
                    e.transpose(ps[:, b0 * 512 + c * 128:b0 * 512 + (c + 1) * 128], XE[xs][:, c, sub * 128:(sub + 1) * 128], identf[:])
                    for c in range(8)][-1])(), reads=[f'XE{xs}', 'identf'], writes=[f'ps{b0}', f'ps{b0 + 1}'])
                tcopy('act' if sub % 2 == 0 else 'dve', OT[osl][:], ps[:, b0 * 512:b0 * 512 + 1024],
                      [f'ps{b0}', f'ps{b0 + 1}'], [f'OT{osl}'])
                r0 = t * 512 + sub * 128
                dma(out_d[r0:r0 + 128, :], OT[osl][:], [f'OT{osl}'], [], f'OT{osl}')
    try:
        _layers()
    except _Stop:
        P.emit(dummies, final_chans=())
        return nc
    P.emit(dummies, final_chans=('OT0', 'OT1'))
    return nc


_CACHE = {}


def kernel(x, c, ctx, c_ctx, w_mod, b_mod, w_in, pool_w, pool_scale, na_rpb, q_norm, k_norm,
           w_out, ln1_g, ln1_b, w_up, conv_w, conv_b, w_down, ln2_g, ln2_b):
    x = np.asarray(x, np.float32)
    B, L, _ = x.shape
    DEPTH = int(np.asarray(w_mod).shape[0])
    inp = dict(w_mod=w_mod, b_mod=b_mod, w_in=w_in, pool_w=pool_w, pool_scale=pool_scale, na_rpb=na_rpb,
               q_norm=q_norm, k_norm=k_norm, w_out=w_out, ln1_g=ln1_g, ln1_b=ln1_b, w_up=w_up, conv_w=conv_w,
               conv_b=conv_b, w_down=w_down, ln2_g=ln2_g, ln2_b=ln2_b)
    inp = {k: np.asarray(v, np.float32) for k, v in inp.items()}
    sh = _prep_shared(inp, L, DEPTH)
    key = (L, DEPTH)
    if key not in _CACHE:
        _CACHE[key] = build(L, DEPTH)
    nc = _CACHE[key]
    c = np.asarray(c, np.float32)
    ctx = np.asarray(ctx, np.float32)
    cc = _chunk(np.asarray(c_ctx, np.float32), 8)
    in_maps = []
    for b in range(B):
        cv = np.stack([_chunk(c[b], 8), cc], axis=2).reshape(128, 16)
        m = dict(sh)
        m['x'] = np.ascontiguousarray(x[b])
        m['ctx'] = np.ascontiguousarray(ctx[b])
        m['cvec'] = np.ascontiguousarray(cv)
        in_maps.append(m)
    res = run_bass_kernel_spmd(nc, in_maps, core_ids=list(range(B)))
    return np.stack([np.asarray(r['out'], np.float32) for r in res.results], 0)
```
